# Optimizing a Trainium2 kernel written in Bass

```python
import math
import jax
import jax.numpy as jnp
from jax import lax
import numpy as np

D_MODEL = 2048
BATCH = 4
SEQ = 4096
DEPTH = 2

GRID_W = 64
CTX_LEN = 256
EPS = 1e-6
N_MOD = 6
HY_WIDTH = D_MODEL // 4
HY_ORDER = 2
HY_BANDS = 16
HY_EMB = 1 + 2 * HY_BANDS
HY_HIDDEN = 64
HY_FAST_DECAY = 0.3
HY_SLOW_DECAY = 1.5
HY_TARGET = 1e-2
RET_HEADS = 8
RET_V_W = D_MODEL // 2
RET_DV = RET_V_W // RET_HEADS
RET_DK = RET_DV // 2
RET_QK_W = RET_HEADS * RET_DK
RET_CHUNK = 128
S5_WIDTH = D_MODEL // 4
S5_GROUP = 16
S5_GROUPS = S5_WIDTH // S5_GROUP
S5_STATE = 64
D_FF = 256 * ((8 * D_MODEL // 3 + 255) // 256)
IN_WIDTH = 3 * HY_WIDTH + 2 * RET_QK_W + 2 * RET_V_W + S5_WIDTH
MIX_WIDTH = HY_WIDTH + RET_V_W + S5_WIDTH
F32 = jnp.float32

kernel_name = 'hybrid_hyena_retention_s5_dit'


def _rmsnorm(x, g):
    xf = x.astype(F32)
    y = xf * lax.rsqrt(jnp.mean(xf * xf, axis=-1, keepdims=True) + EPS)
    return (y * g.astype(F32)).astype(x.dtype)


def _modulate(x, g, shift, scale):
    return _rmsnorm(x, g) * (1.0 + scale) + shift


def _split_proj(a):
    h3 = 3 * HY_WIDTH
    cuts = [h3, h3 + RET_QK_W, h3 + 2 * RET_QK_W, h3 + 2 * RET_QK_W + RET_V_W,
            h3 + 2 * RET_QK_W + 2 * RET_V_W]
    return jnp.split(a, cuts, axis=-1)


def _short_conv(x, w, b):
    xp = jnp.pad(x, ((0, 0), (1, 1), (0, 0)))
    return xp[:, :-2] * w[0] + xp[:, 1:-1] * w[1] + xp[:, 2:] * w[2] + b


def _hyena_filters(length, w1, b1, w2, b2, w3, freq):
    idx = jnp.arange(length, dtype=F32)
    t = (idx / max(length - 1, 1))[:, None]
    bands = jnp.linspace(1e-4, HY_BANDS - 1, HY_BANDS, dtype=F32)
    ang = (2.0 * math.pi * idx / length)[:, None] * bands[None]
    z = jnp.concatenate([t, jnp.cos(ang), -jnp.sin(ang)], axis=-1)
    freq = freq.astype(F32)
    h = jnp.sin(freq[0] * (z @ w1.astype(F32) + b1.astype(F32)))
    h = jnp.sin(freq[1] * (h @ w2.astype(F32) + b2.astype(F32)))
    h = (h @ w3.astype(F32)).reshape(length, 2, HY_ORDER, HY_WIDTH)
    max_decay = math.log(HY_TARGET) / HY_FAST_DECAY
    min_decay = math.log(HY_TARGET) / HY_SLOW_DECAY
    deltas = jnp.linspace(min_decay, max_decay, HY_WIDTH, dtype=F32)
    h = h * jnp.exp(-t * jnp.abs(deltas))[:, None, None, :]
    k_full = jnp.concatenate([h[:, 0], jnp.zeros_like(h[:1, 0]), jnp.flip(h[1:, 1], axis=0)], axis=0)
    k_full = k_full / jnp.sum(jnp.abs(k_full), axis=0, keepdims=True)
    return jnp.fft.rfft(k_full, axis=0)


def _fft_conv(u, k_f):
    length = u.shape[1]
    u_f = jnp.fft.rfft(u, n=2 * length, axis=1)
    return jnp.fft.irfft(u_f * k_f[None], n=2 * length, axis=1)[:, :length]


def _hyena(p, conv_w, conv_b, w1, b1, w2, b2, w3, freq, bias):
    dtype = p.dtype
    length = p.shape[1]
    p = _short_conv(p, conv_w, conv_b).astype(F32)
    v, x1, x2 = jnp.split(p, 3, axis=-1)
    k_f = _hyena_filters(length, w1, b1, w2, b2, w3, freq)
    bias = bias.astype(F32)
    z = x1 * (_fft_conv(v, k_f[:, 0]) + bias[0] * v)
    y = x2 * (_fft_conv(z, k_f[:, 1]) + bias[1] * z)
    return y.astype(dtype)


def _heads(t, d):
    b, l, _ = t.shape
    return t.astype(F32).reshape(b, l, RET_HEADS, d).transpose(0, 2, 1, 3)


def _retention_dir(q, k, v, log_gamma, init, strict):
    bsz, nh, length, dk = q.shape
    dv = v.shape[-1]
    nc = length // RET_CHUNK
    q = q.reshape(bsz, nh, nc, RET_CHUNK, dk)
    k = k.reshape(bsz, nh, nc, RET_CHUNK, dk)
    v = v.reshape(bsz, nh, nc, RET_CHUNK, dv)
    pos = jnp.arange(RET_CHUNK, dtype=F32)
    diff = pos[:, None] - pos[None, :]
    keep = (diff > 0) if strict else (diff >= 0)
    decay_in = jnp.where(keep, jnp.exp(jnp.maximum(diff, 0.0) * log_gamma[:, None, None]), 0.0)
    scores = jnp.einsum('bhncd,bhnmd->bhncm', q, k) * decay_in[:, None]
    inner = jnp.einsum('bhncm,bhnme->bhnce', scores, v)
    k_dec = k * jnp.exp((RET_CHUNK - 1.0 - pos) * log_gamma[:, None])[:, None, :, None]
    kv = jnp.einsum('bhnmd,bhnme->bhnde', k_dec, v)
    chunk_decay = jnp.exp(RET_CHUNK * log_gamma)[:, None, None]

    def step(state, kv_n):
        return chunk_decay * state + kv_n, state

    final, s_in = lax.scan(step, init, jnp.moveaxis(kv, 2, 0))
    s_in = jnp.moveaxis(s_in, 0, 2)
    q_dec = q * jnp.exp((pos + 1.0) * log_gamma[:, None])[:, None, :, None]
    cross = jnp.einsum('bhncd,bhnde->bhnce', q_dec, s_in)
    return (inner + cross).reshape(bsz, nh, length, dv), final


def _retention_scan(q, k, v, log_gamma, init_f, init_b):
    y_f, s_f = _retention_dir(q, k, v, log_gamma[0], init_f, False)
    fl = lambda t: jnp.flip(t, axis=2)
    y_b, s_b = _retention_dir(fl(q), fl(k), fl(v), log_gamma[1], init_b, True)
    return y_f + fl(y_b), s_f, s_b


def _retention_out(y, g):
    y = y * lax.rsqrt(jnp.mean(y * y, axis=-1, keepdims=True) + EPS)
    b, h, l, dv = y.shape
    y = y.transpose(0, 2, 1, 3).reshape(b, l, h * dv).astype(g.dtype)
    return jax.nn.silu(g) * y


def _s5_params(lam_re, lam_im, log_step, b_re, b_im, c_re, c_im):
    lam = lax.complex(jnp.minimum(lam_re.astype(F32), -1e-4), lam_im.astype(F32))
    step = jnp.exp(log_step.astype(F32))
    b_mat = lax.complex(b_re.astype(F32), b_im.astype(F32))
    c_mat = lax.complex(c_re.astype(F32), c_im.astype(F32))
    return lam, step, b_mat, c_mat


def _s5_combine(e1, e2):
    a1, b1 = e1
    a2, b2 = e2
    return a1 * a2, a2 * b1 + b2


def _s5_dir(u, lam, step, b_mat, c_mat, init):
    lam_dt = lam * step[:, None]
    lam_bar = jnp.exp(lam_dt)
    b_bar = ((lam_bar - 1.0) / lam)[:, :, None] * b_mat
    bu = jnp.einsum('blgi,gpi->blgp', u, b_bar)
    a = jnp.broadcast_to(lam_bar, bu.shape)
    _, xs = lax.associative_scan(_s5_combine, (a, bu), axis=1)
    if init is not None:
        n = jnp.arange(1, u.shape[1] + 1, dtype=F32)
        xs = xs + jnp.exp(n[:, None, None] * lam_dt)[None] * init[:, None]
    y = jnp.einsum('blgp,gip->blgi', xs, c_mat).real
    return y, xs[:, -1]


def _s5_scan(u, lam, step, b_mat, c_mat, init_f, init_b):
    bsz, l, _ = u.shape
    ug = u.astype(F32).reshape(bsz, l, S5_GROUPS, S5_GROUP).astype(jnp.complex64)
    y_f, s_f = _s5_dir(ug, lam[0], step[0], b_mat[0], c_mat[0], init_f)
    y_b, s_b = _s5_dir(jnp.flip(ug, axis=1), lam[1], step[1], b_mat[1], c_mat[1], init_b)
    y = (y_f + jnp.flip(y_b, axis=1)).reshape(bsz, l, S5_WIDTH)
    return y, s_f, s_b


def _s5_out(y, u, d, glu_w, glu_b):
    z = jax.nn.gelu((y + d.astype(F32) * u.astype(F32)).astype(u.dtype))
    return z * jax.nn.sigmoid(z @ glu_w + glu_b)


def _conv_ffn(h, w_up, conv_w, conv_b, w_down, rows, width):
    bsz, l, _ = h.shape
    gate, val = jnp.split(h @ w_up, 2, axis=-1)
    gate = gate.reshape(bsz, rows, width, D_FF)
    gate = lax.conv_general_dilated(gate, conv_w[:, :, None, :], (1, 1), 'SAME',
                                    dimension_numbers=('NHWC', 'HWIO', 'NHWC'),
                                    feature_group_count=D_FF) + conv_b
    gate = gate.reshape(bsz, l, D_FF)
    return (jax.nn.gelu(gate) * val) @ w_down


def setup_inputs(seed: int = 0) -> dict:
    key = jax.random.key(seed)
    ks = iter(jax.random.split(key, 48))
    nrm = lambda shape, std: std * jax.random.normal(next(ks), shape, F32)
    D = D_MODEL
    x = nrm((BATCH, SEQ, D), 1.0)
    c = nrm((BATCH, D), 1.0)
    ctx = nrm((BATCH, CTX_LEN, D), 1.0)
    c_ctx = nrm((D,), 1.0)
    ada_w = nrm((DEPTH, D, N_MOD * D), 0.5 * D ** -0.5)
    ada_b = nrm((DEPTH, N_MOD * D), 0.01)
    norm1_g = 1.0 + nrm((DEPTH, D), 0.02)
    w_in = nrm((DEPTH, D, IN_WIDTH), D ** -0.5)
    hy_conv_w = nrm((DEPTH, 3, 3 * HY_WIDTH), 3 ** -0.5)
    hy_conv_b = nrm((DEPTH, 3 * HY_WIDTH), 0.01)
    hy_w1 = nrm((DEPTH, HY_EMB, HY_HIDDEN), HY_EMB ** -0.5)
    hy_b1 = nrm((DEPTH, HY_HIDDEN), 0.1)
    hy_w2 = nrm((DEPTH, HY_HIDDEN, HY_HIDDEN), HY_HIDDEN ** -0.5)
    hy_b2 = nrm((DEPTH, HY_HIDDEN), 0.1)
    hy_w3 = nrm((DEPTH, HY_HIDDEN, 2 * HY_ORDER * HY_WIDTH), HY_HIDDEN ** -0.5)
    hy_freq = 1.0 + nrm((DEPTH, 2, HY_HIDDEN), 0.02)
    hy_bias = nrm((DEPTH, HY_ORDER, HY_WIDTH), 0.5)
    gamma = 1.0 - 2.0 ** (-5.0 - jnp.arange(RET_HEADS, dtype=F32))
    ret_decay = jnp.log(-jnp.log(gamma))[None, None] + nrm((DEPTH, 2, RET_HEADS), 0.01)
    s5_shape = (DEPTH, 2, S5_GROUPS, S5_STATE)
    s5_lam_re = -0.5 + nrm(s5_shape, 0.01)
    s5_lam_im = jnp.broadcast_to(math.pi * jnp.arange(S5_STATE, dtype=F32), s5_shape) + nrm(s5_shape, 0.01)
    s5_log_step = jax.random.uniform(next(ks), (DEPTH, 2, S5_GROUPS), F32, math.log(1e-3), math.log(1e-1))
    s5_b_re = nrm((DEPTH, 2, S5_GROUPS, S5_STATE, S5_GROUP), (2 * S5_GROUP) ** -0.5)
    s5_b_im = nrm((DEPTH, 2, S5_GROUPS, S5_STATE, S5_GROUP), (2 * S5_GROUP) ** -0.5)
    s5_c_re = nrm((DEPTH, 2, S5_GROUPS, S5_GROUP, S5_STATE), 0.35)
    s5_c_im = nrm((DEPTH, 2, S5_GROUPS, S5_GROUP, S5_STATE), 0.35)
    s5_d = nrm((DEPTH, S5_WIDTH), 1.0)
    s5_glu_w = nrm((DEPTH, S5_WIDTH, S5_WIDTH), S5_WIDTH ** -0.5)
    s5_glu_b = nrm((DEPTH, S5_WIDTH), 0.01)
    w_out = nrm((DEPTH, MIX_WIDTH, D), MIX_WIDTH ** -0.5)
    norm2_g = 1.0 + nrm((DEPTH, D), 0.02)
    ffn_w_up = nrm((DEPTH, D, 2 * D_FF), D ** -0.5)
    ffn_conv_w = nrm((DEPTH, 3, 3, D_FF), 1.0 / 3.0)
    ffn_conv_b = nrm((DEPTH, D_FF), 0.01)
    ffn_w_down = nrm((DEPTH, D_FF, D), D_FF ** -0.5)
    norm_f = 1.0 + nrm((D,), 0.02)
    return {'x': x, 'c': c, 'ctx': ctx, 'c_ctx': c_ctx, 'ada_w': ada_w, 'ada_b': ada_b,
            'norm1_g': norm1_g, 'w_in': w_in, 'hy_conv_w': hy_conv_w, 'hy_conv_b': hy_conv_b,
            'hy_w1': hy_w1, 'hy_b1': hy_b1, 'hy_w2': hy_w2, 'hy_b2': hy_b2, 'hy_w3': hy_w3,
            'hy_freq': hy_freq, 'hy_bias': hy_bias, 'ret_decay': ret_decay,
            's5_lam_re': s5_lam_re, 's5_lam_im': s5_lam_im, 's5_log_step': s5_log_step,
            's5_b_re': s5_b_re, 's5_b_im': s5_b_im, 's5_c_re': s5_c_re, 's5_c_im': s5_c_im,
            's5_d': s5_d, 's5_glu_w': s5_glu_w, 's5_glu_b': s5_glu_b, 'w_out': w_out,
            'norm2_g': norm2_g, 'ffn_w_up': ffn_w_up, 'ffn_conv_w': ffn_conv_w,
            'ffn_conv_b': ffn_conv_b, 'ffn_w_down': ffn_w_down, 'norm_f': norm_f}


def reference(x, c, ctx, c_ctx, ada_w, ada_b, norm1_g, w_in, hy_conv_w, hy_conv_b,
              hy_w1, hy_b1, hy_w2, hy_b2, hy_w3, hy_freq, hy_bias, ret_decay,
              s5_lam_re, s5_lam_im, s5_log_step, s5_b_re, s5_b_im, s5_c_re, s5_c_im,
              s5_d, s5_glu_w, s5_glu_b, w_out, norm2_g, ffn_w_up, ffn_conv_w,
              ffn_conv_b, ffn_w_down, norm_f):
    bsz, length, _ = x.shape
    rows = length // GRID_W
    ctx_len = ctx.shape[1]
    h_lat, h_ctx = x, ctx
    for i in range(DEPTH):
        last = i == DEPTH - 1
        m_lat = jnp.split((jax.nn.silu(c) @ ada_w[i] + ada_b[i])[:, None, :], N_MOD, axis=-1)
        m_ctx = jnp.split(jax.nn.silu(c_ctx) @ ada_w[i] + ada_b[i], N_MOD, axis=-1)
        a_lat = _modulate(h_lat, norm1_g[i], m_lat[0], m_lat[1]) @ w_in[i]
        a_ctx = _modulate(h_ctx, norm1_g[i], m_ctx[0], m_ctx[1]) @ w_in[i]
        hy_l, q_l, k_l, v_l, g_l, u_l = _split_proj(a_lat)
        hy_c, q_c, k_c, v_c, g_c, u_c = _split_proj(a_ctx)
        log_gamma = -jnp.exp(ret_decay[i].astype(F32))
        lam, step, b_mat, c_mat = _s5_params(s5_lam_re[i], s5_lam_im[i], s5_log_step[i],
                                             s5_b_re[i], s5_b_im[i], s5_c_re[i], s5_c_im[i])
        zero = jnp.zeros((bsz, RET_HEADS, RET_DK, RET_DV), F32)
        r_c, rs_f, rs_b = _retention_scan(_heads(q_c, RET_DK), _heads(k_c, RET_DK) * RET_DK ** -0.5,
                                          _heads(v_c, RET_DV), log_gamma, zero, zero)
        y5_c, ss_f, ss_b = _s5_scan(u_c, lam, step, b_mat, c_mat, None, None)
        r_l, _, _ = _retention_scan(_heads(q_l, RET_DK), _heads(k_l, RET_DK) * RET_DK ** -0.5,
                                    _heads(v_l, RET_DV), log_gamma, rs_f, rs_b)
        y5_l, _, _ = _s5_scan(u_l, lam, step, b_mat, c_mat, ss_f, ss_b)
        hyena_l = _hyena(hy_l, hy_conv_w[i], hy_conv_b[i], hy_w1[i], hy_b1[i], hy_w2[i], hy_b2[i],
                         hy_w3[i], hy_freq[i], hy_bias[i])
        mix_l = jnp.concatenate([hyena_l, _retention_out(r_l, g_l),
                                 _s5_out(y5_l, u_l, s5_d[i], s5_glu_w[i], s5_glu_b[i])], axis=-1) @ w_out[i]
        if not last:
            hyena_c = _hyena(hy_c, hy_conv_w[i], hy_conv_b[i], hy_w1[i], hy_b1[i], hy_w2[i], hy_b2[i],
                             hy_w3[i], hy_freq[i], hy_bias[i])
            mix_c = jnp.concatenate([hyena_c, _retention_out(r_c, g_c),
                                     _s5_out(y5_c, u_c, s5_d[i], s5_glu_w[i], s5_glu_b[i])], axis=-1) @ w_out[i]
            h_ctx = h_ctx + m_ctx[2] * mix_c
            h_ctx = h_ctx + m_ctx[5] * _conv_ffn(_modulate(h_ctx, norm2_g[i], m_ctx[3], m_ctx[4]),
                                                 ffn_w_up[i], ffn_conv_w[i], ffn_conv_b[i], ffn_w_down[i],
                                                 1, ctx_len)
        h_lat = h_lat + m_lat[2] * mix_l
        h_lat = h_lat + m_lat[5] * _conv_ffn(_modulate(h_lat, norm2_g[i], m_lat[3], m_lat[4]),
                                             ffn_w_up[i], ffn_conv_w[i], ffn_conv_b[i], ffn_w_down[i],
                                             rows, GRID_W)
    return _rmsnorm(h_lat, norm_f)
```

```python
import contextlib
import math
import numpy as np
import ml_dtypes
import concourse.bass as bass
import concourse.mybir as mybir
from concourse.bass_utils import run_bass_kernel_spmd

F32 = mybir.dt.float32
BF16 = mybir.dt.bfloat16
I32 = mybir.dt.int32
AF = mybir.ActivationFunctionType
ALU = mybir.AluOpType
AX = mybir.AxisListType

D = 2048
KC = 16
SEQ = 4096
CTX = 256
T = SEQ + CTX
EPS = 1e-6
DFF = 5632
NCORES = 8
import os
NO_SELF_WAIT = os.environ.get('NO_SELF_WAIT', '0') == '1'


class Buf:
    def __init__(self, name, t):
        self.name = name
        self.t = t
        self.w = None
        self.r = {}

    def __getitem__(self, idx):
        return self.t[idx]

    def view(self):
        return Buf(self.name + "_v", self.t)


class Ctx:
    NDMA = 8

    def __init__(self):
        self.nc = bass.Bass("TRN2", target_bir_lowering=False, num_devices=NCORES)
        nc = self.nc
        self.eng = {"pe": nc.tensor, "dve": nc.vector, "act": nc.scalar, "pool": nc.gpsimd, "sp": nc.sync}
        self.sem = {}
        self.cnt = {}
        for e in self.eng:
            self.sem[e] = nc.alloc_semaphore("s_" + e)
            self.cnt[e] = 0
        self.dsem, self.dcnt, self.dptr = {}, {}, {}
        for q in ("sp", "act", "pool"):
            self.dsem[q] = [nc.alloc_semaphore(f"d_{q}{i}") for i in range(self.NDMA)]
            self.dcnt[q] = [0] * self.NDMA
            self.dptr[q] = 0
        self.seen = {e: {} for e in self.eng}
        self.bg = set()
        self.nbuf = 0
        self.stack = contextlib.ExitStack()
        self.scopes = []
        self._es = contextlib.ExitStack()
        self._es.enter_context(nc.allow_low_precision("bf16 matmul operands, fp32 accumulation"))
        self._es.enter_context(nc.allow_non_contiguous_dma("layout transforms"))

    def push(self):
        self.scopes.append(contextlib.ExitStack())

    def pop(self):
        self.barrier()
        self.scopes.pop().close()

    def sbuf(self, shape, dtype=F32, name=None):
        self.nbuf += 1
        name = name or f"sb{self.nbuf}"
        t = self.scopes[-1].enter_context(self.nc.sbuf_tensor(name, list(shape), dtype))
        return Buf(name, t)

    def psum(self, shape, dtype=F32, name=None):
        self.nbuf += 1
        name = name or f"ps{self.nbuf}"
        t = self.scopes[-1].enter_context(self.nc.psum_tensor(name, list(shape), dtype))
        return Buf(name, t)

    def dram(self, name, shape, dtype=F32, kind="Internal"):
        return Buf(name, self.nc.dram_tensor(name, list(shape), dtype, kind=kind))

    def _semh(self, key):
        if isinstance(key, tuple):
            return self.dsem[key[0]][key[1]]
        return self.sem[key]

    def _wait(self, e, k, v):
        if self.seen[e].get(k, -1) >= v:
            return
        self.eng[e].wait_ge(self._semh(k), v)
        self.seen[e][k] = v

    def _deps(self, e, reads, writes):
        need = {}

        def add(k, v):
            if need.get(k, -1) < v:
                need[k] = v
        for b in reads:
            if b.w is not None:
                add(*b.w)
        for b in writes:
            if b.w is not None:
                add(*b.w)
            for k, v in b.r.items():
                add(k, v)
        for k, v in need.items():
            if k == "pe" and e == "pe":
                continue
            if k == e and NO_SELF_WAIT:
                continue
            self._wait(e, k, v)

    def _mark(self, key, val, reads, writes):
        for b in reads:
            if b.r.get(key, -1) < val:
                b.r[key] = val
        for b in writes:
            b.w = (key, val)
            b.r = {}

    def op(self, e, fn, reads=(), writes=()):
        self._deps(e, reads, writes)
        ins = fn(self.eng[e])
        self.cnt[e] += 1
        ins.then_inc(self.sem[e], 1)
        self._mark(e, self.cnt[e], reads, writes)
        return ins

    def dma(self, q, out, in_, reads=(), writes=(), own_sem=False):
        if own_sem:
            self.dsem[q].append(self.nc.alloc_semaphore(f"d_{q}{len(self.dsem[q])}"))
            self.dcnt[q].append(0)
            i = len(self.dsem[q]) - 1
        else:
            i = self.dptr[q]
            self.dptr[q] = (i + 1) % self.NDMA
        key = (q, i)
        if self.dcnt[q][i] > 0:
            self._wait(q, key, self.dcnt[q][i])
        self._deps(q, reads, writes)
        ins = self.eng[q].dma_start(out=out, in_=in_)
        self.dcnt[q][i] += 16
        ins.then_inc(self.dsem[q][i], 16)
        self._mark(key, self.dcnt[q][i], reads, writes)
        if own_sem:
            self.bg.add(key)
        return ins

    def barrier(self):
        for e in self.eng:
            for k in self.eng:
                if k != e and self.cnt[k] > 0:
                    self._wait(e, k, self.cnt[k])
            for q in self.dsem:
                for i in range(len(self.dsem[q])):
                    if self.dcnt[q][i] > 0 and not (q, i) in self.bg:
                        self._wait(e, (q, i), self.dcnt[q][i])

    def finish(self):
        self.barrier()


_rr = {"i": 0}


def rr(lst):
    _rr["i"] += 1
    return lst[_rr["i"] % len(lst)]


class Rot:
    def __init__(self, c, n, shape, dtype=F32, psum=False):
        self.b = [(c.psum if psum else c.sbuf)(shape, dtype) for _ in range(n)]
        self.i = 0

    def get(self):
        self.i = (self.i + 1) % len(self.b)
        return self.b[self.i]


NF = 1536
NT = 1280
NW = NF + NT


def emit_mods(c, cT_d, adaw_d, adabT_d, modsT, R=2, ntile=96):
    c.push()
    cs = c.sbuf([128, KC, R])
    c.dma("sp", cs[:], cT_d.t.ap().rearrange("(kc p) r -> p kc r", p=128), reads=[cT_d], writes=[cs])
    sc = c.sbuf([128, KC, R])
    c.op("act", lambda e: e.activation(out=sc[:], in_=cs[:], func=AF.Silu), reads=[cs], writes=[sc])
    bT = c.sbuf([128, ntile])
    c.dma("sp", bT[:], adabT_d.t.ap(), reads=[adabT_d], writes=[bT])
    wrot = Rot(c, 2, [128, KC, 512])
    prot = Rot(c, 2, [128, 512], psum=True)
    wv = adaw_d.t.ap().rearrange("(kc p) n -> p kc n", p=128)
    for nb in range(ntile // 4):
        w = wrot.get()
        for h in range(2):
            c.dma(("sp", "pool")[h], w[:, h * 8:(h + 1) * 8, :], wv[:, h * 8:(h + 1) * 8, nb * 512:(nb + 1) * 512],
                  reads=[adaw_d], writes=[w])
        ps = prot.get()
        for jj in range(4):
            for kc in range(KC):
                c.op("pe", lambda e: e.matmul(out=ps[:, jj * R:(jj + 1) * R], lhsT=w[:, kc, jj * 128:(jj + 1) * 128],
                                              rhs=sc[:, kc, :], start=(kc == 0), stop=(kc == KC - 1)),
                     reads=[w, sc], writes=[ps])
        for jj in range(4):
            j = nb * 4 + jj
            c.op("dve", lambda e: e.tensor_scalar(out=modsT[:, j, :], in0=ps[:, jj * R:(jj + 1) * R],
                                                  scalar1=bT[:, j:j + 1], scalar2=None, op0=ALU.add),
                 reads=[ps, bT], writes=[modsT])
    c.pop()


def emit_inproj(c, hT_d, win_d, g1T_d, modsT, outs):
    c.push()
    w_sb = c.sbuf([128, KC, NW], BF16)
    wviews = [w_sb.view() for _ in range(KC)]
    wv = win_d.t.ap().rearrange("(kc p) n -> p kc n", p=128)
    for kc in range(KC):
        c.dma("pool", w_sb[:, kc, :], wv[:, kc, :], reads=[win_d], writes=[wviews[kc]])
    g1 = c.sbuf([128, KC])
    c.dma("sp", g1[:], g1T_d.t.ap(), reads=[g1T_d], writes=[g1])
    G = c.sbuf([128, 2, KC])
    S = c.sbuf([128, 2, KC])
    for r in range(2):
        c.op("dve", lambda e: e.scalar_tensor_tensor(out=G[:, r, :], in0=modsT[:, 16:32, r], scalar=1.0, in1=g1[:],
                                                     op0=ALU.add, op1=ALU.mult), reads=[modsT, g1], writes=[G])
        c.op("dve", lambda e: e.tensor_copy(out=S[:, r, :], in_=modsT[:, 0:16, r]), reads=[modsT], writes=[S])
    ones = c.sbuf([128, 128])
    c.op("dve", lambda e: e.memset(ones[:], 1.0), writes=[ones])
    hblk = c.sbuf([128, KC, 512])
    hviews = [hblk.view() for _ in range(KC)]
    xrot = Rot(c, 2, [128, KC, 512], BF16)
    sqrot = Rot(c, 3, [128, 512])
    tmprot = Rot(c, 3, [128, 512])
    ssps = Rot(c, 1, [128, 512], psum=True)
    prot = Rot(c, 5, [128, 512], psum=True)
    ev32 = Rot(c, 3, [128, 512])
    ev16 = Rot(c, 3, [128, 512], BF16)
    sd = c.sbuf([128, 512])
    rstd = c.sbuf([128, 512])
    hv = hT_d.t.ap().rearrange("(kc p) t -> p kc t", p=128)
    blocks = [(0, CTX, 1)] + [(CTX + i * 512, 512, 0) for i in range(SEQ // 512)]
    evi = 0
    for (t0, nb, r) in blocks:
        for kc in range(KC):
            c.dma(("sp", "pool")[kc % 2], hblk[:, kc, :nb], hv[:, kc, t0:t0 + nb], reads=[hT_d], writes=[hviews[kc]])
        ss = ssps.get()
        for kc in range(KC):
            sq = sqrot.get()
            c.op("act", lambda e: e.activation(out=sq[:, :nb], in_=hblk[:, kc, :nb], func=AF.Square),
                 reads=[hviews[kc]], writes=[sq])
            c.op("pe", lambda e: e.matmul(out=ss[:, :nb], lhsT=ones[:], rhs=sq[:, :nb], start=(kc == 0), stop=(kc == KC - 1)),
                 reads=[ones, sq], writes=[ss])
        c.op("act", lambda e: e.activation(out=sd[:, :nb], in_=ss[:, :nb], func=AF.Sqrt, scale=1.0 / D, bias=EPS),
             reads=[ss], writes=[sd])
        c.op("dve", lambda e: e.reciprocal(out=rstd[:, :nb], in_=sd[:, :nb]), reads=[sd], writes=[rstd])
        xn = xrot.get()
        for kc in range(KC):
            tmp = tmprot.get()
            c.op("dve", lambda e: e.tensor_tensor(out=tmp[:, :nb], in0=hblk[:, kc, :nb], in1=rstd[:, :nb], op=ALU.mult),
                 reads=[hviews[kc], rstd], writes=[tmp])
            c.op("act", lambda e: e.activation(out=xn[:, kc, :nb], in_=tmp[:, :nb], func=AF.Identity,
                                               scale=G[:, r, kc:kc + 1], bias=S[:, r, kc:kc + 1]),
                 reads=[tmp, G, S], writes=[xn])
        fm = [("hyT", 0, 6, F32, 1.0), ("qT", 6, 2, BF16, 1.0), ("kT", 8, 2, BF16, 0.125), ("uT", 10, 2, F32, 1.0)]
        for (name, m0, nm, dt, scl) in fm:
            for mi in range(nm):
                col0 = (m0 + mi) * 128
                ps = prot.get()
                for kc in range(KC):
                    c.op("pe", lambda e: e.matmul(out=ps[:, :nb], lhsT=w_sb[:, kc, col0:col0 + 128], rhs=xn[:, kc, :nb],
                                                  start=(kc == 0), stop=(kc == KC - 1)),
                         reads=[wviews[kc], xn], writes=[ps])
                ev = (ev32 if dt == F32 else ev16).get()
                evi += 1
                if evi % 2 == 0:
                    c.op("act", lambda e: e.activation(out=ev[:, :nb], in_=ps[:, :nb], func=AF.Copy, scale=scl),
                         reads=[ps], writes=[ev])
                else:
                    c.op("dve", lambda e: e.tensor_scalar(out=ev[:, :nb], in0=ps[:, :nb], scalar1=scl, scalar2=None,
                                                          op0=ALU.mult), reads=[ps], writes=[ev])
                o = outs[name]
                c.dma("sp", o[mi * 128:(mi + 1) * 128, t0:t0 + nb], ev[:, :nb], reads=[ev], writes=[o])
        tm = [("k", NF, 256, BF16, 0.125, None), ("v", NF + 256, 512, BF16, 1.0, None), ("sg", NF + 768, 512, F32, 1.0, AF.Silu)]
        for ti in range(nb // 128):
            for (name, col0, ncol, dt, scl, fn) in tm:
                ps = prot.get()
                for kc in range(KC):
                    c.op("pe", lambda e: e.matmul(out=ps[:, :ncol], lhsT=xn[:, kc, ti * 128:(ti + 1) * 128],
                                                  rhs=w_sb[:, kc, col0:col0 + ncol], start=(kc == 0), stop=(kc == KC - 1)),
                         reads=[wviews[kc], xn], writes=[ps])
                ev = (ev32 if dt == F32 else ev16).get()
                evi += 1
                if fn is not None:
                    c.op("act", lambda e: e.activation(out=ev[:, :ncol], in_=ps[:, :ncol], func=fn), reads=[ps], writes=[ev])
                elif evi % 2 == 0:
                    c.op("act", lambda e: e.activation(out=ev[:, :ncol], in_=ps[:, :ncol], func=AF.Copy, scale=scl),
                         reads=[ps], writes=[ev])
                else:
                    c.op("dve", lambda e: e.tensor_scalar(out=ev[:, :ncol], in0=ps[:, :ncol], scalar1=scl, scalar2=None,
                                                          op0=ALU.mult), reads=[ps], writes=[ev])
                o = outs[name]
                c.dma("pool", o[t0 + ti * 128:t0 + (ti + 1) * 128, :], ev[:, :ncol], reads=[ev], writes=[o])
    c.pop()


NCH = T // 128


def ret_consts():
    m = np.arange(128)[:, None].astype(np.float64)
    cc = np.arange(128)[None, :].astype(np.float64)
    pf = np.where(cc >= m, cc - m, 1e9)
    pb = np.where(m > cc, m - cc, 1e9)
    posq = np.stack([np.broadcast_to(cc + 1, (128, 128)), np.broadcast_to(128 - cc, (128, 128))])
    posk = np.stack([127 - m[:, 0], m[:, 0]], 1)
    return {"ret_pf": pf.astype(np.float32), "ret_pb": pb.astype(np.float32),
            "ret_posq": np.ascontiguousarray(posq.transpose(1, 0, 2)).astype(np.float32),
            "ret_posk": posk.astype(np.float32)}


def emit_retention(c, qT_d, kT_d, k_d, v_d, sg_d, rdec_d, cst, ident, mixT_d, row0):
    c.push()
    lg = c.sbuf([128, 8])
    c.dma("sp", lg[:], rdec_d.t.ap().partition_broadcast(128), reads=[rdec_d], writes=[lg])
    c.op("act", lambda e: e.activation(out=lg[:], in_=lg[:], func=AF.Exp), reads=[lg], writes=[lg])
    c.op("dve", lambda e: e.tensor_scalar(out=lg[:], in0=lg[:], scalar1=-1.0, scalar2=None, op0=ALU.mult), reads=[lg], writes=[lg])
    pf = c.sbuf([128, 128]); pb = c.sbuf([128, 128]); posq = c.sbuf([128, 2, 128]); posk = c.sbuf([128, 2])
    c.dma("sp", pf[:], cst["ret_pf"].t.ap(), reads=[cst["ret_pf"]], writes=[pf])
    c.dma("sp", pb[:], cst["ret_pb"].t.ap(), reads=[cst["ret_pb"]], writes=[pb])
    c.dma("sp", posq[:], cst["ret_posq"].t.ap(), reads=[cst["ret_posq"]], writes=[posq])
    c.dma("sp", posk[:], cst["ret_posk"].t.ap(), reads=[cst["ret_posk"]], writes=[posk])
    cd = c.sbuf([128, 8])
    c.op("act", lambda e: e.activation(out=cd[:], in_=lg[:], func=AF.Exp, scale=128.0), reads=[lg], writes=[cd])
    qT_h = c.sbuf([64, T], BF16); kT_h = c.sbuf([64, T], BF16)
    k_h = c.sbuf([128, NCH, 64], BF16); v_h = c.sbuf([128, NCH, 128], BF16); sg_h = c.sbuf([128, NCH, 128])
    qdf = c.sbuf([64, T], BF16); qdb = c.sbuf([64, T], BF16)
    kdf = c.sbuf([128, NCH, 64], BF16); kdb = c.sbuf([128, NCH, 64], BF16)
    Dm = c.sbuf([128, 128]); Dtmp = c.sbuf([128, 128]); Eq = c.sbuf([64, 2, 128]); ks = c.sbuf([128, 2])
    SB_all = c.sbuf([64, NCH, 128], BF16)
    Sb = c.sbuf([64, 128]); Sf = c.sbuf([64, 128]); Sf16 = c.sbuf([64, 128], BF16)
    ps_sc = Rot(c, 2, [128, 512], psum=True); ps_y = Rot(c, 2, [128, 512], psum=True)
    ps_kv = Rot(c, 2, [128, 512], psum=True); ps_tr = Rot(c, 2, [128, 512], psum=True)
    Prot = Rot(c, 2, [128, 128], BF16)
    orot = Rot(c, 3, [128, 128]); ssq = Rot(c, 2, [128, 1]); rs = Rot(c, 2, [128, 1]); junk = Rot(c, 2, [128, 128])
    oTst = Rot(c, 2, [128, 512])
    for hh in range(4):
        c.dma("sp", qT_h[:], qT_d[hh * 64:(hh + 1) * 64, :], reads=[qT_d], writes=[qT_h])
        c.dma("pool", kT_h[:], kT_d[hh * 64:(hh + 1) * 64, :], reads=[kT_d], writes=[kT_h])
        c.dma("sp", k_h[:], k_d.t.ap().rearrange("(n p) d -> p n d", p=128)[:, :, hh * 64:(hh + 1) * 64], reads=[k_d], writes=[k_h])
        c.dma("pool", v_h[:], v_d.t.ap().rearrange("(n p) d -> p n d", p=128)[:, :, hh * 128:(hh + 1) * 128], reads=[v_d], writes=[v_h])
        c.dma("sp", sg_h[:], sg_d.t.ap().rearrange("(n p) d -> p n d", p=128)[:, :, hh * 128:(hh + 1) * 128], reads=[sg_d], writes=[sg_h])
        lf = lg[:, hh:hh + 1]; lb = lg[:, 4 + hh:5 + hh]
        c.op("act", lambda e: e.activation(out=Dm[:], in_=pf[:], func=AF.Exp, scale=lf), reads=[pf, lg], writes=[Dm])
        c.op("act", lambda e: e.activation(out=Dtmp[:], in_=pb[:], func=AF.Exp, scale=lb), reads=[pb, lg], writes=[Dtmp])
        c.op("dve", lambda e: e.tensor_tensor(out=Dm[:], in0=Dm[:], in1=Dtmp[:], op=ALU.add), reads=[Dm, Dtmp], writes=[Dm])
        c.op("act", lambda e: e.activation(out=Eq[:, 0, :], in_=posq[0:64, 0, :], func=AF.Exp, scale=lg[0:64, hh:hh + 1]), reads=[posq, lg], writes=[Eq])
        c.op("act", lambda e: e.activation(out=Eq[:, 1, :], in_=posq[0:64, 1, :], func=AF.Exp, scale=lg[0:64, 4 + hh:5 + hh]), reads=[posq, lg], writes=[Eq])
        c.op("act", lambda e: e.activation(out=ks[:, 0:1], in_=posk[:, 0:1], func=AF.Exp, scale=lf), reads=[posk, lg], writes=[ks])
        c.op("act", lambda e: e.activation(out=ks[:, 1:2], in_=posk[:, 1:2], func=AF.Exp, scale=lb), reads=[posk, lg], writes=[ks])
        q3 = qT_h[:].rearrange("p (n c) -> p n c", c=128)
        c.op("dve", lambda e: e.tensor_tensor(out=qdf[:].rearrange("p (n c) -> p n c", c=128), in0=q3,
                                              in1=Eq[:, 0:1, :].broadcast_to([64, NCH, 128]), op=ALU.mult), reads=[qT_h, Eq], writes=[qdf])
        c.op("dve", lambda e: e.tensor_tensor(out=qdb[:].rearrange("p (n c) -> p n c", c=128), in0=q3,
                                              in1=Eq[:, 1:2, :].broadcast_to([64, NCH, 128]), op=ALU.mult), reads=[qT_h, Eq], writes=[qdb])
        c.op("dve", lambda e: e.tensor_scalar(out=kdf[:], in0=k_h[:], scalar1=ks[:, 0:1], scalar2=None, op0=ALU.mult), reads=[k_h, ks], writes=[kdf])
        c.op("dve", lambda e: e.tensor_scalar(out=kdb[:], in0=k_h[:], scalar1=ks[:, 1:2], scalar2=None, op0=ALU.mult), reads=[k_h, ks], writes=[kdb])
        c.op("dve", lambda e: e.memset(Sb[:], 0.0), writes=[Sb])
        for n in [1, 0] + list(range(NCH - 1, 1, -1)):
            c.op("act", lambda e: e.copy(out=SB_all[:, n, :], in_=Sb[:]), reads=[Sb], writes=[SB_all])
            kv = ps_kv.get()
            c.op("pe", lambda e: e.matmul(out=kv[0:64, 0:128], lhsT=kdb[:, n, :], rhs=v_h[:, n, :], start=True, stop=True),
                 reads=[kdb, v_h], writes=[kv])
            c.op("dve", lambda e: e.scalar_tensor_tensor(out=Sb[:], in0=Sb[:], scalar=cd[0:64, 4 + hh:5 + hh], in1=kv[0:64, 0:128],
                                                         op0=ALU.mult, op1=ALU.add), reads=[Sb, cd, kv], writes=[Sb])
        c.op("dve", lambda e: e.memset(Sf[:], 0.0), writes=[Sf])
        c.op("dve", lambda e: e.memset(Sf16[:], 0.0), writes=[Sf16])
        ost = None

        def emit_sc(n):
            ts_ = slice(n * 128, (n + 1) * 128)
            sc_ = ps_sc.get()
            c.op("pe", lambda e: e.matmul(out=sc_[:, 0:128], lhsT=kT_h[:, ts_], rhs=qT_h[:, ts_], start=True, stop=True),
                 reads=[kT_h, qT_h], writes=[sc_])
            return sc_

        def emit_tr(n, o):
            nonlocal ost
            tr = ps_tr.get()
            c.op("pe", lambda e: e.transpose(out=tr[:, 0:128], in_=o[:], identity=ident[:]), reads=[o, ident], writes=[tr])
            if n % 4 == 0:
                ost = oTst.get()
            j = n % 4
            c.op("act", lambda e: e.copy(out=ost[:, j * 128:(j + 1) * 128], in_=tr[:, 0:128]), reads=[tr], writes=[ost])
            if j == 3 or n == NCH - 1:
                n0 = n - j
                c.dma("sp", mixT_d[row0 + hh * 128:row0 + (hh + 1) * 128, n0 * 128:(n + 1) * 128], ost[:, 0:(j + 1) * 128],
                      reads=[ost], writes=[mixT_d])
        sc_next = emit_sc(0)
        o_prev = None
        for n in range(NCH):
            ts = slice(n * 128, (n + 1) * 128)
            sc = sc_next
            if n + 1 < NCH:
                sc_next = emit_sc(n + 1)
            P = Prot.get()
            c.op("dve", lambda e: e.tensor_tensor(out=P[:], in0=sc[:, 0:128], in1=Dm[:], op=ALU.mult), reads=[sc, Dm], writes=[P])
            y = ps_y.get()
            c.op("pe", lambda e: e.matmul(out=y[:, 0:128], lhsT=P[:], rhs=v_h[:, n, :], start=True, stop=False), reads=[P, v_h], writes=[y])
            c.op("pe", lambda e: e.matmul(out=y[:, 0:128], lhsT=qdf[:, ts], rhs=Sf16[:], start=False, stop=False), reads=[qdf, Sf16], writes=[y])
            c.op("pe", lambda e: e.matmul(out=y[:, 0:128], lhsT=qdb[:, ts], rhs=SB_all[:, n, :], start=False, stop=True), reads=[qdb, SB_all], writes=[y])
            kv = ps_kv.get()
            c.op("pe", lambda e: e.matmul(out=kv[0:64, 0:128], lhsT=kdf[:, n, :], rhs=v_h[:, n, :], start=True, stop=True),
                 reads=[kdf, v_h], writes=[kv])
            if o_prev is not None:
                emit_tr(n - 1, o_prev)
            c.op("dve", lambda e: e.scalar_tensor_tensor(out=Sf[:], in0=Sf[:], scalar=cd[0:64, hh:hh + 1], in1=kv[0:64, 0:128],
                                                         op0=ALU.mult, op1=ALU.add), reads=[Sf, cd, kv], writes=[Sf])
            c.op("act", lambda e: e.copy(out=Sf16[:], in_=Sf[:]), reads=[Sf], writes=[Sf16])
            sq = ssq.get(); jk = junk.get(); r_ = rs.get()
            c.op("act", lambda e: e.activation(out=jk[:], in_=y[:, 0:128], func=AF.Square, accum_out=sq[:]), reads=[y], writes=[jk, sq])
            c.op("act", lambda e: e.activation(out=sq[:], in_=sq[:], func=AF.Sqrt, scale=1.0 / 128, bias=EPS), reads=[sq], writes=[sq])
            c.op("dve", lambda e: e.reciprocal(out=r_[:], in_=sq[:]), reads=[sq], writes=[r_])
            o = orot.get()
            c.op("dve", lambda e: e.scalar_tensor_tensor(out=o[:], in0=y[:, 0:128], scalar=r_[:], in1=sg_h[:, n, :],
                                                         op0=ALU.mult, op1=ALU.mult), reads=[y, r_, sg_h], writes=[o])
            o_prev = o
        emit_tr(NCH - 1, o_prev)
    c.pop()


TWO_PI = 2.0 * math.pi
MAGIC = 12582912.0


def emit_sin(c, out_ap, arg_ap, shape, tmps, reads, writes):
    t, k = tmps
    sl = tuple(slice(0, s) for s in shape)
    c.op("dve", lambda e: e.tensor_scalar(out=t[sl], in0=arg_ap, scalar1=1.0 / TWO_PI, scalar2=MAGIC, op0=ALU.mult, op1=ALU.add),
         reads=reads, writes=[t])
    c.op("dve", lambda e: e.tensor_scalar(out=k[sl], in0=t[sl], scalar1=MAGIC, scalar2=-TWO_PI, op0=ALU.subtract, op1=ALU.mult),
         reads=[t], writes=[k])
    c.op("dve", lambda e: e.tensor_tensor(out=k[sl], in0=k[sl], in1=arg_ap, op=ALU.add), reads=[k] + list(reads), writes=[k])
    c.op("dve", lambda e: e.tensor_scalar(out=k[sl], in0=k[sl], scalar1=math.pi, scalar2=-math.pi, op0=ALU.min, op1=ALU.max),
         reads=[k], writes=[k])
    c.op("act", lambda e: e.activation(out=out_ap, in_=k[sl], func=AF.Sin), reads=[k], writes=writes)


def s5_consts():
    j = np.arange(128, dtype=np.float32)
    pos = np.stack([np.broadcast_to(j + 1, (128, 128)), np.broadcast_to(128 - j, (128, 128))], 1)
    return {"s5_pos": np.ascontiguousarray(pos).astype(np.float32)}


def s5_host(inp, l, hf):
    G0 = 16 * hf
    Bb = np.zeros([128, 2, 8, 2, 128], np.float32)
    Cb = np.zeros([128, 2, 8, 2, 128], np.float32)
    lam = np.zeros([128, 2, 16], np.float32)
    lst = np.zeros([128, 16], np.float32)
    for d in range(2):
        for pt in range(8):
            for g2 in range(2):
                gl = 2 * pt + g2
                G = G0 + gl
                r0 = (gl % 8) * 16
                Bb[r0:r0 + 16, d, pt, 0, g2 * 64:(g2 + 1) * 64] = inp["s5_b_re"][l, d, G].T
                Bb[r0:r0 + 16, d, pt, 1, g2 * 64:(g2 + 1) * 64] = inp["s5_b_im"][l, d, G].T
                Cb[g2 * 64:(g2 + 1) * 64, d, pt, 0, r0:r0 + 16] = inp["s5_c_re"][l, d, G].T
                Cb[g2 * 64:(g2 + 1) * 64, d, pt, 1, r0:r0 + 16] = inp["s5_c_im"][l, d, G].T
                lam[g2 * 64:(g2 + 1) * 64, 0, d * 8 + pt] = inp["s5_lam_re"][l, d, G]
                lam[g2 * 64:(g2 + 1) * 64, 1, d * 8 + pt] = inp["s5_lam_im"][l, d, G]
                lst[g2 * 64:(g2 + 1) * 64, d * 8 + pt] = inp["s5_log_step"][l, d, G]
    dT = np.ascontiguousarray(inp["s5_d"][l][256 * hf:256 * hf + 256].reshape(2, 128).T)
    return {"s5_B": Bb, "s5_C": Cb, "s5_lam": lam, "s5_lst": lst, "s5_dT": dT}


def emit_s5(c, uT_d, w, mixT_d, row0):
    c.push()
    Bb = c.sbuf([128, 2, 8, 2, 128]); Cb = c.sbuf([128, 2, 8, 2, 128])
    c.dma("sp", Bb[:], w["s5_B"].t.ap(), reads=[w["s5_B"]], writes=[Bb])
    c.dma("pool", Cb[:], w["s5_C"].t.ap(), reads=[w["s5_C"]], writes=[Cb])
    c.op("dve", lambda e: e.tensor_scalar(out=Cb[:, :, :, 1, :], in0=Cb[:, :, :, 1, :], scalar1=-1.0, scalar2=None, op0=ALU.mult),
         reads=[Cb], writes=[Cb])
    lam = c.sbuf([128, 2, 16]); lst = c.sbuf([128, 16]); dT = c.sbuf([128, 2]); pos = c.sbuf([128, 2, 128])
    c.dma("sp", lam[:], w["s5_lam"].t.ap(), reads=[w["s5_lam"]], writes=[lam])
    c.dma("sp", lst[:], w["s5_lst"].t.ap(), reads=[w["s5_lst"]], writes=[lst])
    c.dma("sp", dT[:], w["s5_dT"].t.ap(), reads=[w["s5_dT"]], writes=[dT])
    c.dma("sp", pos[:], w["s5_pos"].t.ap(), reads=[w["s5_pos"]], writes=[pos])
    P = {n: c.sbuf([128, 16]) for n in ["st", "lr", "ex", "rho", "th", "th2", "sn", "cs", "nr", "ni", "den", "cr", "ci", "t1", "t2"]}
    tm16 = (c.sbuf([128, 16]), c.sbuf([128, 16]))

    def tt(o, a, b, op):
        c.op("dve", lambda e: e.tensor_tensor(out=P[o][:], in0=P[a][:] if isinstance(a, str) else a, in1=P[b][:] if isinstance(b, str) else b, op=op),
             reads=[P[a] if isinstance(a, str) else lam, P[b] if isinstance(b, str) else lam], writes=[P[o]])
    c.op("act", lambda e: e.activation(out=P["st"][:], in_=lst[:], func=AF.Exp), reads=[lst], writes=[P["st"]])
    c.op("dve", lambda e: e.tensor_scalar(out=P["lr"][:], in0=lam[:, 0, :], scalar1=-1e-4, scalar2=None, op0=ALU.min), reads=[lam], writes=[P["lr"]])
    tt("ex", "lr", "st", ALU.mult)
    c.op("act", lambda e: e.activation(out=P["rho"][:], in_=P["ex"][:], func=AF.Exp), reads=[P["ex"]], writes=[P["rho"]])
    tt("th", lam[:, 1, :], "st", ALU.mult)
    c.op("dve", lambda e: e.tensor_scalar(out=P["th2"][:], in0=P["th"][:], scalar1=math.pi / 2, scalar2=None, op0=ALU.add), reads=[P["th"]], writes=[P["th2"]])
    emit_sin(c, P["sn"][:], P["th"][:], [128, 16], tm16, [P["th"]], [P["sn"]])
    emit_sin(c, P["cs"][:], P["th2"][:], [128, 16], tm16, [P["th2"]], [P["cs"]])
    tt("nr", "rho", "cs", ALU.mult)
    c.op("dve", lambda e: e.tensor_scalar(out=P["nr"][:], in0=P["nr"][:], scalar1=-1.0, scalar2=None, op0=ALU.add), reads=[P["nr"]], writes=[P["nr"]])
    tt("ni", "rho", "sn", ALU.mult)
    tt("t1", "lr", "lr", ALU.mult)
    tt("t2", lam[:, 1, :], lam[:, 1, :], ALU.mult)
    tt("den", "t1", "t2", ALU.add)
    c.op("dve", lambda e: e.reciprocal(out=P["den"][:], in_=P["den"][:]), reads=[P["den"]], writes=[P["den"]])
    tt("t1", "nr", "lr", ALU.mult); tt("t2", "ni", lam[:, 1, :], ALU.mult); tt("cr", "t1", "t2", ALU.add); tt("cr", "cr", "den", ALU.mult)
    tt("t1", "ni", "lr", ALU.mult); tt("t2", "nr", lam[:, 1, :], ALU.mult); tt("ci", "t1", "t2", ALU.subtract); tt("ci", "ci", "den", ALU.mult)
    Tab = c.sbuf([128, 16, 4, 128])
    arg = c.sbuf([128, 128]); arg2 = c.sbuf([128, 128]); tm = (c.sbuf([128, 128]), c.sbuf([128, 128])); ta = c.sbuf([128, 128]); tb = c.sbuf([128, 128])
    for q in range(16):
        d = q // 8
        c.op("dve", lambda e: e.tensor_scalar(out=arg[:], in0=pos[:, d, :], scalar1=P["th"][:, q:q + 1], scalar2=None, op0=ALU.mult), reads=[pos, P["th"]], writes=[arg])
        c.op("dve", lambda e: e.tensor_scalar(out=arg2[:], in0=arg[:], scalar1=math.pi / 2, scalar2=None, op0=ALU.add), reads=[arg], writes=[arg2])
        emit_sin(c, Tab[:, q, 1, :], arg[:], [128, 128], tm, [arg], [Tab])
        emit_sin(c, Tab[:, q, 0, :], arg2[:], [128, 128], tm, [arg2], [Tab])
        c.op("dve", lambda e: e.tensor_scalar(out=ta[:], in0=Tab[:, q, 1, :], scalar1=P["ci"][:, q:q + 1], scalar2=None, op0=ALU.mult), reads=[Tab, P["ci"]], writes=[ta])
        c.op("dve", lambda e: e.scalar_tensor_tensor(out=Tab[:, q, 2, :], in0=Tab[:, q, 0, :], scalar=P["cr"][:, q:q + 1], in1=ta[:], op0=ALU.mult, op1=ALU.add), reads=[Tab, P["cr"], ta], writes=[Tab])
        c.op("dve", lambda e: e.tensor_scalar(out=tb[:], in0=Tab[:, q, 1, :], scalar1=P["cr"][:, q:q + 1], scalar2=None, op0=ALU.mult), reads=[Tab, P["cr"]], writes=[tb])
        c.op("dve", lambda e: e.scalar_tensor_tensor(out=Tab[:, q, 3, :], in0=Tab[:, q, 0, :], scalar=P["ci"][:, q:q + 1], in1=tb[:], op0=ALU.mult, op1=ALU.subtract), reads=[Tab, P["ci"], tb], writes=[Tab])
    u_sb = c.sbuf([128, 2, T])
    c.dma("sp", u_sb[:, 0, :], uT_d[0:128, :], reads=[uT_d], writes=[u_sb])
    c.dma("pool", u_sb[:, 1, :], uT_d[128:256, :], reads=[uT_d], writes=[u_sb])
    yb = c.sbuf([128, 2, T])
    xrot = Rot(c, 2, [128, 8, 2, 128])
    ps_b = Rot(c, 3, [128, 4, 2, 128], psum=True)
    ps_y = Rot(c, 2, [128, 512], psum=True)
    r1 = Rot(c, 3, [128, 4, 128]); r2 = Rot(c, 3, [128, 4, 128]); rin = Rot(c, 2, [128, 4, 2, 128]); wv = Rot(c, 3, [128, 4, 2, 128])
    p1 = Rot(c, 3, [128, 4, 128]); p2 = Rot(c, 3, [128, 4, 128])
    zst = Rot(c, 2, [128, 512]); ytmp = Rot(c, 2, [128, 128])
    carr = Rot(c, 2, [128, 8, 2]); ctm = Rot(c, 4, [128, 4])
    def emit_bu(d, n, g4):
        ts = slice(n * 128, (n + 1) * 128)
        pb = ps_b.get()
        for p4 in range(4):
            pt = g4 * 4 + p4
            kc = pt // 4
            c.op("pe", lambda e: e.matmul(out=pb[:, p4, 0, :], lhsT=Bb[:, d, pt, 0, :], rhs=u_sb[:, kc, ts], start=True, stop=True), reads=[Bb, u_sb], writes=[pb])
            c.op("pe", lambda e: e.matmul(out=pb[:, p4, 1, :], lhsT=Bb[:, d, pt, 1, :], rhs=u_sb[:, kc, ts], start=True, stop=True), reads=[Bb, u_sb], writes=[pb])
        return pb

    for d, order in ((1, [1, 0] + list(range(NCH - 1, 1, -1))), (0, list(range(NCH)))):
        xprev = None
        car = None
        zcur = [None, None]
        steps = [(n, g4) for n in order for g4 in range(2)]
        pbs = {0: emit_bu(d, steps[0][0], steps[0][1])}
        for si, (n, g4) in enumerate(steps):
            ts = slice(n * 128, (n + 1) * 128)
            if si + 1 < len(steps):
                pbs[si + 1] = emit_bu(d, steps[si + 1][0], steps[si + 1][1])
            pb = pbs.pop(si)
            if g4 == 0:
                x = xrot.get()
                carn = carr.get()
            if True:
                q0 = d * 8 + g4 * 4
                a1 = r1.get(); a2 = r2.get(); ri = rin.get(); w_ = wv.get()
                Tr = Tab[:, q0:q0 + 4, 2, :]; Ti = Tab[:, q0:q0 + 4, 3, :]
                Br = pb[:, :, 0, :]; Bi = pb[:, :, 1, :]
                c.op("dve", lambda e: e.tensor_tensor(out=a1[:], in0=Br, in1=Tr, op=ALU.mult), reads=[pb, Tab], writes=[a1])
                c.op("dve", lambda e: e.tensor_tensor(out=a2[:], in0=Bi, in1=Ti, op=ALU.mult), reads=[pb, Tab], writes=[a2])
                c.op("dve", lambda e: e.tensor_tensor(out=ri[:, :, 0, :], in0=a1[:], in1=a2[:], op=ALU.subtract), reads=[a1, a2], writes=[ri])
                a1 = r1.get(); a2 = r2.get()
                c.op("dve", lambda e: e.tensor_tensor(out=a1[:], in0=Bi, in1=Tr, op=ALU.mult), reads=[pb, Tab], writes=[a1])
                c.op("dve", lambda e: e.tensor_tensor(out=a2[:], in0=Br, in1=Ti, op=ALU.mult), reads=[pb, Tab], writes=[a2])
                c.op("dve", lambda e: e.tensor_tensor(out=ri[:, :, 1, :], in0=a1[:], in1=a2[:], op=ALU.add), reads=[a1, a2], writes=[ri])
                for p4 in range(4):
                    pt = g4 * 4 + p4
                    q = d * 8 + pt
                    rho_b = P["rho"][:, q:q + 1].broadcast_to([128, 128])
                    for ci_ in range(2):
                        if car is None:
                            init = 0.0
                            rd = [ri, P["rho"]]
                        else:
                            init = car[:, pt, ci_:ci_ + 1]
                            rd = [ri, P["rho"], car]
                        if d == 0:
                            c.op("dve", lambda e: e.tensor_tensor_scan(out=w_[:, p4, ci_, :], data0=rho_b, data1=ri[:, p4, ci_, :], initial=init, op0=ALU.mult, op1=ALU.add), reads=rd, writes=[w_])
                        else:
                            c.op("dve", lambda e: e.tensor_tensor_scan(out=w_[:, p4, ci_, ::-1], data0=rho_b, data1=ri[:, p4, ci_, ::-1], initial=init, op0=ALU.mult, op1=ALU.add), reads=rd, writes=[w_])
                Cj = Tab[:, q0:q0 + 4, 0, :]; Sj = Tab[:, q0:q0 + 4, 1, :]
                wr = w_[:, :, 0, :]; wi = w_[:, :, 1, :]
                xs = x[:, g4 * 4:g4 * 4 + 4, :, :]
                Lc = 127 if d == 0 else 0
                cL = Tab[:, q0:q0 + 4, 0, Lc]; sL = Tab[:, q0:q0 + 4, 1, Lc]
                wrL = w_[:, :, 0, Lc]; wiL = w_[:, :, 1, Lc]
                k1 = ctm.get(); k2 = ctm.get()
                c.op("pool", lambda e: e.tensor_tensor(out=k1[:], in0=wrL, in1=cL, op=ALU.mult), reads=[w_, Tab], writes=[k1])
                c.op("pool", lambda e: e.tensor_tensor(out=k2[:], in0=wiL, in1=sL, op=ALU.mult), reads=[w_, Tab], writes=[k2])
                c.op("pool", lambda e: e.tensor_tensor(out=carn[:, g4 * 4:g4 * 4 + 4, 0], in0=k1[:], in1=k2[:], op=ALU.subtract), reads=[k1, k2], writes=[carn])
                k1 = ctm.get(); k2 = ctm.get()
                c.op("pool", lambda e: e.tensor_tensor(out=k1[:], in0=wrL, in1=sL, op=ALU.mult), reads=[w_, Tab], writes=[k1])
                c.op("pool", lambda e: e.tensor_tensor(out=k2[:], in0=wiL, in1=cL, op=ALU.mult), reads=[w_, Tab], writes=[k2])
                c.op("pool", lambda e: e.tensor_tensor(out=carn[:, g4 * 4:g4 * 4 + 4, 1], in0=k1[:], in1=k2[:], op=ALU.add), reads=[k1, k2], writes=[carn])
                b1 = p1.get(); b2 = p2.get()
                c.op("pool", lambda e: e.tensor_tensor(out=b1[:], in0=wr, in1=Cj, op=ALU.mult), reads=[w_, Tab], writes=[b1])
                c.op("pool", lambda e: e.tensor_tensor(out=b2[:], in0=wi, in1=Sj, op=ALU.mult), reads=[w_, Tab], writes=[b2])
                c.op("pool", lambda e: e.tensor_tensor(out=xs[:, :, 0, :], in0=b1[:], in1=b2[:], op=ALU.subtract), reads=[b1, b2], writes=[x])
                b1 = p1.get(); b2 = p2.get()
                c.op("pool", lambda e: e.tensor_tensor(out=b1[:], in0=wr, in1=Sj, op=ALU.mult), reads=[w_, Tab], writes=[b1])
                c.op("pool", lambda e: e.tensor_tensor(out=b2[:], in0=wi, in1=Cj, op=ALU.mult), reads=[w_, Tab], writes=[b2])
                c.op("pool", lambda e: e.tensor_tensor(out=xs[:, :, 1, :], in0=b1[:], in1=b2[:], op=ALU.add), reads=[b1, b2], writes=[x])
            if g4 == 0:
                continue
            xprev = x
            car = carn
            for ot in range(2):
                py = ps_y.get()
                for i_, pt in enumerate(range(4 * ot, 4 * ot + 4)):
                    c.op("pe", lambda e: e.matmul(out=py[:, 0:128], lhsT=Cb[:, d, pt, 0, :], rhs=x[:, pt, 0, :], start=(i_ == 0), stop=False), reads=[Cb, x], writes=[py])
                    c.op("pe", lambda e: e.matmul(out=py[:, 0:128], lhsT=Cb[:, d, pt, 1, :], rhs=x[:, pt, 1, :], start=False, stop=(i_ == 3)), reads=[Cb, x], writes=[py])
                if d == 1:
                    c.op("act", lambda e: e.copy(out=yb[:, ot, ts], in_=py[:, 0:128]), reads=[py], writes=[yb])
                else:
                    yt = ytmp.get()
                    c.op("dve", lambda e: e.scalar_tensor_tensor(out=yt[:], in0=u_sb[:, ot, ts], scalar=dT[:, ot:ot + 1], in1=py[:, 0:128], op0=ALU.mult, op1=ALU.add), reads=[u_sb, dT, py], writes=[yt])
                    c.op("pool", lambda e: e.tensor_tensor(out=yt[:], in0=yt[:], in1=yb[:, ot, ts], op=ALU.add), reads=[yt, yb], writes=[yt])
                    j = n % 4
                    if j == 0 or zcur[ot] is None:
                        zcur[ot] = zst.get()
                    zz = zcur[ot]
                    c.op("act", lambda e: e.activation(out=zz[:, j * 128:(j + 1) * 128], in_=yt[:], func=AF.Gelu_apprx_tanh), reads=[yt], writes=[zz])
                    if j == 3 or n == NCH - 1:
                        n0 = n - j
                        c.dma("sp", mixT_d[row0 + ot * 128:row0 + (ot + 1) * 128, n0 * 128:(n + 1) * 128], zz[:, 0:(j + 1) * 128], reads=[zz], writes=[mixT_d])
    c.pop()


CG = 32


def hy_tables(L):
    N = 2 * L
    N1 = N // 128
    F1 = N1 // 2 + 1
    T1 = L // 128
    t1 = np.arange(N1, dtype=np.float64)[:, None]
    f1 = np.arange(F1, dtype=np.float64)[None, :]
    W1 = np.concatenate([np.cos(2 * np.pi * t1 * f1 / N1), -np.sin(2 * np.pi * t1 * f1 / N1)], 1)
    a = np.arange(128, dtype=np.float64)
    ang = 2 * np.pi * a[:, None, None] * (np.arange(F1)[None, :, None] + N1 * a[None, None, :]) / N
    TB = np.stack([np.cos(ang), np.sin(ang)], 2)
    angT = ang.transpose(2, 1, 0)
    TBT = np.stack([np.cos(angT), np.sin(angT)], 2)
    wf = np.full(F1, 2.0); wf[0] = 1.0; wf[F1 - 1] = 1.0
    tt = np.arange(T1, dtype=np.float64)[None, :]
    ff = np.arange(F1, dtype=np.float64)[:, None]
    Minv = np.concatenate([wf[:, None] / N * np.cos(2 * np.pi * ff * tt / N1), -wf[:, None] / N * np.sin(2 * np.pi * ff * tt / N1)], 0)
    idx = np.arange(L, dtype=np.float32)
    tpos = (idx / np.float32(max(L - 1, 1))).astype(np.float32)
    bands = np.linspace(1e-4, 15.0, 16, dtype=np.float32)
    angp = ((np.float32(2.0 * math.pi) * idx / np.float32(L))[:, None] * bands[None]).astype(np.float32)
    z = np.concatenate([tpos[:, None], np.cos(angp), -np.sin(angp)], -1).astype(np.float32)
    dmap = np.concatenate([np.arange(L), [0], np.arange(L - 1, 0, -1)])
    zT = np.ascontiguousarray(z[dmap].T)
    tn = tpos[dmap].astype(np.float32).copy()
    tn[L] = 1e9
    s = f"hy{L}_"
    out = {s + "W1": W1.astype(np.float32), s + "Minv": Minv.astype(np.float32), s + "zT": zT, s + "tn": tn}
    for nm, tab in (("TB", TB), ("TBT", TBT)):
        t32 = tab.astype(np.float32)
        hi = t32.astype(ml_dtypes.bfloat16)
        lo = (t32 - hi.astype(np.float32)).astype(ml_dtypes.bfloat16)
        out[s + nm + "h"] = hi
        out[s + nm + "l"] = lo
    return out


def hy_host(inp, l, hf):
    cols = np.concatenate([o + 256 * hf + np.arange(256) for o in (0, 512, 1024)])
    cw = inp["hy_conv_w"][l][:, cols]
    cwT = np.ascontiguousarray(cw.reshape(3, 6, 128).transpose(2, 1, 0))
    cbT = np.ascontiguousarray(inp["hy_conv_b"][l][cols].reshape(6, 128).T)
    w3 = inp["hy_w3"][l].reshape(64, 2, 2, 512)[:, :, :, 256 * hf:256 * hf + 256]
    deltas = np.abs(np.linspace(math.log(1e-2) / 1.5, math.log(1e-2) / 0.3, 512, dtype=np.float32))
    nd = -deltas[256 * hf:256 * hf + 256]
    return {"hy_cwT": cwT, "hy_cbT": cbT, "hy_w1": inp["hy_w1"][l], "hy_b1": inp["hy_b1"][l].reshape(64, 1),
            "hy_w2": inp["hy_w2"][l], "hy_b2": inp["hy_b2"][l].reshape(64, 1), "hy_w3": np.ascontiguousarray(w3),
            "hy_freqT": np.ascontiguousarray(inp["hy_freq"][l].T), "hy_ndT": np.ascontiguousarray(nd.reshape(2, 128).T),
            "hy_bias": np.ascontiguousarray(inp["hy_bias"][l][:, 256 * hf:256 * hf + 256])}


def emit_hy_shortconv(c, hyT_d, w, hyc_d):
    c.push()
    cw = c.sbuf([128, 6, 3]); cb = c.sbuf([128, 6])
    c.dma("sp", cw[:], w["hy_cwT"].t.ap(), reads=[w["hy_cwT"]], writes=[cw])
    c.dma("sp", cb[:], w["hy_cbT"].t.ap(), reads=[w["hy_cbT"]], writes=[cb])
    xr = Rot(c, 2, [128, T]); pr = Rot(c, 2, [128, T])
    for rt in range(6):
        x = xr.get(); p = pr.get()
        c.dma(("sp", "pool")[rt % 2], x[:], hyT_d[rt * 128:(rt + 1) * 128, :], reads=[hyT_d], writes=[x])
        for (a, b) in ((0, CTX), (CTX, T)):
            c.op("dve", lambda e: e.tensor_scalar(out=p[:, a:b], in0=x[:, a:b], scalar1=cw[:, rt, 1:2], scalar2=cb[:, rt:rt + 1], op0=ALU.mult, op1=ALU.add),
                 reads=[x, cw, cb], writes=[p])
            c.op("dve", lambda e: e.scalar_tensor_tensor(out=p[:, a + 1:b], in0=x[:, a:b - 1], scalar=cw[:, rt, 0:1], in1=p[:, a + 1:b], op0=ALU.mult, op1=ALU.add),
                 reads=[x, cw, p], writes=[p])
            c.op("dve", lambda e: e.scalar_tensor_tensor(out=p[:, a:b - 1], in0=x[:, a + 1:b], scalar=cw[:, rt, 2:3], in1=p[:, a:b - 1], op0=ALU.mult, op1=ALU.add),
                 reads=[x, cw, p], writes=[p])
        c.dma(("sp", "pool")[rt % 2], hyc_d[rt * 128:(rt + 1) * 128, :], p[:], reads=[p], writes=[hyc_d])
    c.pop()


def emit_hy_filters(c, L, w, taps_d):
    N = 2 * L
    CH = min(512, L)
    nch = N // CH
    s = f"hy{L}_"
    c.push()
    w1 = c.sbuf([33, 64]); w2 = c.sbuf([64, 64]); b1 = c.sbuf([64, 1]); b2 = c.sbuf([64, 1]); fq = c.sbuf([64, 2]); fb = c.sbuf([64, 2])
    w3 = c.sbuf([64, 2, 2, 256]); nd = c.sbuf([128, 2])
    for (dst, nm) in ((w1, "hy_w1"), (w2, "hy_w2"), (b1, "hy_b1"), (b2, "hy_b2"), (fq, "hy_freqT"), (w3, "hy_w3"), (nd, "hy_ndT")):
        c.dma("sp", dst[:], w[nm].t.ap(), reads=[w[nm]], writes=[dst])
    c.op("dve", lambda e: e.tensor_tensor(out=fb[:, 0:1], in0=fq[:, 0:1], in1=b1[:], op=ALU.mult), reads=[fq, b1], writes=[fb])
    c.op("dve", lambda e: e.tensor_tensor(out=fb[:, 1:2], in0=fq[:, 1:2], in1=b2[:], op=ALU.mult), reads=[fq, b2], writes=[fb])
    zT = c.sbuf([33, N]); h2T = c.sbuf([64, N]); tnb = c.sbuf([128, N])
    c.dma("sp", zT[:], w[s + "zT"].t.ap(), reads=[w[s + "zT"]], writes=[zT])
    c.dma("pool", tnb[:], w[s + "tn"].t.ap().partition_broadcast(128), reads=[w[s + "tn"]], writes=[tnb])
    ps = Rot(c, 3, [128, 512], psum=True)
    argr = Rot(c, 2, [64, CH]); h1r = Rot(c, 2, [64, CH]); tm = (c.sbuf([64, CH]), c.sbuf([64, CH]))
    for ch in range(nch):
        sl = slice(ch * CH, (ch + 1) * CH)
        p = ps.get()
        c.op("pe", lambda e: e.matmul(out=p[0:64, 0:CH], lhsT=w1[:], rhs=zT[:, sl], start=True, stop=True), reads=[w1, zT], writes=[p])
        ar = argr.get()
        c.op("act", lambda e: e.activation(out=ar[:], in_=p[0:64, 0:CH], func=AF.Identity, scale=fq[:, 0:1], bias=fb[:, 0:1]), reads=[p, fq, fb], writes=[ar])
        h1 = h1r.get()
        emit_sin(c, h1[:], ar[:], [64, CH], tm, [ar], [h1])
        p = ps.get()
        c.op("pe", lambda e: e.matmul(out=p[0:64, 0:CH], lhsT=w2[:], rhs=h1[:], start=True, stop=True), reads=[w2, h1], writes=[p])
        ar = argr.get()
        c.op("act", lambda e: e.activation(out=ar[:], in_=p[0:64, 0:CH], func=AF.Identity, scale=fq[:, 1:2], bias=fb[:, 1:2]), reads=[p, fq, fb], writes=[ar])
        emit_sin(c, h2T[:, sl], ar[:], [64, CH], tm, [ar], [h2T])
    tp = [c.sbuf([128, N]), c.sbuf([128, N])]
    acc = c.sbuf([128, 2, nch]); tot = c.sbuf([128, 2]); decr = Rot(c, 2, [128, CH])
    for ct in range(2):
        for ch in range(nch):
            sl = slice(ch * CH, (ch + 1) * CH)
            dr = 0 if ch * CH < L else 1
            dec = decr.get()
            c.op("act", lambda e: e.activation(out=dec[:], in_=tnb[:, sl], func=AF.Exp, scale=nd[:, ct:ct + 1]), reads=[tnb, nd], writes=[dec])
            for o in range(2):
                p = ps.get()
                c.op("pe", lambda e: e.matmul(out=p[:, 0:CH], lhsT=w3[:, dr, o, ct * 128:(ct + 1) * 128], rhs=h2T[:, sl], start=True, stop=True), reads=[w3, h2T], writes=[p])
                c.op("dve", lambda e: e.tensor_tensor(out=tp[o][:, sl], in0=p[:, 0:CH], in1=dec[:], op=ALU.mult), reads=[p, dec], writes=[tp[o]])
                c.op("dve", lambda e: e.tensor_reduce(out=acc[:, o, ch:ch + 1], in_=tp[o][:, sl], axis=AX.X, op=ALU.add, apply_absolute_value=True), reads=[tp[o]], writes=[acc])
        for o in range(2):
            c.op("dve", lambda e: e.tensor_reduce(out=tot[:, o:o + 1], in_=acc[:, o, :], axis=AX.X, op=ALU.add), reads=[acc], writes=[tot])
        c.op("dve", lambda e: e.reciprocal(out=tot[:], in_=tot[:]), reads=[tot], writes=[tot])
        for o in range(2):
            c.op(("dve", "pool")[o], lambda e: e.tensor_scalar(out=tp[o][:], in0=tp[o][:], scalar1=tot[:, o:o + 1], scalar2=1.0, op0=ALU.mult, op1=ALU.mult), reads=[tp[o], tot], writes=[tp[o]])
            c.dma(("sp", "pool")[o], taps_d[o, ct * 128:(ct + 1) * 128, :], tp[o][:], reads=[tp[o]], writes=[taps_d])
    c.pop()


def emit_hy_conv(c, L, tok0, w, hyc_d, taps_d, ident, mixT_d, row0):
    N = 2 * L
    N1 = N // 128
    F1 = N1 // 2 + 1
    T1 = L // 128
    F2 = 2 * F1
    NPB = 512 // (2 * CG)
    NPA = 512 // F2
    s = f"hy{L}_"
    c.push()
    W1 = c.sbuf([N1, F2]); Minv = c.sbuf([F2, T1])
    TBh = c.sbuf([128, F1, 2, 128], BF16); TBl = c.sbuf([128, F1, 2, 128], BF16)
    TBTh = c.sbuf([128, F1, 2, 128], BF16); TBTl = c.sbuf([128, F1, 2, 128], BF16)
    c.dma("sp", W1[:], w[s + "W1"].t.ap(), reads=[w[s + "W1"]], writes=[W1])
    for (dst_, nm_, q_) in ((TBh, "TBh", "sp"), (TBl, "TBl", "pool"), (TBTh, "TBTh", "sp"), (TBTl, "TBTl", "pool")):
        c.dma(q_, dst_[:], w[s + nm_].t.ap(), reads=[w[s + nm_]], writes=[dst_])
    c.dma("sp", Minv[:], w[s + "Minv"].t.ap(), reads=[w[s + "Minv"]], writes=[Minv])
    bias = c.sbuf([T1, 2, 256])
    c.dma("sp", bias[:], w["hy_bias"].t.ap().partition_broadcast(T1), reads=[w["hy_bias"]], writes=[bias])
    xp = Rot(c, 3, [N1, CG, 128])
    Ah = c.sbuf([128, F1, 3, CG], BF16); Al = c.sbuf([128, F1, 3, CG], BF16)
    Ksp = [c.sbuf([128, F1, 2, CG]), c.sbuf([128, F1, 2, CG])]
    Yh = c.sbuf([128, F1, 3, CG], BF16); Yl = c.sbuf([128, F1, 3, CG], BF16)
    y32 = Rot(c, 2, [128, 2, NPB, CG])
    G = c.sbuf([128, CG, F2])
    GT = c.sbuf([F2, CG, 128])
    tmp = c.sbuf([T1, 512]); tmq = Rot(c, 2, [128, NPB, CG]); tmr = Rot(c, 2, [128, NPB, CG])
    psA = Rot(c, 2, [128, 512], psum=True); psB = Rot(c, 3, [128, 512], psum=True); psT = Rot(c, 2, [128, 512], psum=True)
    evi = [0]

    def evac(out, in_, rd, wr, scale=None):
        evi[0] += 1
        if scale is not None:
            c.op("act", lambda e: e.activation(out=out, in_=in_, func=AF.Copy, scale=scale), reads=rd, writes=wr)
        elif evi[0] % 2 == 0:
            c.op("act", lambda e: e.copy(out=out, in_=in_), reads=rd, writes=wr)
        else:
            c.op("dve", lambda e: e.tensor_copy(out=out, in_=in_), reads=rd, writes=wr)

    def fwd(X, R, sink):
        for c0 in range(0, CG, NPA):
            n = min(NPA, CG - c0)
            pa = psA.get()
            for ci in range(n):
                c.op("pe", lambda e: e.matmul(out=pa[:, ci * F2:(ci + 1) * F2], lhsT=X[0:R, c0 + ci, :], rhs=W1[0:R, :], start=True, stop=True),
                     reads=[X, W1], writes=[pa])
            src = pa[:, 0:n * F2].rearrange("p (c r f) -> p c r f", c=n, r=2)
            hv = Ah[:, :, 0:2, c0:c0 + n].rearrange("p f r c -> p c r f")
            lv = Al[:, :, 0:2, c0:c0 + n].rearrange("p f r c -> p c r f")
            c.op("act", lambda e: e.copy(out=hv, in_=src), reads=[pa], writes=[Ah])
            c.op("dve", lambda e: e.tensor_tensor(out=lv, in0=src, in1=hv, op=ALU.subtract), reads=[pa, Ah], writes=[Al])
            c.op("act", lambda e: e.activation(out=Ah[:, :, 2, c0:c0 + n], in_=Ah[:, :, 0, c0:c0 + n], func=AF.Copy, scale=-1.0), reads=[Ah], writes=[Ah])
            c.op("pool", lambda e: e.tensor_scalar(out=Al[:, :, 2, c0:c0 + n], in0=Al[:, :, 0, c0:c0 + n], scalar1=-1.0, scalar2=1.0, op0=ALU.mult, op1=ALU.mult), reads=[Al], writes=[Al])
        for f0 in range(0, F1, NPB):
            nf = min(NPB, F1 - f0)
            pb = psB.get()
            for fi in range(nf):
                f = f0 + fi
                o_ = pb[:, fi * 2 * CG:(fi + 1) * 2 * CG]
                seq = [(TBh, 0, Ah, slice(0, 2)), (TBh, 0, Al, slice(0, 2)), (TBl, 0, Ah, slice(0, 2)),
                       (TBh, 1, Ah, slice(1, 3)), (TBh, 1, Al, slice(1, 3)), (TBl, 1, Ah, slice(1, 3))]
                for k_, (tb_, cs_, dat_, sl_) in enumerate(seq):
                    c.op("pe", lambda e: e.matmul(out=o_, lhsT=tb_[:, f, cs_, :], rhs=dat_[:, f, sl_, :].rearrange("p r c -> p (r c)"), start=(k_ == 0), stop=(k_ == 5)),
                         reads=[tb_, dat_], writes=[pb])
            sink(pb, f0, nf)

    def inverse(epi):
        for f0 in range(0, F1, NPB):
            nf = min(NPB, F1 - f0)
            pb = psB.get()
            for fi in range(nf):
                f = f0 + fi
                o_ = pb[:, fi * 2 * CG:(fi + 1) * 2 * CG]
                seq = [(TBTh, 0, Yh, slice(1, 3)), (TBTh, 0, Yl, slice(1, 3)), (TBTl, 0, Yh, slice(1, 3)),
                       (TBTh, 1, Yh, slice(0, 2)), (TBTh, 1, Yl, slice(0, 2)), (TBTl, 1, Yh, slice(0, 2))]
                for k_, (tb_, cs_, dat_, sl_) in enumerate(seq):
                    c.op("pe", lambda e: e.matmul(out=o_, lhsT=tb_[:, f, cs_, :], rhs=dat_[:, f, sl_, :].rearrange("p r c -> p (r c)"), start=(k_ == 0), stop=(k_ == 5)),
                         reads=[tb_, dat_], writes=[pb])
            src = pb[:, 0:nf * 2 * CG].rearrange("p (f r c) -> p f r c", f=nf, r=2)
            for r in range(2):
                evac(G[:, :, r * F1 + f0:r * F1 + f0 + nf].rearrange("p c f -> p f c"), src[:, :, r, :], [pb], [G])
        for c0 in range(0, CG, 4):
            pt_ = psT.get()
            for ci in range(4):
                c.op("pe", lambda e: e.transpose(out=pt_[0:F2, ci * 128:(ci + 1) * 128], in_=G[:, c0 + ci, :], identity=ident[:]), reads=[G, ident], writes=[pt_])
            evac(GT[:, c0:c0 + 4, :], pt_[0:F2, :].rearrange("p (c t) -> p c t", c=4), [pt_], [GT])
        for c0 in range(0, CG, 4):
            py = psA.get()
            c.op("pe", lambda e: e.matmul(out=py[0:T1, :], lhsT=Minv[:], rhs=GT[:, c0:c0 + 4, :], start=True, stop=True), reads=[Minv, GT], writes=[py])
            epi(py, c0)

    for g in range(256 // CG):
        ch0 = g * CG

        def load(dst, R, src_d, r0, t0_):
            c.dma(rr(["sp", "pool"]), dst[0:R, :, :], src_d[r0:r0 + CG, t0_:t0_ + R * 128].rearrange("c (a b) -> a c b", b=128), reads=[src_d], writes=[dst])
        for o in range(2):
            X = xp.get()
            c.dma(rr(["sp", "pool"]), X[:], taps_d[o, ch0:ch0 + CG, :].rearrange("c (a b) -> a c b", b=128), reads=[taps_d], writes=[X])

            def ksink(pb, f0, nf, o=o):
                evac(Ksp[o][:, f0:f0 + nf, :, :], pb[:, 0:nf * 2 * CG].rearrange("p (f r c) -> p f r c", f=nf, r=2), [pb], [Ksp[o]])
            fwd(X, N1, ksink)
        pv = xp.get()
        load(pv, T1, hyc_d, ch0, tok0)
        px1 = xp.get()
        load(px1, T1, hyc_d, 256 + ch0, tok0)

        def make_psink(o):
            def psink(pb, f0, nf):
                src = pb[:, 0:nf * 2 * CG].rearrange("p (f r c) -> p f r c", f=nf, r=2)
                Xr = src[:, :, 0, :]; Xi = src[:, :, 1, :]
                Kr = Ksp[o][:, f0:f0 + nf, 0, :]; Ki = Ksp[o][:, f0:f0 + nf, 1, :]
                a = tmq.get(); b = tmr.get(); yy = y32.get()
                c.op("dve", lambda e: e.tensor_tensor(out=a[:, 0:nf, :], in0=Xr, in1=Kr, op=ALU.mult), reads=[pb, Ksp[o]], writes=[a])
                c.op("dve", lambda e: e.tensor_tensor(out=b[:, 0:nf, :], in0=Xi, in1=Ki, op=ALU.mult), reads=[pb, Ksp[o]], writes=[b])
                c.op("pool", lambda e: e.tensor_tensor(out=yy[:, 0, 0:nf, :], in0=a[:, 0:nf, :], in1=b[:, 0:nf, :], op=ALU.subtract), reads=[a, b], writes=[yy])
                a = tmq.get(); b = tmr.get()
                c.op("dve", lambda e: e.tensor_tensor(out=a[:, 0:nf, :], in0=Xr, in1=Ki, op=ALU.mult), reads=[pb, Ksp[o]], writes=[a])
                c.op("dve", lambda e: e.tensor_tensor(out=b[:, 0:nf, :], in0=Xi, in1=Kr, op=ALU.mult), reads=[pb, Ksp[o]], writes=[b])
                c.op("pool", lambda e: e.tensor_tensor(out=yy[:, 1, 0:nf, :], in0=a[:, 0:nf, :], in1=b[:, 0:nf, :], op=ALU.add), reads=[a, b], writes=[yy])
                hsl = Yh[:, f0:f0 + nf, 1:3, :].rearrange("p f r c -> p r f c")
                lsl = Yl[:, f0:f0 + nf, 1:3, :].rearrange("p f r c -> p r f c")
                c.op("act", lambda e: e.copy(out=hsl, in_=yy[:, :, 0:nf, :]), reads=[yy], writes=[Yh])
                c.op("pool", lambda e: e.tensor_tensor(out=lsl, in0=yy[:, :, 0:nf, :], in1=hsl, op=ALU.subtract), reads=[yy, Yh], writes=[Yl])
                c.op("act", lambda e: e.activation(out=Yh[:, f0:f0 + nf, 0, :], in_=Yh[:, f0:f0 + nf, 2, :], func=AF.Copy, scale=-1.0), reads=[Yh], writes=[Yh])
                c.op("act", lambda e: e.activation(out=Yl[:, f0:f0 + nf, 0, :], in_=Yl[:, f0:f0 + nf, 2, :], func=AF.Copy, scale=-1.0), reads=[Yl], writes=[Yl])
            return psink

        def make_epi(o, src_lin, gate, dst):
            def epi(py, c0):
                bb = bias[:, o, ch0 + c0:ch0 + c0 + 4].unsqueeze(2).broadcast_to([T1, 4, 128])
                t3 = tmp[:, :].rearrange("p (c t) -> p c t", c=4)
                c.op("dve", lambda e: e.tensor_tensor(out=t3, in0=src_lin[0:T1, c0:c0 + 4, :], in1=bb, op=ALU.mult), reads=[src_lin, bias], writes=[tmp])
                c.op("dve", lambda e: e.tensor_tensor(out=t3, in0=py[0:T1, :].rearrange("p (c t) -> p c t", c=4), in1=t3, op=ALU.add), reads=[py, tmp], writes=[tmp])
                c.op("pool", lambda e: e.tensor_tensor(out=dst[0:T1, c0:c0 + 4, :], in0=t3, in1=gate[0:T1, c0:c0 + 4, :], op=ALU.mult), reads=[tmp, gate], writes=[dst])
            return epi
        fwd(pv, T1, make_psink(0))
        zt = xp.get()
        inverse(make_epi(0, pv, px1, zt))
        px2 = xp.get()
        load(px2, T1, hyc_d, 512 + ch0, tok0)
        fwd(zt, T1, make_psink(1))
        yt = xp.get()
        inverse(make_epi(1, zt, px2, yt))
        c.dma(rr(["sp", "pool"]), mixT_d[row0 + ch0:row0 + ch0 + CG, tok0:tok0 + L].rearrange("c (a b) -> a c b", b=128), yt[0:T1, :, :], reads=[yt], writes=[mixT_d])
    c.pop()


NFF = DFF // 128
WIN = 640
OWN = 512
NWINTOK = 2048 + 128


def emit_cast_dram(c, src_d, dst_d, R, X):
    c.push()
    st = Rot(c, 2, [128, X]); sb = Rot(c, 2, [128, X], BF16)
    for r in range(R):
        a = st.get(); b = sb.get()
        c.dma(("sp", "pool")[r % 2], a[:], src_d[r], reads=[src_d], writes=[a])
        eng = ("act", "dve", "pool")[r % 3]
        if eng == "act":
            c.op("act", lambda e: e.copy(out=b[:], in_=a[:]), reads=[a], writes=[b])
        else:
            c.op(eng, lambda e: e.tensor_copy(out=b[:], in_=a[:]), reads=[a], writes=[b])
        c.dma(("pool", "sp")[r % 2], dst_d[r], b[:], reads=[b], writes=[dst_d])
    c.pop()


def emit_B(c, t, last, blocks):
    ZKC = [6, 7, 14, 15]
    c.push()
    modsT = t["modsT_sb"]
    g2 = c.sbuf([128, KC]); c.dma("sp", g2[:], t["g2T"].t.ap(), reads=[t["g2T"]], writes=[g2])
    G = c.sbuf([128, 2, KC]); S = c.sbuf([128, 2, KC])
    for r in range(2):
        c.op("dve", lambda e: e.scalar_tensor_tensor(out=G[:, r, :], in0=modsT[:, 64:80, r], scalar=1.0, in1=g2[:], op0=ALU.add, op1=ALU.mult), reads=[modsT, g2], writes=[G])
        c.op("dve", lambda e: e.tensor_copy(out=S[:, r, :], in_=modsT[:, 48:64, r]), reads=[modsT], writes=[S])
    gluw = c.sbuf([128, 4, 512], BF16); glub = c.sbuf([128, 4])
    c.push()
    gst = c.sbuf([128, 4, 512])
    c.dma("sp", gst[:], t["gluw"].t.ap().rearrange("(kc p) n -> p kc n", p=128), reads=[t["gluw"]], writes=[gst])
    c.op("dve", lambda e: e.tensor_copy(out=gluw[:], in_=gst[:]), reads=[gst], writes=[gluw])
    c.pop()
    c.dma("sp", glub[:], t["glubT"].t.ap(), reads=[t["glubT"]], writes=[glub])
    cw = c.sbuf([128, NFF, 9]); cb = c.sbuf([128, NFF])
    c.dma("sp", cw[:], t["cwT"].t.ap(), reads=[t["cwT"]], writes=[cw])
    c.dma("sp", cb[:], t["cbT"].t.ap(), reads=[t["cbT"]], writes=[cb])
    nf = c.sbuf([128, KC])
    if last:
        c.dma("sp", nf[:], t["nfT"].t.ap(), reads=[t["nfT"]], writes=[nf])
    ones = c.sbuf([128, 128]); c.op("dve", lambda e: e.memset(ones[:], 1.0), writes=[ones])
    mask = c.sbuf([128, WIN])
    hb = c.sbuf([128, KC, WIN]); hviews = [hb.view() for _ in range(KC)]
    mx = c.sbuf([128, KC, WIN], BF16); mviews = [mx.view() for _ in range(KC)]
    w640 = Rot(c, 6, [128, WIN]); w512 = Rot(c, 5, [128, OWN])
    mst = sqr = tmpr = gr = sgr = w640
    accr = ger = outr = w512
    hff = c.sbuf([128, NFF, OWN], BF16); fviews = [hff.view() for _ in range(NFF)]
    wor = Rot(c, 2, [128, KC, 128], BF16); wur = Rot(c, 2, [128, KC, 256], BF16); wdr = Rot(c, 3, [128, NFF // 2, 128], BF16)
    psG = Rot(c, 2, [128, 512], psum=True); psV = Rot(c, 2, [128, 512], psum=True); psO = Rot(c, 2, [128, 512], psum=True); psS = Rot(c, 1, [128, 512], psum=True)
    sd = c.sbuf([128, WIN]); rstd = c.sbuf([128, WIN])
    z32v = [c.sbuf([128, WIN]) for _ in range(4)]
    z16 = c.sbuf([128, 4, WIN], BF16); z16v = [z16.view() for _ in range(4)]
    hsrc = t["hsrc"]
    mixbufs = t["mixbufs"]

    def mm_tok(ps_list, lhsT_fn, rhs_buf, rviews, a, b, nk, extra_reads):
        pieces = []
        s0 = a
        while s0 < b:
            n = min(512, b - s0)
            pieces.append((s0, n))
            s0 += n
        for (pi, (s0, n)) in enumerate(pieces):
            ps = ps_list[pi]
            for kc in range(nk):
                c.op("pe", lambda e: e.matmul(out=ps[:, 0:n], lhsT=lhsT_fn(kc), rhs=rhs_buf[:, kc, s0:s0 + n], start=(kc == 0), stop=(kc == nk - 1)),
                     reads=[rviews[kc]] + extra_reads, writes=[ps])
        return pieces

    def block(col0, W, oo, O, r, grid, mk, dst, dyn):
        def q2(i):
            return ("sp", "pool")[i % 2]

        def load(i, dst_ap, dst_buf, src_buf, row0, out_dt_buf=None):
            c.dma(q2(i), dst_ap, src_buf[row0:row0 + 128, col0:col0 + W], reads=[src_buf], writes=[dst_buf])
            if dyn is not None:
                col1, sel = dyn
                b_ = mst.get()
                c.dma(q2(i + 1), b_[:, 0:W], src_buf[row0:row0 + 128, col1:col1 + W], reads=[src_buf], writes=[b_])
                e1 = ("dve", "pool")[i % 2]
                c.op(e1, lambda e: e.tensor_scalar(out=dst_ap, in0=dst_ap, scalar1=sel[:, 0:1], scalar2=1.0, op0=ALU.mult, op1=ALU.mult), reads=[dst_buf, sel], writes=[dst_buf])
                c.op("dve", lambda e: e.scalar_tensor_tensor(out=dst_ap, in0=b_[:, 0:W], scalar=sel[:, 1:2], in1=dst_ap, op0=ALU.mult, op1=ALU.add), reads=[b_, sel, dst_buf], writes=[dst_buf])
        if mk is not None:
            c.dma("sp", mask[:, 0:W], mk[0][:, mk[1]:mk[1] + W], reads=[mk[0]], writes=[mask])
        for kc in range(KC):
            mb = mixbufs[kc // 8]
            mrow = (kc % 8) * 128
            if kc not in ZKC and dyn is None:
                c.dma("pool", mx[:, kc, 0:W], mb[mrow:mrow + 128, col0:col0 + W], reads=[mb], writes=[mviews[kc]])
            elif kc not in ZKC:
                m_ = mst.get()
                load(kc, m_[:, 0:W], m_, mb, mrow)
                if kc % 2 == 0:
                    c.op("act", lambda e: e.copy(out=mx[:, kc, 0:W], in_=m_[:, 0:W]), reads=[m_], writes=[mviews[kc]])
                else:
                    c.op("pool", lambda e: e.tensor_copy(out=mx[:, kc, 0:W], in_=m_[:, 0:W]), reads=[m_], writes=[mviews[kc]])
            else:
                zi = ZKC.index(kc)
                load(kc, z32v[zi][:, 0:W], z32v[zi], mb, mrow)
                c.op("act", lambda e: e.copy(out=z16[:, zi, 0:W], in_=z32v[zi][:, 0:W]), reads=[z32v[zi]], writes=[z16v[zi]])
            load(kc + 1, hb[:, kc, 0:W], hviews[kc], hsrc, kc * 128)
        for n_ in range(4):
            pl = [psO.get(), psO.get()]
            pieces = mm_tok(pl, lambda kc: gluw[:, kc, n_ * 128:(n_ + 1) * 128], z16, z16v, 0, W, 4, [gluw])
            for (pi, (s0, n)) in enumerate(pieces):
                sg = sgr.get()
                c.op("act", lambda e: e.activation(out=sg[:, 0:n], in_=pl[pi][:, 0:n], func=AF.Sigmoid, bias=glub[:, n_:n_ + 1]), reads=[pl[pi], glub], writes=[sg])
                c.op("dve", lambda e: e.tensor_tensor(out=mx[:, ZKC[n_], s0:s0 + n], in0=sg[:, 0:n], in1=z32v[n_][:, s0:s0 + n], op=ALU.mult), reads=[sg, z32v[n_]], writes=[mviews[ZKC[n_]]])
        for m in range(KC):
            wo = wor.get()
            c.dma(("sp", "pool")[m % 2], wo[:], t["wout16"][m].rearrange("p (k n) -> p k n", k=KC), reads=[t["wout16_v"][m // 8]], writes=[wo])
            pl = [psO.get(), psO.get()]
            pieces = mm_tok(pl, lambda kc: wo[:, kc, :], mx, mviews, 0, W, KC, [wo])
            for (pi, (s0, n)) in enumerate(pieces):
                c.op("dve", lambda e: e.scalar_tensor_tensor(out=hb[:, m, s0:s0 + n], in0=pl[pi][:, 0:n], scalar=modsT[:, 32 + m, r:r + 1], in1=hb[:, m, s0:s0 + n], op0=ALU.mult, op1=ALU.add),
                     reads=[pl[pi], modsT, hviews[m]], writes=[hviews[m]])
        pieces = []
        s0 = 0
        while s0 < W:
            n = min(512, W - s0); pieces.append((s0, n)); s0 += n
        for (s0, n) in pieces:
            ss = psS.get()
            for kc in range(KC):
                sq = sqr.get()
                c.op("act", lambda e: e.activation(out=sq[:, 0:n], in_=hb[:, kc, s0:s0 + n], func=AF.Square), reads=[hviews[kc]], writes=[sq])
                c.op("pe", lambda e: e.matmul(out=ss[:, 0:n], lhsT=ones[:], rhs=sq[:, 0:n], start=(kc == 0), stop=(kc == KC - 1)), reads=[ones, sq], writes=[ss])
            c.op("act", lambda e: e.activation(out=sd[:, s0:s0 + n], in_=ss[:, 0:n], func=AF.Sqrt, scale=1.0 / D, bias=EPS), reads=[ss], writes=[sd])
        c.op("dve", lambda e: e.reciprocal(out=rstd[:, 0:W], in_=sd[:, 0:W]), reads=[sd], writes=[rstd])
        for kc in range(KC):
            tm_ = tmpr.get()
            c.op("dve", lambda e: e.tensor_tensor(out=tm_[:, 0:W], in0=hb[:, kc, 0:W], in1=rstd[:, 0:W], op=ALU.mult), reads=[hviews[kc], rstd], writes=[tm_])
            c.op("act", lambda e: e.activation(out=mx[:, kc, 0:W], in_=tm_[:, 0:W], func=AF.Identity, scale=G[:, r, kc:kc + 1], bias=S[:, r, kc:kc + 1]),
                 reads=[tm_, G, S], writes=[mviews[kc]])
        for j in range(NFF):
            wu = wur.get()
            c.dma(("sp", "pool")[j % 2], wu[:], t["wup16"][j].rearrange("p (k n) -> p k n", k=KC), reads=[t["wup16_v"][j // 4]], writes=[wu])
            gl = [psG.get(), psG.get()]
            gp = mm_tok(gl, lambda kc: wu[:, kc, 0:128], mx, mviews, 0, W, KC, [wu])
            vl = [psV.get()]
            mm_tok(vl, lambda kc: wu[:, kc, 128:256], mx, mviews, oo, oo + O, KC, [wu])
            g = gr.get()
            for (pi, (s0, n)) in enumerate(gp):
                if mk is None:
                    c.op("act", lambda e: e.copy(out=g[:, s0:s0 + n], in_=gl[pi][:, 0:n]), reads=[gl[pi]], writes=[g])
                else:
                    c.op("dve", lambda e: e.tensor_tensor(out=g[:, s0:s0 + n], in0=gl[pi][:, 0:n], in1=mask[:, s0:s0 + n], op=ALU.mult), reads=[gl[pi], mask], writes=[g])
            acc = accr.get()
            if grid:
                g3 = g[:, 0:W].rearrange("p (r x) -> p r x", x=64)
                a3 = acc[:, 0:O].rearrange("p (r x) -> p r x", x=64)
                nr = O // 64
                c.op("dve", lambda e: e.tensor_scalar(out=a3, in0=g3[:, 1:1 + nr, :], scalar1=cw[:, j, 4:5], scalar2=cb[:, j:j + 1], op0=ALU.mult, op1=ALU.add), reads=[g, cw, cb], writes=[acc])
                for dr in (-1, 0, 1):
                    for dc in (-1, 0, 1):
                        if dr == 0 and dc == 0:
                            continue
                        k = (dr + 1) * 3 + (dc + 1)
                        xo = slice(max(0, -dc), 64 - max(0, dc))
                        xi = slice(max(0, dc), 64 - max(0, -dc))
                        c.op("dve", lambda e: e.scalar_tensor_tensor(out=a3[:, :, xo], in0=g3[:, 1 + dr:1 + dr + nr, xi], scalar=cw[:, j, k:k + 1], in1=a3[:, :, xo], op0=ALU.mult, op1=ALU.add),
                             reads=[g, cw, acc], writes=[acc])
            else:
                c.op("dve", lambda e: e.tensor_scalar(out=acc[:, 0:O], in0=g[:, 0:O], scalar1=cw[:, j, 4:5], scalar2=cb[:, j:j + 1], op0=ALU.mult, op1=ALU.add), reads=[g, cw, cb], writes=[acc])
                c.op("dve", lambda e: e.scalar_tensor_tensor(out=acc[:, 1:O], in0=g[:, 0:O - 1], scalar=cw[:, j, 3:4], in1=acc[:, 1:O], op0=ALU.mult, op1=ALU.add), reads=[g, cw, acc], writes=[acc])
                c.op("dve", lambda e: e.scalar_tensor_tensor(out=acc[:, 0:O - 1], in0=g[:, 1:O], scalar=cw[:, j, 5:6], in1=acc[:, 0:O - 1], op0=ALU.mult, op1=ALU.add), reads=[g, cw, acc], writes=[acc])
            ge = ger.get()
            c.op("act", lambda e: e.activation(out=ge[:, 0:O], in_=acc[:, 0:O], func=AF.Gelu_apprx_tanh), reads=[acc], writes=[ge])
            c.op("dve", lambda e: e.tensor_tensor(out=hff[:, j, 0:O], in0=vl[0][:, 0:O], in1=ge[:, 0:O], op=ALU.mult), reads=[vl[0], ge], writes=[fviews[j]])
        dbuf, dcol = dst
        for m in range(KC):
            wda = wdr.get(); wdb = wdr.get()
            wsrc = t["wdn16"][m].rearrange("p (k n) -> p k n", k=NFF)
            c.dma("sp", wda[:], wsrc[:, 0:NFF // 2, :], reads=[t["wdn16_v"][m // 4]], writes=[wda])
            c.dma("pool", wdb[:], wsrc[:, NFF // 2:NFF, :], reads=[t["wdn16_v"][m // 4]], writes=[wdb])
            pl = [psO.get()]
            mm_tok(pl, lambda kc: (wda[:, kc, :] if kc < NFF // 2 else wdb[:, kc - NFF // 2, :]), hff, fviews, 0, O, NFF, [wda, wdb])
            c.op("dve", lambda e: e.scalar_tensor_tensor(out=hb[:, m, oo:oo + O], in0=pl[0][:, 0:O], scalar=modsT[:, 80 + m, r:r + 1], in1=hb[:, m, oo:oo + O], op0=ALU.mult, op1=ALU.add),
                 reads=[pl[0], modsT, hviews[m]], writes=[hviews[m]])
            if not last:
                c.dma(("pool", "sp")[m % 2], dbuf[m * 128:(m + 1) * 128, dcol:dcol + O], hb[:, m, oo:oo + O], reads=[hviews[m]], writes=[dbuf])
        if last:
            ss = psS.get()
            for kc in range(KC):
                sq = sqr.get()
                c.op("act", lambda e: e.activation(out=sq[:, 0:O], in_=hb[:, kc, oo:oo + O], func=AF.Square), reads=[hviews[kc]], writes=[sq])
                c.op("pe", lambda e: e.matmul(out=ss[:, 0:O], lhsT=ones[:], rhs=sq[:, 0:O], start=(kc == 0), stop=(kc == KC - 1)), reads=[ones, sq], writes=[ss])
            c.op("act", lambda e: e.activation(out=sd[:, 0:O], in_=ss[:, 0:O], func=AF.Sqrt, scale=1.0 / D, bias=EPS), reads=[ss], writes=[sd])
            c.op("dve", lambda e: e.reciprocal(out=rstd[:, 0:O], in_=sd[:, 0:O]), reads=[sd], writes=[rstd])
            for kc in range(KC):
                ot = outr.get()
                c.op("dve", lambda e: e.scalar_tensor_tensor(out=ot[:, 0:O], in0=hb[:, kc, oo:oo + O], scalar=nf[:, kc:kc + 1], in1=rstd[:, 0:O], op0=ALU.mult, op1=ALU.mult),
                     reads=[hviews[kc], nf, rstd], writes=[ot])
                c.dma(("pool", "sp")[kc % 2], dbuf[kc * 128:(kc + 1) * 128, dcol:dcol + O], ot[:, 0:O], reads=[ot], writes=[dbuf])

    for b_ in blocks:
        block(b_["col0"], b_["W"], b_["oo"], b_["O"], b_["r"], b_["grid"], b_.get("mask"), b_["dst"], b_.get("dyn"))
    c.pop()


def B_host_weights(inp, l, last):
    w = inp["ffn_w_up"][l]
    wup = np.stack([w[:, :DFF].reshape(KC, 128, NFF, 128), w[:, DFF:].reshape(KC, 128, NFF, 128)], axis=3)
    wup = np.ascontiguousarray(wup.transpose(2, 1, 0, 3, 4)).reshape(NFF, 128, KC * 256)
    wdn = np.ascontiguousarray(inp["ffn_w_down"][l].reshape(NFF, 128, KC, 128).transpose(2, 1, 0, 3)).reshape(KC, 128, NFF * 128)
    wout = np.ascontiguousarray(inp["w_out"][l].reshape(KC, 128, KC, 128).transpose(2, 1, 0, 3)).reshape(KC, 128, KC * 128)
    d = {"wup": wup, "wdn": wdn, "wout": wout,
         "g2T": np.ascontiguousarray(inp["norm2_g"][l].reshape(KC, 128).T),
         "gluw": inp["s5_glu_w"][l], "glubT": np.ascontiguousarray(inp["s5_glu_b"][l].reshape(4, 128).T),
         "cwT": np.ascontiguousarray(inp["ffn_conv_w"][l].reshape(9, NFF, 128).transpose(2, 1, 0)),
         "cbT": np.ascontiguousarray(inp["ffn_conv_b"][l].reshape(NFF, 128).T)}
    if last:
        d["nfT"] = np.ascontiguousarray(inp["norm_f"].reshape(KC, 128).T)
    return d


TP = T + 64
MIX_PERM = np.concatenate([np.arange(0, 256), np.arange(512, 1024), np.arange(1536, 1792),
                           np.arange(256, 512), np.arange(1024, 1536), np.arange(1792, 2048)])


def A_const_inputs():
    return {**ret_consts(), **s5_consts(), **hy_tables(SEQ), **hy_tables(CTX), "ident": np.eye(128, dtype=np.float32)}


def A_weight_inputs(inp, l, hf):
    r = np.arange
    colsf = np.concatenate([r(256) + 256 * hf, 512 + r(256) + 256 * hf, 1024 + r(256) + 256 * hf, 1536 + 256 * hf + r(256),
                            2048 + 256 * hf + r(256), 4608 + 256 * hf + r(256)])
    colst = np.concatenate([2048 + 256 * hf + r(256), 2560 + 512 * hf + r(512), 3584 + 512 * hf + r(512)])
    d = {"win": np.ascontiguousarray(inp["w_in"][l][:, np.concatenate([colsf, colst])]),
         "rdec": np.ascontiguousarray(inp["ret_decay"][l][:, 4 * hf:4 * hf + 4]).reshape(-1)}
    d.update(s5_host(inp, l, hf))
    d.update(hy_host(inp, l, hf))
    return d


def layer_inputs(inp, l):
    last = l == 1
    d = {"adaw": inp["ada_w"][l], "adabT": np.ascontiguousarray(inp["ada_b"][l].reshape(96, 128).T),
         "g1T": np.ascontiguousarray(inp["norm1_g"][l].reshape(KC, 128).T)}
    inp2 = dict(inp)
    wo = np.array(inp["w_out"], copy=True)
    wo[l] = inp["w_out"][l][MIX_PERM, :]
    inp2["w_out"] = wo
    d.update(B_host_weights(inp2, l, last))
    out = {f"{k}_{l}": v for k, v in d.items()}
    for hf in range(2):
        for k, v in A_weight_inputs(inp, l, hf).items():
            out[f"{k}_{l}{hf}"] = v
    return out


CONST_NAMES = (["ret_pf", "ret_pb", "ret_posq", "ret_posk", "s5_pos", "ident"]
               + [f"hy{L_}_{n_}" for L_ in (SEQ, CTX) for n_ in ("W1", "TBh", "TBl", "TBTh", "TBTl", "Minv", "zT", "tn")])


def build_fused(shapes):
    c = Ctx()
    nc = c.nc
    t = {n: c.dram(n, list(sh), BF16 if is16 else F32, "ExternalInput") for n, (sh, is16) in shapes.items()}
    out_d = c.dram("out", [D, 2048], F32, "ExternalOutput")
    sc = {"hyT": c.dram("hyT", [768, T], F32), "qT": c.dram("qT", [256, T], BF16), "kT": c.dram("kT", [256, T], BF16),
          "uT": c.dram("uT", [256, T], F32), "k": c.dram("k", [T, 256], BF16), "v": c.dram("v", [T, 512], BF16),
          "sg": c.dram("sg", [T, 512], F32)}
    hyc_d = c.dram("hyc", [768, T], F32)
    tapsL = c.dram("tapsL", [2, 256, 2 * SEQ], F32)
    tapsC = c.dram("tapsC", [2, 256, 2 * CTX], F32)
    mixbufs = [c.dram("mixT0", [1024, TP], F32), c.dram("mixT1", [1024, TP], F32)]
    h1T = c.dram("h1T", [D, TP], F32)
    w16 = [{"wup16": c.dram(f"wup16_{l}", [NFF, 128, KC * 256], BF16), "wdn16": c.dram(f"wdn16_{l}", [KC, 128, NFF * 128], BF16),
            "wout16": c.dram(f"wout16_{l}", [KC, 128, KC * 128], BF16)} for l in range(2)]
    c.push()
    modsT = c.sbuf([128, 96, 2])
    ident = c.sbuf([128, 128])
    c.dma("sp", ident[:], t["ident"].t.ap(), reads=[t["ident"]], writes=[ident])
    zt = c.sbuf([128, 16, 64])
    c.op("dve", lambda e: e.memset(zt[:], 0.0), writes=[zt])
    for mb in mixbufs:
        c.dma("sp", mb.t.ap().rearrange("(a p) t -> p a t", p=128)[:, :, T:TP], zt[:, 0:8, :], reads=[zt], writes=[mb])
    c.dma("sp", h1T.t.ap().rearrange("(a p) t -> p a t", p=128)[:, :, T:TP], zt[:], reads=[zt], writes=[h1T])
    sel = c.sbuf([128, 2])
    c.dma("sp", sel[:], t["sel"].t.ap(), reads=[t["sel"]], writes=[sel])
    pending = []
    for l in range(2):
        for (nm, R, step) in (("wup", NFF, 4), ("wdn", KC, 4), ("wout", KC, 8)):
            for r0 in range(0, R, step):
                w16[l].setdefault(nm + "16_v", []).append(w16[l][nm + "16"].view())
                pending.append((w16[l][nm + "16"], w16[l][nm + "16_v"][-1], t[f"{nm}_{l}"], r0, step))

    def bg_issue(k):
        for _ in range(k):
            if pending:
                dst, dview, src_, r0, step = pending.pop(0)
                c.dma("pool", dst[r0:r0 + step], src_[r0:r0 + step], reads=[], writes=[dview], own_sem=True)
    for l in range(2):
        last = l == 1
        emit_mods(c, t["cT"], t[f"adaw_{l}"], t[f"adabT_{l}"], modsT)
        src = t["hT"] if l == 0 else h1T
        for hf in range(2):
            w = {k: t[k] for k in CONST_NAMES}
            sfx = f"_{l}{hf}"
            for k, v in t.items():
                if k.endswith(sfx):
                    w[k[:-len(sfx)]] = v
            emit_inproj(c, src, w["win"], t[f"g1T_{l}"], modsT, sc)
            bg_issue(5)
            emit_hy_shortconv(c, sc["hyT"], w, hyc_d)
            if not last:
                emit_hy_filters(c, CTX, w, tapsC)
            emit_hy_filters(c, SEQ, w, tapsL)
            if not last:
                emit_hy_conv(c, CTX, 0, w, hyc_d, tapsC, ident, mixbufs[hf], 0)
            emit_hy_conv(c, SEQ, CTX, w, hyc_d, tapsL, ident, mixbufs[hf], 0)
            bg_issue(5)
            emit_retention(c, sc["qT"], sc["kT"], sc["k"], sc["v"], sc["sg"], w["rdec"], w, ident, mixbufs[hf], 256)
            bg_issue(5)
            emit_s5(c, sc["uT"], w, mixbufs[hf], 768)
        bg_issue(100 if l == 1 else 4)
        tb = {"modsT_sb": modsT, "mixbufs": mixbufs, "hsrc": src, **w16[l]}
        for k in ("wup", "wdn", "wout", "g2T", "gluw", "glubT", "cwT", "cbT"):
            tb[k] = t[f"{k}_{l}"]
        if last:
            tb["nfT"] = t["nfT_1"]
        blocks = []
        if not last:
            blocks.append(dict(col0=0, W=CTX, oo=0, O=CTX, r=1, grid=False, mask=None, dst=(h1T, 0)))
            for blk in range(8):
                mk = (t["maskL0"], 0) if blk == 0 else ((t["maskL0"], WIN) if blk == 7 else None)
                blocks.append(dict(col0=CTX - 64 + blk * OWN, W=WIN, oo=64, O=OWN, r=0, grid=True, mask=mk, dst=(h1T, CTX + blk * OWN)))
        else:
            for blk in range(4):
                blocks.append(dict(col0=CTX - 64 + blk * OWN, W=WIN, oo=64, O=OWN, r=0, grid=True,
                                   mask=(t["mask"], blk * OWN), dst=(out_d, blk * OWN), dyn=(CTX - 64 + blk * OWN + 2048, sel)))
        emit_B(c, tb, last, blocks)
    c.pop()
    return c


def kernel(**inp):
    inp = {k: np.asarray(v) for k, v in inp.items()}
    x, cc, ctx, c_ctx = inp["x"], inp["c"], inp["ctx"], inp["c_ctx"]
    NB = x.shape[0]
    cores = list(range(NCORES))
    consts = A_const_inputs()
    maskL0 = np.ones([128, 2 * WIN], np.float32)
    maskL0[:, 0:64] = 0.0
    maskL0[:, WIN + 576:WIN + 640] = 0.0
    shared = {**consts, "maskL0": maskL0}
    for l in range(2):
        shared.update(layer_inputs(inp, l))
    maps = []
    for i in cores:
        b, hf = i // 2, i % 2
        hT = np.zeros([D, TP], np.float32)
        hT[:, 0:CTX] = ctx[b].T
        hT[:, CTX:T] = x[b].T
        mask = np.ones([128, NWINTOK], np.float32)
        if hf == 0:
            mask[:, 0:64] = 0.0
        else:
            mask[:, NWINTOK - 64:] = 0.0
        sel = np.zeros([128, 2], np.float32)
        sel[:, hf] = 1.0
        m = {**shared, "hT": hT, "cT": np.ascontiguousarray(np.stack([cc[b], c_ctx], 1)), "mask": mask, "sel": sel}
        maps.append(m)
    cF = build_fused({n: (a.shape, a.dtype == ml_dtypes.bfloat16) for n, a in maps[0].items()})
    res = run_bass_kernel_spmd(cF.nc, maps, core_ids=cores).results
    out = np.stack([np.concatenate([res[2 * b]["out"], res[2 * b + 1]["out"]], 1).T for b in range(NB)], 0)
    return np.ascontiguousarray(out.astype(np.float32))
```

```python
import contextlib
import math
import numpy as np
import ml_dtypes
import concourse.bass as bass
import concourse.mybir as mybir
from concourse.bass_utils import run_bass_kernel_spmd

F32 = mybir.dt.float32
BF16 = mybir.dt.bfloat16
I32 = mybir.dt.int32
AF = mybir.ActivationFunctionType
ALU = mybir.AluOpType
AX = mybir.AxisListType

D = 2048
KC = 16
SEQ = 4096
CTX = 256
T = SEQ + CTX
EPS = 1e-6
DFF = 5632
NCORES = 8
import os
NO_SELF_WAIT = os.environ.get('NO_SELF_WAIT', '0') == '1'


class Buf:
    def __init__(self, name, t):
        self.name = name
        self.t = t
        self.w = None
        self.r = {}

    def __getitem__(self, idx):
        return self.t[idx]

    def view(self):
        return Buf(self.name + "_v", self.t)


class Ctx:
    NDMA = 8

    def __init__(self):
        self.nc = bass.Bass("TRN2", target_bir_lowering=False, num_devices=NCORES)
        nc = self.nc
        self.eng = {"pe": nc.tensor, "dve": nc.vector, "act": nc.scalar, "pool": nc.gpsimd, "sp": nc.sync}
        self.sem = {}
        self.cnt = {}
        for e in self.eng:
            self.sem[e] = nc.alloc_semaphore("s_" + e)
            self.cnt[e] = 0
        self.dsem, self.dcnt, self.dptr = {}, {}, {}
        for q in ("sp", "act", "pool"):
            self.dsem[q] = [nc.alloc_semaphore(f"d_{q}{i}") for i in range(self.NDMA)]
            self.dcnt[q] = [0] * self.NDMA
            self.dptr[q] = 0
        self.seen = {e: {} for e in self.eng}
        self.bg = set()
        self.nbuf = 0
        self.stack = contextlib.ExitStack()
        self.scopes = []
        self._es = contextlib.ExitStack()
        self._es.enter_context(nc.allow_low_precision("bf16 matmul operands, fp32 accumulation"))
        self._es.enter_context(nc.allow_non_contiguous_dma("layout transforms"))

    def push(self):
        self.scopes.append(contextlib.ExitStack())

    def pop(self):
        self.barrier()
        self.scopes.pop().close()

    def sbuf(self, shape, dtype=F32, name=None):
        self.nbuf += 1
        name = name or f"sb{self.nbuf}"
        t = self.scopes[-1].enter_context(self.nc.sbuf_tensor(name, list(shape), dtype))
        return Buf(name, t)

    def psum(self, shape, dtype=F32, name=None):
        self.nbuf += 1
        name = name or f"ps{self.nbuf}"
        t = self.scopes[-1].enter_context(self.nc.psum_tensor(name, list(shape), dtype))
        return Buf(name, t)

    def dram(self, name, shape, dtype=F32, kind="Internal"):
        return Buf(name, self.nc.dram_tensor(name, list(shape), dtype, kind=kind))

    def _semh(self, key):
        if isinstance(key, tuple):
            return self.dsem[key[0]][key[1]]
        return self.sem[key]

    def _wait(self, e, k, v):
        if self.seen[e].get(k, -1) >= v:
            return
        self.eng[e].wait_ge(self._semh(k), v)
        self.seen[e][k] = v

    def _deps(self, e, reads, writes):
        need = {}

        def add(k, v):
            if need.get(k, -1) < v:
                need[k] = v
        for b in reads:
            if b.w is not None:
                add(*b.w)
        for b in writes:
            if b.w is not None:
                add(*b.w)
            for k, v in b.r.items():
                add(k, v)
        for k, v in need.items():
            if k == "pe" and e == "pe":
                continue
            if k == e and NO_SELF_WAIT:
                continue
            self._wait(e, k, v)

    def _mark(self, key, val, reads, writes):
        for b in reads:
            if b.r.get(key, -1) < val:
                b.r[key] = val
        for b in writes:
            b.w = (key, val)
            b.r = {}

    def op(self, e, fn, reads=(), writes=()):
        self._deps(e, reads, writes)
        ins = fn(self.eng[e])
        self.cnt[e] += 1
        ins.then_inc(self.sem[e], 1)
        self._mark(e, self.cnt[e], reads, writes)
        return ins

    def dma(self, q, out, in_, reads=(), writes=(), own_sem=False):
        if own_sem:
            self.dsem[q].append(self.nc.alloc_semaphore(f"d_{q}{len(self.dsem[q])}"))
            self.dcnt[q].append(0)
            i = len(self.dsem[q]) - 1
        else:
            i = self.dptr[q]
            self.dptr[q] = (i + 1) % self.NDMA
        key = (q, i)
        if self.dcnt[q][i] > 0:
            self._wait(q, key, self.dcnt[q][i])
        self._deps(q, reads, writes)
        ins = self.eng[q].dma_start(out=out, in_=in_)
        self.dcnt[q][i] += 16
        ins.then_inc(self.dsem[q][i], 16)
        self._mark(key, self.dcnt[q][i], reads, writes)
        if own_sem:
            self.bg.add(key)
        return ins

    def barrier(self):
        for e in self.eng:
            for k in self.eng:
                if k != e and self.cnt[k] > 0:
                    self._wait(e, k, self.cnt[k])
            for q in self.dsem:
                for i in range(len(self.dsem[q])):
                    if self.dcnt[q][i] > 0 and not (q, i) in self.bg:
                        self._wait(e, (q, i), self.dcnt[q][i])

    def finish(self):
        self.barrier()


_rr = {"i": 0}


def rr(lst):
    _rr["i"] += 1
    return lst[_rr["i"] % len(lst)]


class Rot:
    def __init__(self, c, n, shape, dtype=F32, psum=False):
        self.b = [(c.psum if psum else c.sbuf)(shape, dtype) for _ in range(n)]
        self.i = 0

    def get(self):
        self.i = (self.i + 1) % len(self.b)
        return self.b[self.i]


NF = 1536
NT = 1280
NW = NF + NT


def emit_mods(c, cT_d, adaw_d, adabT_d, modsT, R=2, ntile=96):
    c.push()
    cs = c.sbuf([128, KC, R])
    c.dma("sp", cs[:], cT_d.t.ap().rearrange("(kc p) r -> p kc r", p=128), reads=[cT_d], writes=[cs])
    sc = c.sbuf([128, KC, R])
    c.op("act", lambda e: e.activation(out=sc[:], in_=cs[:], func=AF.Silu), reads=[cs], writes=[sc])
    bT = c.sbuf([128, ntile])
    c.dma("sp", bT[:], adabT_d.t.ap(), reads=[adabT_d], writes=[bT])
    wrot = Rot(c, 2, [128, KC, 512])
    prot = Rot(c, 2, [128, 512], psum=True)
    wv = adaw_d.t.ap().rearrange("(kc p) n -> p kc n", p=128)
    for nb in range(ntile // 4):
        w = wrot.get()
        for h in range(2):
            c.dma(("sp", "pool")[h], w[:, h * 8:(h + 1) * 8, :], wv[:, h * 8:(h + 1) * 8, nb * 512:(nb + 1) * 512],
                  reads=[adaw_d], writes=[w])
        ps = prot.get()
        for jj in range(4):
            for kc in range(KC):
                c.op("pe", lambda e: e.matmul(out=ps[:, jj * R:(jj + 1) * R], lhsT=w[:, kc, jj * 128:(jj + 1) * 128],
                                              rhs=sc[:, kc, :], start=(kc == 0), stop=(kc == KC - 1)),
                     reads=[w, sc], writes=[ps])
        for jj in range(4):
            j = nb * 4 + jj
            c.op("dve", lambda e: e.tensor_scalar(out=modsT[:, j, :], in0=ps[:, jj * R:(jj + 1) * R],
                                                  scalar1=bT[:, j:j + 1], scalar2=None, op0=ALU.add),
                 reads=[ps, bT], writes=[modsT])
    c.pop()


def emit_inproj(c, hT_d, win_d, g1T_d, modsT, outs):
    c.push()
    w_sb = c.sbuf([128, KC, NW], BF16)
    wv = win_d.t.ap().rearrange("(kc p) n -> p kc n", p=128)
    WG = [(0, 384), (384, 768), (768, 1024), (1024, 1280), (1280, 1536), (1536, 1792), (1792, 2304), (2304, 2816)]
    wgv = [w_sb.view() for _ in WG]
    for gi, (g0, g1) in enumerate(WG):
        c.dma("pool", w_sb[:, :, g0:g1], wv[:, :, g0:g1], reads=[win_d], writes=[wgv[gi]])

    def wview(col):
        for gi, (g0, g1) in enumerate(WG):
            if g0 <= col < g1:
                return wgv[gi]
    g1 = c.sbuf([128, KC])
    c.dma("sp", g1[:], g1T_d.t.ap(), reads=[g1T_d], writes=[g1])
    G = c.sbuf([128, 2, KC])
    S = c.sbuf([128, 2, KC])
    for r in range(2):
        c.op("dve", lambda e: e.scalar_tensor_tensor(out=G[:, r, :], in0=modsT[:, 16:32, r], scalar=1.0, in1=g1[:],
                                                     op0=ALU.add, op1=ALU.mult), reads=[modsT, g1], writes=[G])
        c.op("dve", lambda e: e.tensor_copy(out=S[:, r, :], in_=modsT[:, 0:16, r]), reads=[modsT], writes=[S])
    ones = c.sbuf([128, 128])
    c.op("dve", lambda e: e.memset(ones[:], 1.0), writes=[ones])
    hblk = c.sbuf([128, KC, 512])
    hviews = [hblk.view() for _ in range(KC)]
    xrot = Rot(c, 2, [128, KC, 512], BF16)
    sqrot = Rot(c, 3, [128, 512])
    tmprot = Rot(c, 3, [128, 512])
    ssps = Rot(c, 1, [128, 512], psum=True)
    prot = Rot(c, 5, [128, 512], psum=True)
    ev32 = Rot(c, 3, [128, 512])
    ev16 = Rot(c, 3, [128, 512], BF16)
    sd = c.sbuf([128, 512])
    rstd = c.sbuf([128, 512])
    hv = hT_d.t.ap().rearrange("(kc p) t -> p kc t", p=128)
    blocks = [(0, CTX, 1)] + [(CTX + i * 512, 512, 0) for i in range(SEQ // 512)]
    evi = 0
    def do_norm(t0, nb, r):
        for kc in range(KC):
            c.dma(("sp", "pool")[kc % 2], hblk[:, kc, :nb], hv[:, kc, t0:t0 + nb], reads=[hT_d], writes=[hviews[kc]])
        ss = ssps.get()
        for kc in range(KC):
            sq = sqrot.get()
            c.op("act", lambda e: e.activation(out=sq[:, :nb], in_=hblk[:, kc, :nb], func=AF.Square),
                 reads=[hviews[kc]], writes=[sq])
            c.op("pe", lambda e: e.matmul(out=ss[:, :nb], lhsT=ones[:], rhs=sq[:, :nb], start=(kc == 0), stop=(kc == KC - 1)),
                 reads=[ones, sq], writes=[ss])
        c.op("act", lambda e: e.activation(out=sd[:, :nb], in_=ss[:, :nb], func=AF.Sqrt, scale=1.0 / D, bias=EPS),
             reads=[ss], writes=[sd])
        c.op("dve", lambda e: e.reciprocal(out=rstd[:, :nb], in_=sd[:, :nb]), reads=[sd], writes=[rstd])
        xn = xrot.get()
        for kc in range(KC):
            tmp = tmprot.get()
            c.op("dve", lambda e: e.tensor_tensor(out=tmp[:, :nb], in0=hblk[:, kc, :nb], in1=rstd[:, :nb], op=ALU.mult),
                 reads=[hviews[kc], rstd], writes=[tmp])
            c.op("act", lambda e: e.activation(out=xn[:, kc, :nb], in_=tmp[:, :nb], func=AF.Identity,
                                               scale=G[:, r, kc:kc + 1], bias=S[:, r, kc:kc + 1]),
                 reads=[tmp, G, S], writes=[xn])
        return xn

    def do_mm(t0, nb, r, xn):
        nonlocal evi
        fm = [("hyT", 0, 6, F32, 1.0), ("qT", 6, 2, BF16, 1.0), ("kT", 8, 2, BF16, 0.125), ("uT", 10, 2, F32, 1.0)]
        for (name, m0, nm, dt, scl) in fm:
            for mi in range(nm):
                col0 = (m0 + mi) * 128
                ps = prot.get()
                for kc in range(KC):
                    c.op("pe", lambda e: e.matmul(out=ps[:, :nb], lhsT=w_sb[:, kc, col0:col0 + 128], rhs=xn[:, kc, :nb],
                                                  start=(kc == 0), stop=(kc == KC - 1)),
                         reads=[wview(col0), xn], writes=[ps])
                ev = (ev32 if dt == F32 else ev16).get()
                evi += 1
                if evi % 2 == 0:
                    c.op("act", lambda e: e.activation(out=ev[:, :nb], in_=ps[:, :nb], func=AF.Copy, scale=scl),
                         reads=[ps], writes=[ev])
                else:
                    c.op("dve", lambda e: e.tensor_scalar(out=ev[:, :nb], in0=ps[:, :nb], scalar1=scl, scalar2=None,
                                                          op0=ALU.mult), reads=[ps], writes=[ev])
                o = outs[name]
                c.dma("sp", o[mi * 128:(mi + 1) * 128, t0:t0 + nb], ev[:, :nb], reads=[ev], writes=[o])
        tm = [("k", NF, 256, BF16, 0.125, None), ("v", NF + 256, 512, BF16, 1.0, None), ("sg", NF + 768, 512, F32, 1.0, AF.Silu)]
        for ti in range(nb // 128):
            for (name, col0, ncol, dt, scl, fn) in tm:
                ps = prot.get()
                for kc in range(KC):
                    c.op("pe", lambda e: e.matmul(out=ps[:, :ncol], lhsT=xn[:, kc, ti * 128:(ti + 1) * 128],
                                                  rhs=w_sb[:, kc, col0:col0 + ncol], start=(kc == 0), stop=(kc == KC - 1)),
                         reads=[wview(col0), xn], writes=[ps])
                ev = (ev32 if dt == F32 else ev16).get()
                evi += 1
                if fn is not None:
                    c.op("act", lambda e: e.activation(out=ev[:, :ncol], in_=ps[:, :ncol], func=fn), reads=[ps], writes=[ev])
                elif evi % 2 == 0:
                    c.op("act", lambda e: e.activation(out=ev[:, :ncol], in_=ps[:, :ncol], func=AF.Copy, scale=scl),
                         reads=[ps], writes=[ev])
                else:
                    c.op("dve", lambda e: e.tensor_scalar(out=ev[:, :ncol], in0=ps[:, :ncol], scalar1=scl, scalar2=None,
                                                          op0=ALU.mult), reads=[ps], writes=[ev])
                o = outs[name]
                c.dma("pool", o[t0 + ti * 128:t0 + (ti + 1) * 128, :], ev[:, :ncol], reads=[ev], writes=[o])

    xn_next = do_norm(*blocks[0])
    for bi, (t0, nb, r) in enumerate(blocks):
        xn_cur = xn_next
        if bi + 1 < len(blocks):
            xn_next = do_norm(*blocks[bi + 1])
        do_mm(t0, nb, r, xn_cur)
    c.pop()


NCH = T // 128


def ret_consts():
    m = np.arange(128)[:, None].astype(np.float64)
    cc = np.arange(128)[None, :].astype(np.float64)
    pf = np.where(cc >= m, cc - m, 1e9)
    pb = np.where(m > cc, m - cc, 1e9)
    posq = np.stack([np.broadcast_to(cc + 1, (128, 128)), np.broadcast_to(128 - cc, (128, 128))])
    posk = np.stack([127 - m[:, 0], m[:, 0]], 1)
    return {"ret_pf": pf.astype(np.float32), "ret_pb": pb.astype(np.float32),
            "ret_posq": np.ascontiguousarray(posq.transpose(1, 0, 2)).astype(np.float32),
            "ret_posk": posk.astype(np.float32)}


def emit_retention(c, qT_d, kT_d, k_d, v_d, sg_d, rdec_d, cst, ident, mixT_d, row0, extra=()):
    c.push()
    lg = c.sbuf([128, 8])
    c.dma("sp", lg[:], rdec_d.t.ap().partition_broadcast(128), reads=[rdec_d], writes=[lg])
    c.op("act", lambda e: e.activation(out=lg[:], in_=lg[:], func=AF.Exp), reads=[lg], writes=[lg])
    c.op("dve", lambda e: e.tensor_scalar(out=lg[:], in0=lg[:], scalar1=-1.0, scalar2=None, op0=ALU.mult), reads=[lg], writes=[lg])
    pf = c.sbuf([128, 128]); pb = c.sbuf([128, 128]); posq = c.sbuf([128, 2, 128]); posk = c.sbuf([128, 2])
    c.dma("sp", pf[:], cst["ret_pf"].t.ap(), reads=[cst["ret_pf"]], writes=[pf])
    c.dma("sp", pb[:], cst["ret_pb"].t.ap(), reads=[cst["ret_pb"]], writes=[pb])
    c.dma("sp", posq[:], cst["ret_posq"].t.ap(), reads=[cst["ret_posq"]], writes=[posq])
    c.dma("sp", posk[:], cst["ret_posk"].t.ap(), reads=[cst["ret_posk"]], writes=[posk])
    cd = c.sbuf([128, 8])
    c.op("act", lambda e: e.activation(out=cd[:], in_=lg[:], func=AF.Exp, scale=128.0), reads=[lg], writes=[cd])
    qT_h = c.sbuf([64, T], BF16); kT_h = c.sbuf([64, T], BF16)
    k_h = c.sbuf([128, NCH, 64], BF16); v_h = c.sbuf([128, NCH, 128], BF16); sg_h = c.sbuf([128, NCH, 128])
    qdf = c.sbuf([64, T], BF16); qdb = c.sbuf([64, T], BF16)
    kdf = c.sbuf([128, NCH, 64], BF16); kdb = c.sbuf([128, NCH, 64], BF16)
    Dm = c.sbuf([128, 128]); Dtmp = c.sbuf([128, 128]); Eq = c.sbuf([64, 2, 128]); ks = c.sbuf([128, 2])
    SB_all = c.sbuf([64, NCH, 128], BF16)
    Sb = c.sbuf([64, 128]); Sf = c.sbuf([64, 128]); Sf16 = c.sbuf([64, 128], BF16)
    ps_sc = Rot(c, 2, [128, 512], psum=True); ps_y = Rot(c, 2, [128, 512], psum=True)
    ps_kv = Rot(c, 2, [128, 512], psum=True); ps_tr = Rot(c, 2, [128, 512], psum=True)
    Prot = Rot(c, 2, [128, 128], BF16)
    orot = Rot(c, 3, [128, 128]); ssq = Rot(c, 2, [128, 1]); rs = Rot(c, 2, [128, 1]); junk = Rot(c, 2, [128, 128])
    oTst = Rot(c, 2, [128, 512])
    for hh in range(4):
        c.dma("sp", qT_h[:], qT_d[hh * 64:(hh + 1) * 64, :], reads=[qT_d], writes=[qT_h])
        c.dma("pool", kT_h[:], kT_d[hh * 64:(hh + 1) * 64, :], reads=[kT_d], writes=[kT_h])
        c.dma("sp", k_h[:], k_d.t.ap().rearrange("(n p) d -> p n d", p=128)[:, :, hh * 64:(hh + 1) * 64], reads=[k_d], writes=[k_h])
        c.dma("pool", v_h[:], v_d.t.ap().rearrange("(n p) d -> p n d", p=128)[:, :, hh * 128:(hh + 1) * 128], reads=[v_d], writes=[v_h])
        c.dma("sp", sg_h[:], sg_d.t.ap().rearrange("(n p) d -> p n d", p=128)[:, :, hh * 128:(hh + 1) * 128], reads=[sg_d], writes=[sg_h])
        lf = lg[:, hh:hh + 1]; lb = lg[:, 4 + hh:5 + hh]
        c.op("act", lambda e: e.activation(out=Dm[:], in_=pf[:], func=AF.Exp, scale=lf), reads=[pf, lg], writes=[Dm])
        c.op("act", lambda e: e.activation(out=Dtmp[:], in_=pb[:], func=AF.Exp, scale=lb), reads=[pb, lg], writes=[Dtmp])
        c.op("dve", lambda e: e.tensor_tensor(out=Dm[:], in0=Dm[:], in1=Dtmp[:], op=ALU.add), reads=[Dm, Dtmp], writes=[Dm])
        c.op("act", lambda e: e.activation(out=Eq[:, 0, :], in_=posq[0:64, 0, :], func=AF.Exp, scale=lg[0:64, hh:hh + 1]), reads=[posq, lg], writes=[Eq])
        c.op("act", lambda e: e.activation(out=Eq[:, 1, :], in_=posq[0:64, 1, :], func=AF.Exp, scale=lg[0:64, 4 + hh:5 + hh]), reads=[posq, lg], writes=[Eq])
        c.op("act", lambda e: e.activation(out=ks[:, 0:1], in_=posk[:, 0:1], func=AF.Exp, scale=lf), reads=[posk, lg], writes=[ks])
        c.op("act", lambda e: e.activation(out=ks[:, 1:2], in_=posk[:, 1:2], func=AF.Exp, scale=lb), reads=[posk, lg], writes=[ks])
        q3 = qT_h[:].rearrange("p (n c) -> p n c", c=128)
        c.op("dve", lambda e: e.tensor_tensor(out=qdf[:].rearrange("p (n c) -> p n c", c=128), in0=q3,
                                              in1=Eq[:, 0:1, :].broadcast_to([64, NCH, 128]), op=ALU.mult), reads=[qT_h, Eq], writes=[qdf])
        c.op("dve", lambda e: e.tensor_tensor(out=qdb[:].rearrange("p (n c) -> p n c", c=128), in0=q3,
                                              in1=Eq[:, 1:2, :].broadcast_to([64, NCH, 128]), op=ALU.mult), reads=[qT_h, Eq], writes=[qdb])
        c.op("dve", lambda e: e.tensor_scalar(out=kdf[:], in0=k_h[:], scalar1=ks[:, 0:1], scalar2=None, op0=ALU.mult), reads=[k_h, ks], writes=[kdf])
        c.op("dve", lambda e: e.tensor_scalar(out=kdb[:], in0=k_h[:], scalar1=ks[:, 1:2], scalar2=None, op0=ALU.mult), reads=[k_h, ks], writes=[kdb])
        c.op("dve", lambda e: e.memset(Sb[:], 0.0), writes=[Sb])
        for n in [1, 0] + list(range(NCH - 1, 1, -1)):
            c.op("act", lambda e: e.copy(out=SB_all[:, n, :], in_=Sb[:]), reads=[Sb], writes=[SB_all])
            kv = ps_kv.get()
            c.op("pe", lambda e: e.matmul(out=kv[0:64, 0:128], lhsT=kdb[:, n, :], rhs=v_h[:, n, :], start=True, stop=True),
                 reads=[kdb, v_h], writes=[kv])
            c.op("dve", lambda e: e.scalar_tensor_tensor(out=Sb[:], in0=Sb[:], scalar=cd[0:64, 4 + hh:5 + hh], in1=kv[0:64, 0:128],
                                                         op0=ALU.mult, op1=ALU.add), reads=[Sb, cd, kv], writes=[Sb])
        extra = list(extra)
        for _ in range(2 if hh < 2 else 1):
            if extra:
                extra.pop(0)()
        c.op("dve", lambda e: e.memset(Sf[:], 0.0), writes=[Sf])
        c.op("dve", lambda e: e.memset(Sf16[:], 0.0), writes=[Sf16])
        ost = None

        def emit_sc(n):
            ts_ = slice(n * 128, (n + 1) * 128)
            sc_ = ps_sc.get()
            c.op("pe", lambda e: e.matmul(out=sc_[:, 0:128], lhsT=kT_h[:, ts_], rhs=qT_h[:, ts_], start=True, stop=True),
                 reads=[kT_h, qT_h], writes=[sc_])
            return sc_

        def emit_tr(n, o):
            nonlocal ost
            tr = ps_tr.get()
            c.op("pe", lambda e: e.transpose(out=tr[:, 0:128], in_=o[:], identity=ident[:]), reads=[o, ident], writes=[tr])
            if n % 4 == 0:
                ost = oTst.get()
            j = n % 4
            c.op("act", lambda e: e.copy(out=ost[:, j * 128:(j + 1) * 128], in_=tr[:, 0:128]), reads=[tr], writes=[ost])
            if j == 3 or n == NCH - 1:
                n0 = n - j
                c.dma("sp", mixT_d[row0 + hh * 128:row0 + (hh + 1) * 128, n0 * 128:(n + 1) * 128], ost[:, 0:(j + 1) * 128],
                      reads=[ost], writes=[mixT_d])
        sc_next = emit_sc(0)
        o_prev = None
        for n in range(NCH):
            ts = slice(n * 128, (n + 1) * 128)
            sc = sc_next
            if n + 1 < NCH:
                sc_next = emit_sc(n + 1)
            P = Prot.get()
            c.op("dve", lambda e: e.tensor_tensor(out=P[:], in0=sc[:, 0:128], in1=Dm[:], op=ALU.mult), reads=[sc, Dm], writes=[P])
            y = ps_y.get()
            c.op("pe", lambda e: e.matmul(out=y[:, 0:128], lhsT=P[:], rhs=v_h[:, n, :], start=True, stop=False), reads=[P, v_h], writes=[y])
            c.op("pe", lambda e: e.matmul(out=y[:, 0:128], lhsT=qdf[:, ts], rhs=Sf16[:], start=False, stop=False), reads=[qdf, Sf16], writes=[y])
            c.op("pe", lambda e: e.matmul(out=y[:, 0:128], lhsT=qdb[:, ts], rhs=SB_all[:, n, :], start=False, stop=True), reads=[qdb, SB_all], writes=[y])
            kv = ps_kv.get()
            c.op("pe", lambda e: e.matmul(out=kv[0:64, 0:128], lhsT=kdf[:, n, :], rhs=v_h[:, n, :], start=True, stop=True),
                 reads=[kdf, v_h], writes=[kv])
            if o_prev is not None:
                emit_tr(n - 1, o_prev)
            c.op("dve", lambda e: e.scalar_tensor_tensor(out=Sf[:], in0=Sf[:], scalar=cd[0:64, hh:hh + 1], in1=kv[0:64, 0:128],
                                                         op0=ALU.mult, op1=ALU.add), reads=[Sf, cd, kv], writes=[Sf])
            c.op("act", lambda e: e.copy(out=Sf16[:], in_=Sf[:]), reads=[Sf], writes=[Sf16])
            sq = ssq.get(); jk = junk.get(); r_ = rs.get()
            c.op("act", lambda e: e.activation(out=jk[:], in_=y[:, 0:128], func=AF.Square, accum_out=sq[:]), reads=[y], writes=[jk, sq])
            c.op("act", lambda e: e.activation(out=sq[:], in_=sq[:], func=AF.Sqrt, scale=1.0 / 128, bias=EPS), reads=[sq], writes=[sq])
            c.op("dve", lambda e: e.reciprocal(out=r_[:], in_=sq[:]), reads=[sq], writes=[r_])
            o = orot.get()
            c.op("dve", lambda e: e.scalar_tensor_tensor(out=o[:], in0=y[:, 0:128], scalar=r_[:], in1=sg_h[:, n, :],
                                                         op0=ALU.mult, op1=ALU.mult), reads=[y, r_, sg_h], writes=[o])
            o_prev = o
        emit_tr(NCH - 1, o_prev)
    c.pop()


TWO_PI = 2.0 * math.pi
MAGIC = 12582912.0


def emit_sin(c, out_ap, arg_ap, shape, tmps, reads, writes):
    t, k = tmps
    sl = tuple(slice(0, s) for s in shape)
    c.op("dve", lambda e: e.tensor_scalar(out=t[sl], in0=arg_ap, scalar1=1.0 / TWO_PI, scalar2=MAGIC, op0=ALU.mult, op1=ALU.add),
         reads=reads, writes=[t])
    c.op("dve", lambda e: e.tensor_scalar(out=k[sl], in0=t[sl], scalar1=MAGIC, scalar2=-TWO_PI, op0=ALU.subtract, op1=ALU.mult),
         reads=[t], writes=[k])
    c.op("dve", lambda e: e.tensor_tensor(out=k[sl], in0=k[sl], in1=arg_ap, op=ALU.add), reads=[k] + list(reads), writes=[k])
    c.op("dve", lambda e: e.tensor_scalar(out=k[sl], in0=k[sl], scalar1=math.pi, scalar2=-math.pi, op0=ALU.min, op1=ALU.max),
         reads=[k], writes=[k])
    c.op("act", lambda e: e.activation(out=out_ap, in_=k[sl], func=AF.Sin), reads=[k], writes=writes)


def s5_consts():
    j = np.arange(128, dtype=np.float32)
    pos = np.stack([np.broadcast_to(j + 1, (128, 128)), np.broadcast_to(128 - j, (128, 128))], 1)
    return {"s5_pos": np.ascontiguousarray(pos).astype(np.float32)}


def s5_host(inp, l, hf):
    G0 = 16 * hf
    Bb = np.zeros([128, 2, 8, 2, 128], np.float32)
    Cb = np.zeros([128, 2, 8, 2, 128], np.float32)
    lam = np.zeros([128, 2, 16], np.float32)
    lst = np.zeros([128, 16], np.float32)
    for d in range(2):
        for pt in range(8):
            for g2 in range(2):
                gl = 2 * pt + g2
                G = G0 + gl
                r0 = (gl % 8) * 16
                Bb[r0:r0 + 16, d, pt, 0, g2 * 64:(g2 + 1) * 64] = inp["s5_b_re"][l, d, G].T
                Bb[r0:r0 + 16, d, pt, 1, g2 * 64:(g2 + 1) * 64] = inp["s5_b_im"][l, d, G].T
                Cb[g2 * 64:(g2 + 1) * 64, d, pt, 0, r0:r0 + 16] = inp["s5_c_re"][l, d, G].T
                Cb[g2 * 64:(g2 + 1) * 64, d, pt, 1, r0:r0 + 16] = inp["s5_c_im"][l, d, G].T
                lam[g2 * 64:(g2 + 1) * 64, 0, d * 8 + pt] = inp["s5_lam_re"][l, d, G]
                lam[g2 * 64:(g2 + 1) * 64, 1, d * 8 + pt] = inp["s5_lam_im"][l, d, G]
                lst[g2 * 64:(g2 + 1) * 64, d * 8 + pt] = inp["s5_log_step"][l, d, G]
    dT = np.ascontiguousarray(inp["s5_d"][l][256 * hf:256 * hf + 256].reshape(2, 128).T)
    return {"s5_B": Bb, "s5_C": Cb, "s5_lam": lam, "s5_lst": lst, "s5_dT": dT}


def emit_s5(c, uT_d, w, mixT_d, row0):
    c.push()
    Bb = c.sbuf([128, 2, 8, 2, 128]); Cb = c.sbuf([128, 2, 8, 2, 128])
    c.dma("sp", Bb[:], w["s5_B"].t.ap(), reads=[w["s5_B"]], writes=[Bb])
    c.dma("pool", Cb[:], w["s5_C"].t.ap(), reads=[w["s5_C"]], writes=[Cb])
    c.op("dve", lambda e: e.tensor_scalar(out=Cb[:, :, :, 1, :], in0=Cb[:, :, :, 1, :], scalar1=-1.0, scalar2=None, op0=ALU.mult),
         reads=[Cb], writes=[Cb])
    lam = c.sbuf([128, 2, 16]); lst = c.sbuf([128, 16]); dT = c.sbuf([128, 2]); pos = c.sbuf([128, 2, 128])
    c.dma("sp", lam[:], w["s5_lam"].t.ap(), reads=[w["s5_lam"]], writes=[lam])
    c.dma("sp", lst[:], w["s5_lst"].t.ap(), reads=[w["s5_lst"]], writes=[lst])
    c.dma("sp", dT[:], w["s5_dT"].t.ap(), reads=[w["s5_dT"]], writes=[dT])
    c.dma("sp", pos[:], w["s5_pos"].t.ap(), reads=[w["s5_pos"]], writes=[pos])
    P = {n: c.sbuf([128, 16]) for n in ["st", "lr", "ex", "rho", "th", "th2", "sn", "cs", "nr", "ni", "den", "cr", "ci", "t1", "t2"]}
    tm16 = (c.sbuf([128, 16]), c.sbuf([128, 16]))

    def tt(o, a, b, op):
        c.op("dve", lambda e: e.tensor_tensor(out=P[o][:], in0=P[a][:] if isinstance(a, str) else a, in1=P[b][:] if isinstance(b, str) else b, op=op),
             reads=[P[a] if isinstance(a, str) else lam, P[b] if isinstance(b, str) else lam], writes=[P[o]])
    c.op("act", lambda e: e.activation(out=P["st"][:], in_=lst[:], func=AF.Exp), reads=[lst], writes=[P["st"]])
    c.op("dve", lambda e: e.tensor_scalar(out=P["lr"][:], in0=lam[:, 0, :], scalar1=-1e-4, scalar2=None, op0=ALU.min), reads=[lam], writes=[P["lr"]])
    tt("ex", "lr", "st", ALU.mult)
    c.op("act", lambda e: e.activation(out=P["rho"][:], in_=P["ex"][:], func=AF.Exp), reads=[P["ex"]], writes=[P["rho"]])
    tt("th", lam[:, 1, :], "st", ALU.mult)
    c.op("dve", lambda e: e.tensor_scalar(out=P["th2"][:], in0=P["th"][:], scalar1=math.pi / 2, scalar2=None, op0=ALU.add), reads=[P["th"]], writes=[P["th2"]])
    emit_sin(c, P["sn"][:], P["th"][:], [128, 16], tm16, [P["th"]], [P["sn"]])
    emit_sin(c, P["cs"][:], P["th2"][:], [128, 16], tm16, [P["th2"]], [P["cs"]])
    tt("nr", "rho", "cs", ALU.mult)
    c.op("dve", lambda e: e.tensor_scalar(out=P["nr"][:], in0=P["nr"][:], scalar1=-1.0, scalar2=None, op0=ALU.add), reads=[P["nr"]], writes=[P["nr"]])
    tt("ni", "rho", "sn", ALU.mult)
    tt("t1", "lr", "lr", ALU.mult)
    tt("t2", lam[:, 1, :], lam[:, 1, :], ALU.mult)
    tt("den", "t1", "t2", ALU.add)
    c.op("dve", lambda e: e.reciprocal(out=P["den"][:], in_=P["den"][:]), reads=[P["den"]], writes=[P["den"]])
    tt("t1", "nr", "lr", ALU.mult); tt("t2", "ni", lam[:, 1, :], ALU.mult); tt("cr", "t1", "t2", ALU.add); tt("cr", "cr", "den", ALU.mult)
    tt("t1", "ni", "lr", ALU.mult); tt("t2", "nr", lam[:, 1, :], ALU.mult); tt("ci", "t1", "t2", ALU.subtract); tt("ci", "ci", "den", ALU.mult)
    Tab = c.sbuf([128, 16, 4, 128])
    arg = c.sbuf([128, 128]); arg2 = c.sbuf([128, 128]); tm = (c.sbuf([128, 128]), c.sbuf([128, 128])); ta = c.sbuf([128, 128]); tb = c.sbuf([128, 128])
    for q in range(16):
        d = q // 8
        c.op("dve", lambda e: e.tensor_scalar(out=arg[:], in0=pos[:, d, :], scalar1=P["th"][:, q:q + 1], scalar2=None, op0=ALU.mult), reads=[pos, P["th"]], writes=[arg])
        c.op("dve", lambda e: e.tensor_scalar(out=arg2[:], in0=arg[:], scalar1=math.pi / 2, scalar2=None, op0=ALU.add), reads=[arg], writes=[arg2])
        emit_sin(c, Tab[:, q, 1, :], arg[:], [128, 128], tm, [arg], [Tab])
        emit_sin(c, Tab[:, q, 0, :], arg2[:], [128, 128], tm, [arg2], [Tab])
        c.op("dve", lambda e: e.tensor_scalar(out=ta[:], in0=Tab[:, q, 1, :], scalar1=P["ci"][:, q:q + 1], scalar2=None, op0=ALU.mult), reads=[Tab, P["ci"]], writes=[ta])
        c.op("dve", lambda e: e.scalar_tensor_tensor(out=Tab[:, q, 2, :], in0=Tab[:, q, 0, :], scalar=P["cr"][:, q:q + 1], in1=ta[:], op0=ALU.mult, op1=ALU.add), reads=[Tab, P["cr"], ta], writes=[Tab])
        c.op("dve", lambda e: e.tensor_scalar(out=tb[:], in0=Tab[:, q, 1, :], scalar1=P["cr"][:, q:q + 1], scalar2=None, op0=ALU.mult), reads=[Tab, P["cr"]], writes=[tb])
        c.op("dve", lambda e: e.scalar_tensor_tensor(out=Tab[:, q, 3, :], in0=Tab[:, q, 0, :], scalar=P["ci"][:, q:q + 1], in1=tb[:], op0=ALU.mult, op1=ALU.subtract), reads=[Tab, P["ci"], tb], writes=[Tab])
    CL = c.sbuf([128, 16, 3])
    for q in range(16):
        Lq = 127 if q < 8 else 0
        c.op("dve", lambda e: e.tensor_copy(out=CL[:, q, 0:2], in_=Tab[:, q, 0:2, Lq]), reads=[Tab], writes=[CL])
    c.op("dve", lambda e: e.tensor_scalar(out=CL[:, :, 2], in0=CL[:, :, 1], scalar1=-1.0, scalar2=None, op0=ALU.mult), reads=[CL], writes=[CL])
    u_sb = c.sbuf([128, 2, T])
    c.dma("sp", u_sb[:, 0, :], uT_d[0:128, :], reads=[uT_d], writes=[u_sb])
    c.dma("pool", u_sb[:, 1, :], uT_d[128:256, :], reads=[uT_d], writes=[u_sb])
    yb = c.sbuf([128, 2, T])
    xrot = Rot(c, 2, [128, 8, 2, 128])
    ps_b = Rot(c, 3, [128, 4, 2, 128], psum=True)
    ps_y = Rot(c, 2, [128, 512], psum=True)
    r1 = Rot(c, 3, [128, 4, 128]); r2 = Rot(c, 3, [128, 4, 128]); rin = Rot(c, 2, [128, 4, 2, 128]); wv = Rot(c, 3, [128, 4, 2, 128])
    p1 = Rot(c, 3, [128, 4, 128]); p2 = Rot(c, 3, [128, 4, 128])
    zst = Rot(c, 2, [128, 512]); ytmp = Rot(c, 2, [128, 128])
    carr = Rot(c, 2, [128, 8, 2]); ctm = Rot(c, 4, [128, 4])
    def emit_bu(d, n, g4):
        ts = slice(n * 128, (n + 1) * 128)
        pb = ps_b.get()
        for p4 in range(4):
            pt = g4 * 4 + p4
            kc = pt // 4
            c.op("pe", lambda e: e.matmul(out=pb[:, p4, 0, :], lhsT=Bb[:, d, pt, 0, :], rhs=u_sb[:, kc, ts], start=True, stop=True), reads=[Bb, u_sb], writes=[pb])
            c.op("pe", lambda e: e.matmul(out=pb[:, p4, 1, :], lhsT=Bb[:, d, pt, 1, :], rhs=u_sb[:, kc, ts], start=True, stop=True), reads=[Bb, u_sb], writes=[pb])
        return pb

    for d, order in ((1, [1, 0] + list(range(NCH - 1, 1, -1))), (0, list(range(NCH)))):
        xprev = None
        car = None
        zcur = [None, None]
        steps = [(n, g4) for n in order for g4 in range(2)]
        pbs = {0: emit_bu(d, steps[0][0], steps[0][1])}
        for si, (n, g4) in enumerate(steps):
            ts = slice(n * 128, (n + 1) * 128)
            if si + 1 < len(steps):
                pbs[si + 1] = emit_bu(d, steps[si + 1][0], steps[si + 1][1])
            pb = pbs.pop(si)
            if g4 == 0:
                x = xrot.get()
                carn = carr.get()
            if True:
                q0 = d * 8 + g4 * 4
                a1 = r1.get(); a2 = r2.get(); ri = rin.get(); w_ = wv.get()
                Tr = Tab[:, q0:q0 + 4, 2, :]; Ti = Tab[:, q0:q0 + 4, 3, :]
                Br = pb[:, :, 0, :]; Bi = pb[:, :, 1, :]
                c.op("dve", lambda e: e.tensor_tensor(out=a1[:], in0=Br, in1=Tr, op=ALU.mult), reads=[pb, Tab], writes=[a1])
                c.op("dve", lambda e: e.tensor_tensor(out=a2[:], in0=Bi, in1=Ti, op=ALU.mult), reads=[pb, Tab], writes=[a2])
                c.op("dve", lambda e: e.tensor_tensor(out=ri[:, :, 0, :], in0=a1[:], in1=a2[:], op=ALU.subtract), reads=[a1, a2], writes=[ri])
                a1 = r1.get(); a2 = r2.get()
                c.op("dve", lambda e: e.tensor_tensor(out=a1[:], in0=Bi, in1=Tr, op=ALU.mult), reads=[pb, Tab], writes=[a1])
                c.op("dve", lambda e: e.tensor_tensor(out=a2[:], in0=Br, in1=Ti, op=ALU.mult), reads=[pb, Tab], writes=[a2])
                c.op("dve", lambda e: e.tensor_tensor(out=ri[:, :, 1, :], in0=a1[:], in1=a2[:], op=ALU.add), reads=[a1, a2], writes=[ri])
                for p4 in range(4):
                    pt = g4 * 4 + p4
                    q = d * 8 + pt
                    rho_b = P["rho"][:, q:q + 1].broadcast_to([128, 128])
                    for ci_ in range(2):
                        if car is None:
                            init = 0.0
                            rd = [ri, P["rho"]]
                        else:
                            init = car[:, pt, ci_:ci_ + 1]
                            rd = [ri, P["rho"], car]
                        if d == 0:
                            c.op("dve", lambda e: e.tensor_tensor_scan(out=w_[:, p4, ci_, :], data0=rho_b, data1=ri[:, p4, ci_, :], initial=init, op0=ALU.mult, op1=ALU.add), reads=rd, writes=[w_])
                        else:
                            c.op("dve", lambda e: e.tensor_tensor_scan(out=w_[:, p4, ci_, ::-1], data0=rho_b, data1=ri[:, p4, ci_, ::-1], initial=init, op0=ALU.mult, op1=ALU.add), reads=rd, writes=[w_])
                Cj = Tab[:, q0:q0 + 4, 0, :]; Sj = Tab[:, q0:q0 + 4, 1, :]
                wr = w_[:, :, 0, :]; wi = w_[:, :, 1, :]
                xs = x[:, g4 * 4:g4 * 4 + 4, :, :]
                Lc = 127 if d == 0 else 0
                for p4 in range(4):
                    pt = g4 * 4 + p4
                    q = d * 8 + pt
                    wrL = w_[:, p4, 0, Lc:Lc + 1]; wiL = w_[:, p4, 1, Lc:Lc + 1]
                    k1 = ctm.get(); k2 = ctm.get()
                    c.op("act", lambda e: e.activation(out=k1[:, 0:1], in_=wiL, func=AF.Copy, scale=CL[:, q, 2:3]), reads=[w_, CL], writes=[k1])
                    c.op("act", lambda e: e.activation(out=carn[:, pt, 0:1], in_=wrL, func=AF.Identity, scale=CL[:, q, 0:1], bias=k1[:, 0:1]), reads=[w_, CL, k1], writes=[carn])
                    c.op("act", lambda e: e.activation(out=k2[:, 0:1], in_=wiL, func=AF.Copy, scale=CL[:, q, 0:1]), reads=[w_, CL], writes=[k2])
                    c.op("act", lambda e: e.activation(out=carn[:, pt, 1:2], in_=wrL, func=AF.Identity, scale=CL[:, q, 1:2], bias=k2[:, 0:1]), reads=[w_, CL, k2], writes=[carn])
                b1 = p1.get(); b2 = p2.get()
                c.op("pool", lambda e: e.tensor_tensor(out=b1[:], in0=wr, in1=Cj, op=ALU.mult), reads=[w_, Tab], writes=[b1])
                c.op("pool", lambda e: e.tensor_tensor(out=b2[:], in0=wi, in1=Sj, op=ALU.mult), reads=[w_, Tab], writes=[b2])
                c.op("pool", lambda e: e.tensor_tensor(out=xs[:, :, 0, :], in0=b1[:], in1=b2[:], op=ALU.subtract), reads=[b1, b2], writes=[x])
                b1 = p1.get(); b2 = p2.get()
                c.op("pool", lambda e: e.tensor_tensor(out=b1[:], in0=wr, in1=Sj, op=ALU.mult), reads=[w_, Tab], writes=[b1])
                c.op("pool", lambda e: e.tensor_tensor(out=b2[:], in0=wi, in1=Cj, op=ALU.mult), reads=[w_, Tab], writes=[b2])
                c.op("pool", lambda e: e.tensor_tensor(out=xs[:, :, 1, :], in0=b1[:], in1=b2[:], op=ALU.add), reads=[b1, b2], writes=[x])
            if g4 == 0:
                continue
            xprev = x
            car = carn
            for ot in range(2):
                py = ps_y.get()
                for i_, pt in enumerate(range(4 * ot, 4 * ot + 4)):
                    c.op("pe", lambda e: e.matmul(out=py[:, 0:128], lhsT=Cb[:, d, pt, 0, :], rhs=x[:, pt, 0, :], start=(i_ == 0), stop=False), reads=[Cb, x], writes=[py])
                    c.op("pe", lambda e: e.matmul(out=py[:, 0:128], lhsT=Cb[:, d, pt, 1, :], rhs=x[:, pt, 1, :], start=False, stop=(i_ == 3)), reads=[Cb, x], writes=[py])
                if d == 1:
                    c.op("act", lambda e: e.copy(out=yb[:, ot, ts], in_=py[:, 0:128]), reads=[py], writes=[yb])
                else:
                    yt = ytmp.get()
                    c.op("dve", lambda e: e.scalar_tensor_tensor(out=yt[:], in0=u_sb[:, ot, ts], scalar=dT[:, ot:ot + 1], in1=py[:, 0:128], op0=ALU.mult, op1=ALU.add), reads=[u_sb, dT, py], writes=[yt])
                    c.op("pool", lambda e: e.tensor_tensor(out=yt[:], in0=yt[:], in1=yb[:, ot, ts], op=ALU.add), reads=[yt, yb], writes=[yt])
                    j = n % 4
                    if j == 0 or zcur[ot] is None:
                        zcur[ot] = zst.get()
                    zz = zcur[ot]
                    c.op("act", lambda e: e.activation(out=zz[:, j * 128:(j + 1) * 128], in_=yt[:], func=AF.Gelu_apprx_tanh), reads=[yt], writes=[zz])
                    if j == 3 or n == NCH - 1:
                        n0 = n - j
                        c.dma("sp", mixT_d[row0 + ot * 128:row0 + (ot + 1) * 128, n0 * 128:(n + 1) * 128], zz[:, 0:(j + 1) * 128], reads=[zz], writes=[mixT_d])
    c.pop()


CG = 32


def hy_tables(L):
    N = 2 * L
    N1 = N // 128
    F1 = N1 // 2 + 1
    T1 = L // 128
    t1 = np.arange(N1, dtype=np.float64)[:, None]
    f1 = np.arange(F1, dtype=np.float64)[None, :]
    W1 = np.concatenate([np.cos(2 * np.pi * t1 * f1 / N1), -np.sin(2 * np.pi * t1 * f1 / N1)], 1)
    a = np.arange(128, dtype=np.float64)
    ang = 2 * np.pi * a[:, None, None] * (np.arange(F1)[None, :, None] + N1 * a[None, None, :]) / N
    TB = np.stack([np.cos(ang), np.sin(ang)], 2)
    angT = ang.transpose(2, 1, 0)
    TBT = np.stack([np.cos(angT), np.sin(angT)], 2)
    wf = np.full(F1, 2.0); wf[0] = 1.0; wf[F1 - 1] = 1.0
    tt = np.arange(T1, dtype=np.float64)[None, :]
    ff = np.arange(F1, dtype=np.float64)[:, None]
    Minv = np.concatenate([wf[:, None] / N * np.cos(2 * np.pi * ff * tt / N1), -wf[:, None] / N * np.sin(2 * np.pi * ff * tt / N1)], 0)
    idx = np.arange(L, dtype=np.float32)
    tpos = (idx / np.float32(max(L - 1, 1))).astype(np.float32)
    bands = np.linspace(1e-4, 15.0, 16, dtype=np.float32)
    angp = ((np.float32(2.0 * math.pi) * idx / np.float32(L))[:, None] * bands[None]).astype(np.float32)
    z = np.concatenate([tpos[:, None], np.cos(angp), -np.sin(angp)], -1).astype(np.float32)
    dmap = np.concatenate([np.arange(L), [0], np.arange(L - 1, 0, -1)])
    zT = np.ascontiguousarray(z[dmap].T)
    tn = tpos[dmap].astype(np.float32).copy()
    tn[L] = 1e9
    s = f"hy{L}_"
    out = {s + "W1": W1.astype(np.float32), s + "Minv": Minv.astype(np.float32), s + "zT": zT, s + "tn": tn}
    for nm, tab in (("TB", TB), ("TBT", TBT)):
        t32 = tab.astype(np.float32)
        hi = t32.astype(ml_dtypes.bfloat16)
        lo = (t32 - hi.astype(np.float32)).astype(ml_dtypes.bfloat16)
        out[s + nm + "h"] = hi
        out[s + nm + "l"] = lo
    return out


def hy_host(inp, l, hf):
    cols = np.concatenate([o + 256 * hf + np.arange(256) for o in (0, 512, 1024)])
    cw = inp["hy_conv_w"][l][:, cols]
    cwT = np.ascontiguousarray(cw.reshape(3, 6, 128).transpose(2, 1, 0))
    cbT = np.ascontiguousarray(inp["hy_conv_b"][l][cols].reshape(6, 128).T)
    w3 = inp["hy_w3"][l].reshape(64, 2, 2, 512)[:, :, :, 256 * hf:256 * hf + 256]
    deltas = np.abs(np.linspace(math.log(1e-2) / 1.5, math.log(1e-2) / 0.3, 512, dtype=np.float32))
    nd = -deltas[256 * hf:256 * hf + 256]
    return {"hy_cwT": cwT, "hy_cbT": cbT, "hy_w1": inp["hy_w1"][l], "hy_b1": inp["hy_b1"][l].reshape(64, 1),
            "hy_w2": inp["hy_w2"][l], "hy_b2": inp["hy_b2"][l].reshape(64, 1), "hy_w3": np.ascontiguousarray(w3),
            "hy_freqT": np.ascontiguousarray(inp["hy_freq"][l].T), "hy_ndT": np.ascontiguousarray(nd.reshape(2, 128).T),
            "hy_bias": np.ascontiguousarray(inp["hy_bias"][l][:, 256 * hf:256 * hf + 256])}


def hy_shortconv_thunks(c, hyT_d, w, hyc_d):
    cw = c.sbuf([128, 6, 3]); cb = c.sbuf([128, 6])
    c.dma("sp", cw[:], w["hy_cwT"].t.ap(), reads=[w["hy_cwT"]], writes=[cw])
    c.dma("sp", cb[:], w["hy_cbT"].t.ap(), reads=[w["hy_cbT"]], writes=[cb])
    xr = Rot(c, 2, [128, T]); pr = Rot(c, 2, [128, T])

    def mk(rt):
        def run():
            x = xr.get(); p = pr.get()
            c.dma(("sp", "pool")[rt % 2], x[:], hyT_d[rt * 128:(rt + 1) * 128, :], reads=[hyT_d], writes=[x])
            for (a, b) in ((0, CTX), (CTX, T)):
                c.op("dve", lambda e: e.tensor_scalar(out=p[:, a:b], in0=x[:, a:b], scalar1=cw[:, rt, 1:2], scalar2=cb[:, rt:rt + 1], op0=ALU.mult, op1=ALU.add),
                     reads=[x, cw, cb], writes=[p])
                c.op("dve", lambda e: e.scalar_tensor_tensor(out=p[:, a + 1:b], in0=x[:, a:b - 1], scalar=cw[:, rt, 0:1], in1=p[:, a + 1:b], op0=ALU.mult, op1=ALU.add),
                     reads=[x, cw, p], writes=[p])
                c.op("dve", lambda e: e.scalar_tensor_tensor(out=p[:, a:b - 1], in0=x[:, a + 1:b], scalar=cw[:, rt, 2:3], in1=p[:, a:b - 1], op0=ALU.mult, op1=ALU.add),
                     reads=[x, cw, p], writes=[p])
            c.dma(("sp", "pool")[rt % 2], hyc_d[rt * 128:(rt + 1) * 128, :], p[:], reads=[p], writes=[hyc_d])
        return run
    return [mk(rt) for rt in range(6)]


def emit_hy_shortconv(c, hyT_d, w, hyc_d):
    c.push()
    for th in hy_shortconv_thunks(c, hyT_d, w, hyc_d):
        th()
    c.pop()


def emit_hy_filters(c, L, w, taps_d):
    N = 2 * L
    CH = min(512, L)
    nch = N // CH
    s = f"hy{L}_"
    c.push()
    w1 = c.sbuf([33, 64]); w2 = c.sbuf([64, 64]); b1 = c.sbuf([64, 1]); b2 = c.sbuf([64, 1]); fq = c.sbuf([64, 2]); fb = c.sbuf([64, 2])
    w3 = c.sbuf([64, 2, 2, 256]); nd = c.sbuf([128, 2])
    for (dst, nm) in ((w1, "hy_w1"), (w2, "hy_w2"), (b1, "hy_b1"), (b2, "hy_b2"), (fq, "hy_freqT"), (w3, "hy_w3"), (nd, "hy_ndT")):
        c.dma("sp", dst[:], w[nm].t.ap(), reads=[w[nm]], writes=[dst])
    c.op("dve", lambda e: e.tensor_tensor(out=fb[:, 0:1], in0=fq[:, 0:1], in1=b1[:], op=ALU.mult), reads=[fq, b1], writes=[fb])
    c.op("dve", lambda e: e.tensor_tensor(out=fb[:, 1:2], in0=fq[:, 1:2], in1=b2[:], op=ALU.mult), reads=[fq, b2], writes=[fb])
    zT = c.sbuf([33, N]); h2T = c.sbuf([64, N]); tnb = c.sbuf([128, N])
    c.dma("sp", zT[:], w[s + "zT"].t.ap(), reads=[w[s + "zT"]], writes=[zT])
    c.dma("pool", tnb[:], w[s + "tn"].t.ap().partition_broadcast(128), reads=[w[s + "tn"]], writes=[tnb])
    ps = Rot(c, 3, [128, 512], psum=True)
    argr = Rot(c, 2, [64, CH]); h1r = Rot(c, 2, [64, CH]); tm = (c.sbuf([64, CH]), c.sbuf([64, CH]))
    for ch in range(nch):
        sl = slice(ch * CH, (ch + 1) * CH)
        p = ps.get()
        c.op("pe", lambda e: e.matmul(out=p[0:64, 0:CH], lhsT=w1[:], rhs=zT[:, sl], start=True, stop=True), reads=[w1, zT], writes=[p])
        ar = argr.get()
        c.op("act", lambda e: e.activation(out=ar[:], in_=p[0:64, 0:CH], func=AF.Identity, scale=fq[:, 0:1], bias=fb[:, 0:1]), reads=[p, fq, fb], writes=[ar])
        h1 = h1r.get()
        emit_sin(c, h1[:], ar[:], [64, CH], tm, [ar], [h1])
        p = ps.get()
        c.op("pe", lambda e: e.matmul(out=p[0:64, 0:CH], lhsT=w2[:], rhs=h1[:], start=True, stop=True), reads=[w2, h1], writes=[p])
        ar = argr.get()
        c.op("act", lambda e: e.activation(out=ar[:], in_=p[0:64, 0:CH], func=AF.Identity, scale=fq[:, 1:2], bias=fb[:, 1:2]), reads=[p, fq, fb], writes=[ar])
        emit_sin(c, h2T[:, sl], ar[:], [64, CH], tm, [ar], [h2T])
    tp = [c.sbuf([128, N]), c.sbuf([128, N])]
    acc = c.sbuf([128, 2, nch]); tot = c.sbuf([128, 2]); decr = Rot(c, 2, [128, CH])
    for ct in range(2):
        for ch in range(nch):
            sl = slice(ch * CH, (ch + 1) * CH)
            dr = 0 if ch * CH < L else 1
            dec = decr.get()
            c.op("act", lambda e: e.activation(out=dec[:], in_=tnb[:, sl], func=AF.Exp, scale=nd[:, ct:ct + 1]), reads=[tnb, nd], writes=[dec])
            for o in range(2):
                p = ps.get()
                c.op("pe", lambda e: e.matmul(out=p[:, 0:CH], lhsT=w3[:, dr, o, ct * 128:(ct + 1) * 128], rhs=h2T[:, sl], start=True, stop=True), reads=[w3, h2T], writes=[p])
                c.op("dve", lambda e: e.tensor_tensor(out=tp[o][:, sl], in0=p[:, 0:CH], in1=dec[:], op=ALU.mult), reads=[p, dec], writes=[tp[o]])
                c.op("dve", lambda e: e.tensor_reduce(out=acc[:, o, ch:ch + 1], in_=tp[o][:, sl], axis=AX.X, op=ALU.add, apply_absolute_value=True), reads=[tp[o]], writes=[acc])
        for o in range(2):
            c.op("dve", lambda e: e.tensor_reduce(out=tot[:, o:o + 1], in_=acc[:, o, :], axis=AX.X, op=ALU.add), reads=[acc], writes=[tot])
        c.op("dve", lambda e: e.reciprocal(out=tot[:], in_=tot[:]), reads=[tot], writes=[tot])
        for o in range(2):
            c.op(("dve", "pool")[o], lambda e: e.tensor_scalar(out=tp[o][:], in0=tp[o][:], scalar1=tot[:, o:o + 1], scalar2=1.0, op0=ALU.mult, op1=ALU.mult), reads=[tp[o], tot], writes=[tp[o]])
            c.dma(("sp", "pool")[o], taps_d[o, ct * 128:(ct + 1) * 128, :], tp[o][:], reads=[tp[o]], writes=[taps_d])
    c.pop()


def emit_hy_conv(c, L, tok0, w, hyc_d, taps_d, ident, mixT_d, row0):
    N = 2 * L
    N1 = N // 128
    F1 = N1 // 2 + 1
    T1 = L // 128
    F2 = 2 * F1
    NPB = 512 // (2 * CG)
    NPA = 512 // F2
    s = f"hy{L}_"
    c.push()
    W1 = c.sbuf([N1, F2]); Minv = c.sbuf([F2, T1])
    TBh = c.sbuf([128, F1, 2, 128], BF16); TBl = c.sbuf([128, F1, 2, 128], BF16)
    TBTh = c.sbuf([128, F1, 2, 128], BF16); TBTl = c.sbuf([128, F1, 2, 128], BF16)
    c.dma("sp", W1[:], w[s + "W1"].t.ap(), reads=[w[s + "W1"]], writes=[W1])
    for (dst_, nm_, q_) in ((TBh, "TBh", "sp"), (TBl, "TBl", "pool"), (TBTh, "TBTh", "sp"), (TBTl, "TBTl", "pool")):
        c.dma(q_, dst_[:], w[s + nm_].t.ap(), reads=[w[s + nm_]], writes=[dst_])
    c.dma("sp", Minv[:], w[s + "Minv"].t.ap(), reads=[w[s + "Minv"]], writes=[Minv])
    bias = c.sbuf([T1, 2, 256])
    c.dma("sp", bias[:], w["hy_bias"].t.ap().partition_broadcast(T1), reads=[w["hy_bias"]], writes=[bias])
    xp = Rot(c, 3, [N1, CG, 128])
    Ah = c.sbuf([128, F1, 3, CG], BF16); Al = c.sbuf([128, F1, 3, CG], BF16)
    Ksp = [c.sbuf([128, F1, 2, CG]), c.sbuf([128, F1, 2, CG])]
    Yh = c.sbuf([128, F1, 3, CG], BF16); Yl = c.sbuf([128, F1, 3, CG], BF16)
    y32 = Rot(c, 2, [128, 2, NPB, CG])
    G = c.sbuf([128, CG, F2])
    GT = c.sbuf([F2, CG, 128])
    tmp = c.sbuf([T1, 512]); tmq = Rot(c, 2, [128, NPB, CG]); tmr = Rot(c, 2, [128, NPB, CG])
    psA = Rot(c, 2, [128, 512], psum=True); psB = Rot(c, 3, [128, 512], psum=True); psT = Rot(c, 2, [128, 512], psum=True)
    evi = [0]

    def evac(out, in_, rd, wr, scale=None):
        evi[0] += 1
        if scale is not None:
            c.op("act", lambda e: e.activation(out=out, in_=in_, func=AF.Copy, scale=scale), reads=rd, writes=wr)
        elif evi[0] % 2 == 0:
            c.op("act", lambda e: e.copy(out=out, in_=in_), reads=rd, writes=wr)
        else:
            c.op("dve", lambda e: e.tensor_copy(out=out, in_=in_), reads=rd, writes=wr)

    def fwd(X, R, sink):
        for c0 in range(0, CG, NPA):
            n = min(NPA, CG - c0)
            pa = psA.get()
            for ci in range(n):
                c.op("pe", lambda e: e.matmul(out=pa[:, ci * F2:(ci + 1) * F2], lhsT=X[0:R, c0 + ci, :], rhs=W1[0:R, :], start=True, stop=True),
                     reads=[X, W1], writes=[pa])
            src = pa[:, 0:n * F2].rearrange("p (c r f) -> p c r f", c=n, r=2)
            hv = Ah[:, :, 0:2, c0:c0 + n].rearrange("p f r c -> p c r f")
            lv = Al[:, :, 0:2, c0:c0 + n].rearrange("p f r c -> p c r f")
            c.op("act", lambda e: e.copy(out=hv, in_=src), reads=[pa], writes=[Ah])
            c.op("dve", lambda e: e.tensor_tensor(out=lv, in0=src, in1=hv, op=ALU.subtract), reads=[pa, Ah], writes=[Al])
            c.op("act", lambda e: e.activation(out=Ah[:, :, 2, c0:c0 + n], in_=Ah[:, :, 0, c0:c0 + n], func=AF.Copy, scale=-1.0), reads=[Ah], writes=[Ah])
            c.op("pool", lambda e: e.tensor_scalar(out=Al[:, :, 2, c0:c0 + n], in0=Al[:, :, 0, c0:c0 + n], scalar1=-1.0, scalar2=1.0, op0=ALU.mult, op1=ALU.mult), reads=[Al], writes=[Al])
        for f0 in range(0, F1, NPB):
            nf = min(NPB, F1 - f0)
            pb = psB.get()
            for fi in range(nf):
                f = f0 + fi
                o_ = pb[:, fi * 2 * CG:(fi + 1) * 2 * CG]
                seq = [(TBh, 0, Ah, slice(0, 2)), (TBh, 0, Al, slice(0, 2)), (TBl, 0, Ah, slice(0, 2)),
                       (TBh, 1, Ah, slice(1, 3)), (TBh, 1, Al, slice(1, 3)), (TBl, 1, Ah, slice(1, 3))]
                for k_, (tb_, cs_, dat_, sl_) in enumerate(seq):
                    c.op("pe", lambda e: e.matmul(out=o_, lhsT=tb_[:, f, cs_, :], rhs=dat_[:, f, sl_, :].rearrange("p r c -> p (r c)"), start=(k_ == 0), stop=(k_ == 5)),
                         reads=[tb_, dat_], writes=[pb])
            sink(pb, f0, nf)

    def inverse(epi):
        for f0 in range(0, F1, NPB):
            nf = min(NPB, F1 - f0)
            pb = psB.get()
            for fi in range(nf):
                f = f0 + fi
                o_ = pb[:, fi * 2 * CG:(fi + 1) * 2 * CG]
                seq = [(TBTh, 0, Yh, slice(1, 3)), (TBTh, 0, Yl, slice(1, 3)), (TBTl, 0, Yh, slice(1, 3)),
                       (TBTh, 1, Yh, slice(0, 2)), (TBTh, 1, Yl, slice(0, 2)), (TBTl, 1, Yh, slice(0, 2))]
                for k_, (tb_, cs_, dat_, sl_) in enumerate(seq):
                    c.op("pe", lambda e: e.matmul(out=o_, lhsT=tb_[:, f, cs_, :], rhs=dat_[:, f, sl_, :].rearrange("p r c -> p (r c)"), start=(k_ == 0), stop=(k_ == 5)),
                         reads=[tb_, dat_], writes=[pb])
            src = pb[:, 0:nf * 2 * CG].rearrange("p (f r c) -> p f r c", f=nf, r=2)
            for r in range(2):
                evac(G[:, :, r * F1 + f0:r * F1 + f0 + nf].rearrange("p c f -> p f c"), src[:, :, r, :], [pb], [G])
        for c0 in range(0, CG, 4):
            pt_ = psT.get()
            for ci in range(4):
                c.op("pe", lambda e: e.transpose(out=pt_[0:F2, ci * 128:(ci + 1) * 128], in_=G[:, c0 + ci, :], identity=ident[:]), reads=[G, ident], writes=[pt_])
            evac(GT[:, c0:c0 + 4, :], pt_[0:F2, :].rearrange("p (c t) -> p c t", c=4), [pt_], [GT])
        for c0 in range(0, CG, 4):
            py = psA.get()
            c.op("pe", lambda e: e.matmul(out=py[0:T1, :], lhsT=Minv[:], rhs=GT[:, c0:c0 + 4, :], start=True, stop=True), reads=[Minv, GT], writes=[py])
            epi(py, c0)

    for g in range(256 // CG):
        ch0 = g * CG

        def load(dst, R, src_d, r0, t0_):
            c.dma(rr(["sp", "pool"]), dst[0:R, :, :], src_d[r0:r0 + CG, t0_:t0_ + R * 128].rearrange("c (a b) -> a c b", b=128), reads=[src_d], writes=[dst])
        for o in range(2):
            X = xp.get()
            c.dma(rr(["sp", "pool"]), X[:], taps_d[o, ch0:ch0 + CG, :].rearrange("c (a b) -> a c b", b=128), reads=[taps_d], writes=[X])

            def ksink(pb, f0, nf, o=o):
                evac(Ksp[o][:, f0:f0 + nf, :, :], pb[:, 0:nf * 2 * CG].rearrange("p (f r c) -> p f r c", f=nf, r=2), [pb], [Ksp[o]])
            fwd(X, N1, ksink)
        pv = xp.get()
        load(pv, T1, hyc_d, ch0, tok0)
        px1 = xp.get()
        load(px1, T1, hyc_d, 256 + ch0, tok0)

        def make_psink(o):
            def psink(pb, f0, nf):
                src = pb[:, 0:nf * 2 * CG].rearrange("p (f r c) -> p f r c", f=nf, r=2)
                Xr = src[:, :, 0, :]; Xi = src[:, :, 1, :]
                Kr = Ksp[o][:, f0:f0 + nf, 0, :]; Ki = Ksp[o][:, f0:f0 + nf, 1, :]
                a = tmq.get(); b = tmr.get(); yy = y32.get()
                c.op("dve", lambda e: e.tensor_tensor(out=a[:, 0:nf, :], in0=Xr, in1=Kr, op=ALU.mult), reads=[pb, Ksp[o]], writes=[a])
                c.op("dve", lambda e: e.tensor_tensor(out=b[:, 0:nf, :], in0=Xi, in1=Ki, op=ALU.mult), reads=[pb, Ksp[o]], writes=[b])
                c.op("pool", lambda e: e.tensor_tensor(out=yy[:, 0, 0:nf, :], in0=a[:, 0:nf, :], in1=b[:, 0:nf, :], op=ALU.subtract), reads=[a, b], writes=[yy])
                a = tmq.get(); b = tmr.get()
                c.op("dve", lambda e: e.tensor_tensor(out=a[:, 0:nf, :], in0=Xr, in1=Ki, op=ALU.mult), reads=[pb, Ksp[o]], writes=[a])
                c.op("dve", lambda e: e.tensor_tensor(out=b[:, 0:nf, :], in0=Xi, in1=Kr, op=ALU.mult), reads=[pb, Ksp[o]], writes=[b])
                c.op("pool", lambda e: e.tensor_tensor(out=yy[:, 1, 0:nf, :], in0=a[:, 0:nf, :], in1=b[:, 0:nf, :], op=ALU.add), reads=[a, b], writes=[yy])
                hsl = Yh[:, f0:f0 + nf, 1:3, :].rearrange("p f r c -> p r f c")
                lsl = Yl[:, f0:f0 + nf, 1:3, :].rearrange("p f r c -> p r f c")
                c.op("act", lambda e: e.copy(out=hsl, in_=yy[:, :, 0:nf, :]), reads=[yy], writes=[Yh])
                c.op("pool", lambda e: e.tensor_tensor(out=lsl, in0=yy[:, :, 0:nf, :], in1=hsl, op=ALU.subtract), reads=[yy, Yh], writes=[Yl])
                c.op("act", lambda e: e.activation(out=Yh[:, f0:f0 + nf, 0, :], in_=Yh[:, f0:f0 + nf, 2, :], func=AF.Copy, scale=-1.0), reads=[Yh], writes=[Yh])
                c.op("act", lambda e: e.activation(out=Yl[:, f0:f0 + nf, 0, :], in_=Yl[:, f0:f0 + nf, 2, :], func=AF.Copy, scale=-1.0), reads=[Yl], writes=[Yl])
            return psink

        def make_epi(o, src_lin, gate, dst):
            def epi(py, c0):
                bb = bias[:, o, ch0 + c0:ch0 + c0 + 4].unsqueeze(2).broadcast_to([T1, 4, 128])
                t3 = tmp[:, :].rearrange("p (c t) -> p c t", c=4)
                c.op("dve", lambda e: e.tensor_tensor(out=t3, in0=src_lin[0:T1, c0:c0 + 4, :], in1=bb, op=ALU.mult), reads=[src_lin, bias], writes=[tmp])
                c.op("dve", lambda e: e.tensor_tensor(out=t3, in0=py[0:T1, :].rearrange("p (c t) -> p c t", c=4), in1=t3, op=ALU.add), reads=[py, tmp], writes=[tmp])
                c.op("pool", lambda e: e.tensor_tensor(out=dst[0:T1, c0:c0 + 4, :], in0=t3, in1=gate[0:T1, c0:c0 + 4, :], op=ALU.mult), reads=[tmp, gate], writes=[dst])
            return epi
        fwd(pv, T1, make_psink(0))
        zt = xp.get()
        inverse(make_epi(0, pv, px1, zt))
        px2 = xp.get()
        load(px2, T1, hyc_d, 512 + ch0, tok0)
        fwd(zt, T1, make_psink(1))
        yt = xp.get()
        inverse(make_epi(1, zt, px2, yt))
        c.dma(rr(["sp", "pool"]), mixT_d[row0 + ch0:row0 + ch0 + CG, tok0:tok0 + L].rearrange("c (a b) -> a c b", b=128), yt[0:T1, :, :], reads=[yt], writes=[mixT_d])
    c.pop()


NFF = DFF // 128
WIN = 640
OWN = 512
NWINTOK = 2048 + 128


def emit_cast_dram(c, src_d, dst_d, R, X):
    c.push()
    st = Rot(c, 2, [128, X]); sb = Rot(c, 2, [128, X], BF16)
    for r in range(R):
        a = st.get(); b = sb.get()
        c.dma(("sp", "pool")[r % 2], a[:], src_d[r], reads=[src_d], writes=[a])
        eng = ("act", "dve", "pool")[r % 3]
        if eng == "act":
            c.op("act", lambda e: e.copy(out=b[:], in_=a[:]), reads=[a], writes=[b])
        else:
            c.op(eng, lambda e: e.tensor_copy(out=b[:], in_=a[:]), reads=[a], writes=[b])
        c.dma(("pool", "sp")[r % 2], dst_d[r], b[:], reads=[b], writes=[dst_d])
    c.pop()


def emit_B(c, t, last, blocks):
    ZKC = [6, 7, 14, 15]
    c.push()
    modsT = t["modsT_sb"]
    g2 = c.sbuf([128, KC]); c.dma("sp", g2[:], t["g2T"].t.ap(), reads=[t["g2T"]], writes=[g2])
    G = c.sbuf([128, 2, KC]); S = c.sbuf([128, 2, KC])
    for r in range(2):
        c.op("dve", lambda e: e.scalar_tensor_tensor(out=G[:, r, :], in0=modsT[:, 64:80, r], scalar=1.0, in1=g2[:], op0=ALU.add, op1=ALU.mult), reads=[modsT, g2], writes=[G])
        c.op("dve", lambda e: e.tensor_copy(out=S[:, r, :], in_=modsT[:, 48:64, r]), reads=[modsT], writes=[S])
    gluw = c.sbuf([128, 4, 512], BF16); glub = c.sbuf([128, 4])
    c.push()
    gst = c.sbuf([128, 4, 512])
    c.dma("sp", gst[:], t["gluw"].t.ap().rearrange("(kc p) n -> p kc n", p=128), reads=[t["gluw"]], writes=[gst])
    c.op("dve", lambda e: e.tensor_copy(out=gluw[:], in_=gst[:]), reads=[gst], writes=[gluw])
    c.pop()
    c.dma("sp", glub[:], t["glubT"].t.ap(), reads=[t["glubT"]], writes=[glub])
    cw = c.sbuf([128, NFF, 9]); cb = c.sbuf([128, NFF])
    c.dma("sp", cw[:], t["cwT"].t.ap(), reads=[t["cwT"]], writes=[cw])
    c.dma("sp", cb[:], t["cbT"].t.ap(), reads=[t["cbT"]], writes=[cb])
    nf = c.sbuf([128, KC])
    if last:
        c.dma("sp", nf[:], t["nfT"].t.ap(), reads=[t["nfT"]], writes=[nf])
    ones = c.sbuf([128, 128]); c.op("dve", lambda e: e.memset(ones[:], 1.0), writes=[ones])
    mask = c.sbuf([128, WIN])
    hb = c.sbuf([128, KC, WIN]); hviews = [hb.view() for _ in range(KC)]
    mx = c.sbuf([128, KC, WIN], BF16); mviews = [mx.view() for _ in range(KC)]
    w640 = Rot(c, 6, [128, WIN]); w512 = Rot(c, 5, [128, OWN])
    mst = sqr = tmpr = gr = sgr = w640
    accr = ger = outr = w512
    hff = c.sbuf([128, NFF, OWN], BF16); fviews = [hff.view() for _ in range(NFF)]
    wor = Rot(c, 2, [128, KC, 128], BF16); wur = Rot(c, 2, [128, KC, 256], BF16); wdr = Rot(c, 3, [128, NFF // 2, 128], BF16)
    psG = Rot(c, 2, [128, 512], psum=True); psV = Rot(c, 2, [128, 512], psum=True); psO = Rot(c, 2, [128, 512], psum=True); psS = Rot(c, 1, [128, 512], psum=True)
    sd = c.sbuf([128, WIN]); rstd = c.sbuf([128, WIN])
    z32v = [c.sbuf([128, WIN]) for _ in range(4)]
    z16 = c.sbuf([128, 4, WIN], BF16); z16v = [z16.view() for _ in range(4)]
    hsrc = t["hsrc"]
    mixbufs = t["mixbufs"]

    def mm_tok(ps_list, lhsT_fn, rhs_buf, rviews, a, b, nk, extra_reads):
        pieces = []
        s0 = a
        while s0 < b:
            n = min(512, b - s0)
            pieces.append((s0, n))
            s0 += n
        for (pi, (s0, n)) in enumerate(pieces):
            ps = ps_list[pi]
            for kc in range(nk):
                c.op("pe", lambda e: e.matmul(out=ps[:, 0:n], lhsT=lhsT_fn(kc), rhs=rhs_buf[:, kc, s0:s0 + n], start=(kc == 0), stop=(kc == nk - 1)),
                     reads=[rviews[kc]] + extra_reads, writes=[ps])
        return pieces

    def block(col0, W, oo, O, r, grid, mk, dst, dyn):
        def q2(i):
            return ("sp", "pool")[i % 2]

        def load(i, dst_ap, dst_buf, src_buf, row0, out_dt_buf=None):
            c.dma(q2(i), dst_ap, src_buf[row0:row0 + 128, col0:col0 + W], reads=[src_buf], writes=[dst_buf])
            if dyn is not None:
                col1, sel = dyn
                b_ = mst.get()
                c.dma(q2(i + 1), b_[:, 0:W], src_buf[row0:row0 + 128, col1:col1 + W], reads=[src_buf], writes=[b_])
                e1 = ("dve", "pool")[i % 2]
                c.op(e1, lambda e: e.tensor_scalar(out=dst_ap, in0=dst_ap, scalar1=sel[:, 0:1], scalar2=1.0, op0=ALU.mult, op1=ALU.mult), reads=[dst_buf, sel], writes=[dst_buf])
                c.op("dve", lambda e: e.scalar_tensor_tensor(out=dst_ap, in0=b_[:, 0:W], scalar=sel[:, 1:2], in1=dst_ap, op0=ALU.mult, op1=ALU.add), reads=[b_, sel, dst_buf], writes=[dst_buf])
        if mk is not None:
            c.dma("sp", mask[:, 0:W], mk[0][:, mk[1]:mk[1] + W], reads=[mk[0]], writes=[mask])
        for kc in range(KC):
            mb = mixbufs[kc // 8]
            mrow = (kc % 8) * 128
            if kc not in ZKC and dyn is None:
                c.dma("pool", mx[:, kc, 0:W], mb[mrow:mrow + 128, col0:col0 + W], reads=[mb], writes=[mviews[kc]])
            elif kc not in ZKC:
                m_ = mst.get()
                load(kc, m_[:, 0:W], m_, mb, mrow)
                if kc % 2 == 0:
                    c.op("act", lambda e: e.copy(out=mx[:, kc, 0:W], in_=m_[:, 0:W]), reads=[m_], writes=[mviews[kc]])
                else:
                    c.op("pool", lambda e: e.tensor_copy(out=mx[:, kc, 0:W], in_=m_[:, 0:W]), reads=[m_], writes=[mviews[kc]])
            else:
                zi = ZKC.index(kc)
                load(kc, z32v[zi][:, 0:W], z32v[zi], mb, mrow)
                c.op("act", lambda e: e.copy(out=z16[:, zi, 0:W], in_=z32v[zi][:, 0:W]), reads=[z32v[zi]], writes=[z16v[zi]])
            load(kc + 1, hb[:, kc, 0:W], hviews[kc], hsrc, kc * 128)
        for n_ in range(4):
            pl = [psO.get(), psO.get()]
            pieces = mm_tok(pl, lambda kc: gluw[:, kc, n_ * 128:(n_ + 1) * 128], z16, z16v, 0, W, 4, [gluw])
            for (pi, (s0, n)) in enumerate(pieces):
                sg = sgr.get()
                c.op("act", lambda e: e.activation(out=sg[:, 0:n], in_=pl[pi][:, 0:n], func=AF.Sigmoid, bias=glub[:, n_:n_ + 1]), reads=[pl[pi], glub], writes=[sg])
                c.op("dve", lambda e: e.tensor_tensor(out=mx[:, ZKC[n_], s0:s0 + n], in0=sg[:, 0:n], in1=z32v[n_][:, s0:s0 + n], op=ALU.mult), reads=[sg, z32v[n_]], writes=[mviews[ZKC[n_]]])
        for m in range(KC):
            wo = wor.get()
            c.dma(("sp", "pool")[m % 2], wo[:], t["wout16"][m].rearrange("p (k n) -> p k n", k=KC), reads=[t["wout16_v"][m // 8]], writes=[wo])
            pl = [psO.get(), psO.get()]
            pieces = mm_tok(pl, lambda kc: wo[:, kc, :], mx, mviews, 0, W, KC, [wo])
            for (pi, (s0, n)) in enumerate(pieces):
                c.op("dve", lambda e: e.scalar_tensor_tensor(out=hb[:, m, s0:s0 + n], in0=pl[pi][:, 0:n], scalar=modsT[:, 32 + m, r:r + 1], in1=hb[:, m, s0:s0 + n], op0=ALU.mult, op1=ALU.add),
                     reads=[pl[pi], modsT, hviews[m]], writes=[hviews[m]])
        pieces = []
        s0 = 0
        while s0 < W:
            n = min(512, W - s0); pieces.append((s0, n)); s0 += n
        for (s0, n) in pieces:
            ss = psS.get()
            for kc in range(KC):
                sq = sqr.get()
                c.op("act", lambda e: e.activation(out=sq[:, 0:n], in_=hb[:, kc, s0:s0 + n], func=AF.Square), reads=[hviews[kc]], writes=[sq])
                c.op("pe", lambda e: e.matmul(out=ss[:, 0:n], lhsT=ones[:], rhs=sq[:, 0:n], start=(kc == 0), stop=(kc == KC - 1)), reads=[ones, sq], writes=[ss])
            c.op("act", lambda e: e.activation(out=sd[:, s0:s0 + n], in_=ss[:, 0:n], func=AF.Sqrt, scale=1.0 / D, bias=EPS), reads=[ss], writes=[sd])
        c.op("dve", lambda e: e.reciprocal(out=rstd[:, 0:W], in_=sd[:, 0:W]), reads=[sd], writes=[rstd])
        for kc in range(KC):
            tm_ = tmpr.get()
            c.op("dve", lambda e: e.tensor_tensor(out=tm_[:, 0:W], in0=hb[:, kc, 0:W], in1=rstd[:, 0:W], op=ALU.mult), reads=[hviews[kc], rstd], writes=[tm_])
            c.op("act", lambda e: e.activation(out=mx[:, kc, 0:W], in_=tm_[:, 0:W], func=AF.Identity, scale=G[:, r, kc:kc + 1], bias=S[:, r, kc:kc + 1]),
                 reads=[tm_, G, S], writes=[mviews[kc]])
        for j in range(NFF):
            wu = wur.get()
            c.dma(("sp", "pool")[j % 2], wu[:], t["wup16"][j].rearrange("p (k n) -> p k n", k=KC), reads=[t["wup16_v"][j // 4]], writes=[wu])
            gl = [psG.get(), psG.get()]
            gp = mm_tok(gl, lambda kc: wu[:, kc, 0:128], mx, mviews, 0, W, KC, [wu])
            vl = [psV.get()]
            mm_tok(vl, lambda kc: wu[:, kc, 128:256], mx, mviews, oo, oo + O, KC, [wu])
            g = gr.get()
            for (pi, (s0, n)) in enumerate(gp):
                if mk is None:
                    c.op("act", lambda e: e.copy(out=g[:, s0:s0 + n], in_=gl[pi][:, 0:n]), reads=[gl[pi]], writes=[g])
                else:
                    c.op("dve", lambda e: e.tensor_tensor(out=g[:, s0:s0 + n], in0=gl[pi][:, 0:n], in1=mask[:, s0:s0 + n], op=ALU.mult), reads=[gl[pi], mask], writes=[g])
            acc = accr.get()
            if grid:
                g3 = g[:, 0:W].rearrange("p (r x) -> p r x", x=64)
                a3 = acc[:, 0:O].rearrange("p (r x) -> p r x", x=64)
                nr = O // 64
                c.op("dve", lambda e: e.tensor_scalar(out=a3, in0=g3[:, 1:1 + nr, :], scalar1=cw[:, j, 4:5], scalar2=cb[:, j:j + 1], op0=ALU.mult, op1=ALU.add), reads=[g, cw, cb], writes=[acc])
                for dr in (-1, 0, 1):
                    for dc in (-1, 0, 1):
                        if dr == 0 and dc == 0:
                            continue
                        k = (dr + 1) * 3 + (dc + 1)
                        xo = slice(max(0, -dc), 64 - max(0, dc))
                        xi = slice(max(0, dc), 64 - max(0, -dc))
                        c.op("dve", lambda e: e.scalar_tensor_tensor(out=a3[:, :, xo], in0=g3[:, 1 + dr:1 + dr + nr, xi], scalar=cw[:, j, k:k + 1], in1=a3[:, :, xo], op0=ALU.mult, op1=ALU.add),
                             reads=[g, cw, acc], writes=[acc])
            else:
                c.op("dve", lambda e: e.tensor_scalar(out=acc[:, 0:O], in0=g[:, 0:O], scalar1=cw[:, j, 4:5], scalar2=cb[:, j:j + 1], op0=ALU.mult, op1=ALU.add), reads=[g, cw, cb], writes=[acc])
                c.op("dve", lambda e: e.scalar_tensor_tensor(out=acc[:, 1:O], in0=g[:, 0:O - 1], scalar=cw[:, j, 3:4], in1=acc[:, 1:O], op0=ALU.mult, op1=ALU.add), reads=[g, cw, acc], writes=[acc])
                c.op("dve", lambda e: e.scalar_tensor_tensor(out=acc[:, 0:O - 1], in0=g[:, 1:O], scalar=cw[:, j, 5:6], in1=acc[:, 0:O - 1], op0=ALU.mult, op1=ALU.add), reads=[g, cw, acc], writes=[acc])
            ge = ger.get()
            c.op("act", lambda e: e.activation(out=ge[:, 0:O], in_=acc[:, 0:O], func=AF.Gelu_apprx_tanh), reads=[acc], writes=[ge])
            c.op("dve", lambda e: e.tensor_tensor(out=hff[:, j, 0:O], in0=vl[0][:, 0:O], in1=ge[:, 0:O], op=ALU.mult), reads=[vl[0], ge], writes=[fviews[j]])
        dbuf, dcol = dst
        for m in range(KC):
            wda = wdr.get(); wdb = wdr.get()
            wsrc = t["wdn16"][m].rearrange("p (k n) -> p k n", k=NFF)
            c.dma("sp", wda[:], wsrc[:, 0:NFF // 2, :], reads=[t["wdn16_v"][m // 4]], writes=[wda])
            c.dma("pool", wdb[:], wsrc[:, NFF // 2:NFF, :], reads=[t["wdn16_v"][m // 4]], writes=[wdb])
            pl = [psO.get()]
            mm_tok(pl, lambda kc: (wda[:, kc, :] if kc < NFF // 2 else wdb[:, kc - NFF // 2, :]), hff, fviews, 0, O, NFF, [wda, wdb])
            c.op("dve", lambda e: e.scalar_tensor_tensor(out=hb[:, m, oo:oo + O], in0=pl[0][:, 0:O], scalar=modsT[:, 80 + m, r:r + 1], in1=hb[:, m, oo:oo + O], op0=ALU.mult, op1=ALU.add),
                 reads=[pl[0], modsT, hviews[m]], writes=[hviews[m]])
            if not last:
                c.dma(("pool", "sp")[m % 2], dbuf[m * 128:(m + 1) * 128, dcol:dcol + O], hb[:, m, oo:oo + O], reads=[hviews[m]], writes=[dbuf])
        if last:
            ss = psS.get()
            for kc in range(KC):
                sq = sqr.get()
                c.op("act", lambda e: e.activation(out=sq[:, 0:O], in_=hb[:, kc, oo:oo + O], func=AF.Square), reads=[hviews[kc]], writes=[sq])
                c.op("pe", lambda e: e.matmul(out=ss[:, 0:O], lhsT=ones[:], rhs=sq[:, 0:O], start=(kc == 0), stop=(kc == KC - 1)), reads=[ones, sq], writes=[ss])
            c.op("act", lambda e: e.activation(out=sd[:, 0:O], in_=ss[:, 0:O], func=AF.Sqrt, scale=1.0 / D, bias=EPS), reads=[ss], writes=[sd])
            c.op("dve", lambda e: e.reciprocal(out=rstd[:, 0:O], in_=sd[:, 0:O]), reads=[sd], writes=[rstd])
            for kc in range(KC):
                ot = outr.get()
                c.op("dve", lambda e: e.scalar_tensor_tensor(out=ot[:, 0:O], in0=hb[:, kc, oo:oo + O], scalar=nf[:, kc:kc + 1], in1=rstd[:, 0:O], op0=ALU.mult, op1=ALU.mult),
                     reads=[hviews[kc], nf, rstd], writes=[ot])
                c.dma(("pool", "sp")[kc % 2], dbuf[kc * 128:(kc + 1) * 128, dcol:dcol + O], ot[:, 0:O], reads=[ot], writes=[dbuf])

    for b_ in blocks:
        block(b_["col0"], b_["W"], b_["oo"], b_["O"], b_["r"], b_["grid"], b_.get("mask"), b_["dst"], b_.get("dyn"))
    c.pop()


def B_host_weights(inp, l, last):
    w = inp["ffn_w_up"][l]
    wup = np.stack([w[:, :DFF].reshape(KC, 128, NFF, 128), w[:, DFF:].reshape(KC, 128, NFF, 128)], axis=3)
    wup = np.ascontiguousarray(wup.transpose(2, 1, 0, 3, 4)).reshape(NFF, 128, KC * 256)
    wdn = np.ascontiguousarray(inp["ffn_w_down"][l].reshape(NFF, 128, KC, 128).transpose(2, 1, 0, 3)).reshape(KC, 128, NFF * 128)
    wout = np.ascontiguousarray(inp["w_out"][l].reshape(KC, 128, KC, 128).transpose(2, 1, 0, 3)).reshape(KC, 128, KC * 128)
    d = {"wup": wup, "wdn": wdn, "wout": wout,
         "g2T": np.ascontiguousarray(inp["norm2_g"][l].reshape(KC, 128).T),
         "gluw": inp["s5_glu_w"][l], "glubT": np.ascontiguousarray(inp["s5_glu_b"][l].reshape(4, 128).T),
         "cwT": np.ascontiguousarray(inp["ffn_conv_w"][l].reshape(9, NFF, 128).transpose(2, 1, 0)),
         "cbT": np.ascontiguousarray(inp["ffn_conv_b"][l].reshape(NFF, 128).T)}
    if last:
        d["nfT"] = np.ascontiguousarray(inp["norm_f"].reshape(KC, 128).T)
    return d


TP = T + 64
MIX_PERM = np.concatenate([np.arange(0, 256), np.arange(512, 1024), np.arange(1536, 1792),
                           np.arange(256, 512), np.arange(1024, 1536), np.arange(1792, 2048)])


def A_const_inputs():
    return {**ret_consts(), **s5_consts(), **hy_tables(SEQ), **hy_tables(CTX), "ident": np.eye(128, dtype=np.float32)}


def A_weight_inputs(inp, l, hf):
    r = np.arange
    colsf = np.concatenate([r(256) + 256 * hf, 512 + r(256) + 256 * hf, 1024 + r(256) + 256 * hf, 1536 + 256 * hf + r(256),
                            2048 + 256 * hf + r(256), 4608 + 256 * hf + r(256)])
    colst = np.concatenate([2048 + 256 * hf + r(256), 2560 + 512 * hf + r(512), 3584 + 512 * hf + r(512)])
    d = {"win": np.ascontiguousarray(inp["w_in"][l][:, np.concatenate([colsf, colst])]),
         "rdec": np.ascontiguousarray(inp["ret_decay"][l][:, 4 * hf:4 * hf + 4]).reshape(-1)}
    d.update(s5_host(inp, l, hf))
    d.update(hy_host(inp, l, hf))
    return d


def layer_inputs(inp, l):
    last = l == 1
    d = {"adaw": inp["ada_w"][l], "adabT": np.ascontiguousarray(inp["ada_b"][l].reshape(96, 128).T),
         "g1T": np.ascontiguousarray(inp["norm1_g"][l].reshape(KC, 128).T)}
    inp2 = dict(inp)
    wo = np.array(inp["w_out"], copy=True)
    wo[l] = inp["w_out"][l][MIX_PERM, :]
    inp2["w_out"] = wo
    d.update(B_host_weights(inp2, l, last))
    out = {f"{k}_{l}": v for k, v in d.items()}
    for hf in range(2):
        for k, v in A_weight_inputs(inp, l, hf).items():
            out[f"{k}_{l}{hf}"] = v
    return out


CONST_NAMES = (["ret_pf", "ret_pb", "ret_posq", "ret_posk", "s5_pos", "ident"]
               + [f"hy{L_}_{n_}" for L_ in (SEQ, CTX) for n_ in ("W1", "TBh", "TBl", "TBTh", "TBTl", "Minv", "zT", "tn")])


def build_fused(shapes):
    c = Ctx()
    nc = c.nc
    t = {n: c.dram(n, list(sh), BF16 if is16 else F32, "ExternalInput") for n, (sh, is16) in shapes.items()}
    out_d = c.dram("out", [D, 2048], F32, "ExternalOutput")
    sc = {"hyT": c.dram("hyT", [768, T], F32), "qT": c.dram("qT", [256, T], BF16), "kT": c.dram("kT", [256, T], BF16),
          "uT": c.dram("uT", [256, T], F32), "k": c.dram("k", [T, 256], BF16), "v": c.dram("v", [T, 512], BF16),
          "sg": c.dram("sg", [T, 512], F32)}
    hyc_d = c.dram("hyc", [768, T], F32)
    tapsL = c.dram("tapsL", [2, 256, 2 * SEQ], F32)
    tapsC = c.dram("tapsC", [2, 256, 2 * CTX], F32)
    mixbufs = [c.dram("mixT0", [1024, TP], F32), c.dram("mixT1", [1024, TP], F32)]
    h1T = c.dram("h1T", [D, TP], F32)
    w16 = [{"wup16": c.dram(f"wup16_{l}", [NFF, 128, KC * 256], BF16), "wdn16": c.dram(f"wdn16_{l}", [KC, 128, NFF * 128], BF16),
            "wout16": c.dram(f"wout16_{l}", [KC, 128, KC * 128], BF16)} for l in range(2)]
    c.push()
    modsT = c.sbuf([128, 96, 2])
    ident = c.sbuf([128, 128])
    c.dma("sp", ident[:], t["ident"].t.ap(), reads=[t["ident"]], writes=[ident])
    c.push()
    zt = c.sbuf([128, 16, 64])
    c.op("dve", lambda e: e.memset(zt[:], 0.0), writes=[zt])
    for mb in mixbufs:
        c.dma("sp", mb.t.ap().rearrange("(a p) t -> p a t", p=128)[:, :, T:TP], zt[:, 0:8, :], reads=[zt], writes=[mb])
    c.dma("sp", h1T.t.ap().rearrange("(a p) t -> p a t", p=128)[:, :, T:TP], zt[:], reads=[zt], writes=[h1T])
    c.pop()
    sel = c.sbuf([128, 2])
    c.dma("sp", sel[:], t["sel"].t.ap(), reads=[t["sel"]], writes=[sel])
    pending = []
    for l in range(2):
        for (nm, R, step) in (("wup", NFF, 4), ("wdn", KC, 4), ("wout", KC, 8)):
            for r0 in range(0, R, step):
                w16[l].setdefault(nm + "16_v", []).append(w16[l][nm + "16"].view())
                pending.append((w16[l][nm + "16"], w16[l][nm + "16_v"][-1], t[f"{nm}_{l}"], r0, step))

    def bg_issue(k):
        for _ in range(k):
            if pending:
                dst, dview, src_, r0, step = pending.pop(0)
                c.dma("pool", dst[r0:r0 + step], src_[r0:r0 + step], reads=[], writes=[dview], own_sem=True)
    for l in range(2):
        last = l == 1
        emit_mods(c, t["cT"], t[f"adaw_{l}"], t[f"adabT_{l}"], modsT)
        src = t["hT"] if l == 0 else h1T
        for hf in range(2):
            w = {k: t[k] for k in CONST_NAMES}
            sfx = f"_{l}{hf}"
            for k, v in t.items():
                if k.endswith(sfx):
                    w[k[:-len(sfx)]] = v
            emit_inproj(c, src, w["win"], t[f"g1T_{l}"], modsT, sc)
            bg_issue(5)
            c.push()
            thunks = hy_shortconv_thunks(c, sc["hyT"], w, hyc_d)
            emit_retention(c, sc["qT"], sc["kT"], sc["k"], sc["v"], sc["sg"], w["rdec"], w, ident, mixbufs[hf], 256, extra=thunks)
            c.pop()
            if not last:
                emit_hy_filters(c, CTX, w, tapsC)
            emit_hy_filters(c, SEQ, w, tapsL)
            if not last:
                emit_hy_conv(c, CTX, 0, w, hyc_d, tapsC, ident, mixbufs[hf], 0)
            emit_hy_conv(c, SEQ, CTX, w, hyc_d, tapsL, ident, mixbufs[hf], 0)
            bg_issue(5)
            bg_issue(5)
            emit_s5(c, sc["uT"], w, mixbufs[hf], 768)
        bg_issue(100 if l == 1 else 4)
        tb = {"modsT_sb": modsT, "mixbufs": mixbufs, "hsrc": src, **w16[l]}
        for k in ("wup", "wdn", "wout", "g2T", "gluw", "glubT", "cwT", "cbT"):
            tb[k] = t[f"{k}_{l}"]
        if last:
            tb["nfT"] = t["nfT_1"]
        blocks = []
        if not last:
            blocks.append(dict(col0=0, W=CTX, oo=0, O=CTX, r=1, grid=False, mask=None, dst=(h1T, 0)))
            for blk in range(8):
                mk = (t["maskL0"], 0) if blk == 0 else ((t["maskL0"], WIN) if blk == 7 else None)
                blocks.append(dict(col0=CTX - 64 + blk * OWN, W=WIN, oo=64, O=OWN, r=0, grid=True, mask=mk, dst=(h1T, CTX + blk * OWN)))
        else:
            for blk in range(4):
                blocks.append(dict(col0=CTX - 64 + blk * OWN, W=WIN, oo=64, O=OWN, r=0, grid=True,
                                   mask=(t["mask"], blk * OWN), dst=(out_d, blk * OWN), dyn=(CTX - 64 + blk * OWN + 2048, sel)))
        emit_B(c, tb, last, blocks)
    c.pop()
    return c


def kernel(**inp):
    inp = {k: np.asarray(v) for k, v in inp.items()}
    x, cc, ctx, c_ctx = inp["x"], inp["c"], inp["ctx"], inp["c_ctx"]
    NB = x.shape[0]
    cores = list(range(NCORES))
    consts = A_const_inputs()
    maskL0 = np.ones([128, 2 * WIN], np.float32)
    maskL0[:, 0:64] = 0.0
    maskL0[:, WIN + 576:WIN + 640] = 0.0
    shared = {**consts, "maskL0": maskL0}
    for l in range(2):
        shared.update(layer_inputs(inp, l))
    maps = []
    for i in cores:
        b, hf = i // 2, i % 2
        hT = np.zeros([D, TP], np.float32)
        hT[:, 0:CTX] = ctx[b].T
        hT[:, CTX:T] = x[b].T
        mask = np.ones([128, NWINTOK], np.float32)
        if hf == 0:
            mask[:, 0:64] = 0.0
        else:
            mask[:, NWINTOK - 64:] = 0.0
        sel = np.zeros([128, 2], np.float32)
        sel[:, hf] = 1.0
        m = {**shared, "hT": hT, "cT": np.ascontiguousarray(np.stack([cc[b], c_ctx], 1)), "mask": mask, "sel": sel}
        maps.append(m)
    cF = build_fused({n: (a.shape, a.dtype == ml_dtypes.bfloat16) for n, a in maps[0].items()})
    res = run_bass_kernel_spmd(cF.nc, maps, core_ids=cores).results
    out = np.stack([np.concatenate([res[2 * b]["out"], res[2 * b + 1]["out"]], 1).T for b in range(NB)], 0)
    return np.ascontiguousarray(out.astype(np.float32))
```

```python
import contextlib
import math
import numpy as np
import ml_dtypes
import concourse.bass as bass
import concourse.mybir as mybir
from concourse.bass_utils import run_bass_kernel_spmd

F32 = mybir.dt.float32
BF16 = mybir.dt.bfloat16
I32 = mybir.dt.int32
AF = mybir.ActivationFunctionType
ALU = mybir.AluOpType
AX = mybir.AxisListType

D = 2048
KC = 16
SEQ = 4096
CTX = 256
T = SEQ + CTX
EPS = 1e-6
DFF = 5632
NCORES = 8
import os
NO_SELF_WAIT = os.environ.get('NO_SELF_WAIT', '0') == '1'


class Buf:
    def __init__(self, name, t):
        self.name = name
        self.t = t
        self.w = None
        self.r = {}

    def __getitem__(self, idx):
        return self.t[idx]

    def view(self):
        return Buf(self.name + "_v", self.t)


class Ctx:
    NDMA = 8

    def __init__(self):
        self.nc = bass.Bass("TRN2", target_bir_lowering=False, num_devices=NCORES)
        nc = self.nc
        self.eng = {"pe": nc.tensor, "dve": nc.vector, "act": nc.scalar, "pool": nc.gpsimd, "sp": nc.sync}
        self.sem = {}
        self.cnt = {}
        for e in self.eng:
            self.sem[e] = nc.alloc_semaphore("s_" + e)
            self.cnt[e] = 0
        self.dsem, self.dcnt, self.dptr = {}, {}, {}
        for q in ("sp", "act", "pool"):
            self.dsem[q] = [nc.alloc_semaphore(f"d_{q}{i}") for i in range(self.NDMA)]
            self.dcnt[q] = [0] * self.NDMA
            self.dptr[q] = 0
        self.seen = {e: {} for e in self.eng}
        self.bg = set()
        self.nbuf = 0
        self.stack = contextlib.ExitStack()
        self.scopes = []
        self._es = contextlib.ExitStack()
        self._es.enter_context(nc.allow_low_precision("bf16 matmul operands, fp32 accumulation"))
        self._es.enter_context(nc.allow_non_contiguous_dma("layout transforms"))

    def push(self):
        self.scopes.append(contextlib.ExitStack())

    def pop(self):
        self.barrier()
        self.scopes.pop().close()

    def sbuf(self, shape, dtype=F32, name=None):
        self.nbuf += 1
        name = name or f"sb{self.nbuf}"
        t = self.scopes[-1].enter_context(self.nc.sbuf_tensor(name, list(shape), dtype))
        return Buf(name, t)

    def psum(self, shape, dtype=F32, name=None):
        self.nbuf += 1
        name = name or f"ps{self.nbuf}"
        t = self.scopes[-1].enter_context(self.nc.psum_tensor(name, list(shape), dtype))
        return Buf(name, t)

    def dram(self, name, shape, dtype=F32, kind="Internal"):
        return Buf(name, self.nc.dram_tensor(name, list(shape), dtype, kind=kind))

    def _semh(self, key):
        if isinstance(key, tuple):
            return self.dsem[key[0]][key[1]]
        return self.sem[key]

    def _wait(self, e, k, v):
        if self.seen[e].get(k, -1) >= v:
            return
        self.eng[e].wait_ge(self._semh(k), v)
        self.seen[e][k] = v

    def _deps(self, e, reads, writes):
        need = {}

        def add(k, v):
            if need.get(k, -1) < v:
                need[k] = v
        for b in reads:
            if b.w is not None:
                add(*b.w)
        for b in writes:
            if b.w is not None:
                add(*b.w)
            for k, v in b.r.items():
                add(k, v)
        for k, v in need.items():
            if k == "pe" and e == "pe":
                continue
            if k == e and NO_SELF_WAIT:
                continue
            self._wait(e, k, v)

    def _mark(self, key, val, reads, writes):
        for b in reads:
            if b.r.get(key, -1) < val:
                b.r[key] = val
        for b in writes:
            b.w = (key, val)
            b.r = {}

    def op(self, e, fn, reads=(), writes=()):
        self._deps(e, reads, writes)
        ins = fn(self.eng[e])
        self.cnt[e] += 1
        ins.then_inc(self.sem[e], 1)
        self._mark(e, self.cnt[e], reads, writes)
        return ins

    def dma(self, q, out, in_, reads=(), writes=(), own_sem=False):
        if own_sem:
            self.dsem[q].append(self.nc.alloc_semaphore(f"d_{q}{len(self.dsem[q])}"))
            self.dcnt[q].append(0)
            i = len(self.dsem[q]) - 1
        else:
            i = self.dptr[q]
            self.dptr[q] = (i + 1) % self.NDMA
        key = (q, i)
        if self.dcnt[q][i] > 0:
            self._wait(q, key, self.dcnt[q][i])
        self._deps(q, reads, writes)
        ins = self.eng[q].dma_start(out=out, in_=in_)
        self.dcnt[q][i] += 16
        ins.then_inc(self.dsem[q][i], 16)
        self._mark(key, self.dcnt[q][i], reads, writes)
        if own_sem:
            self.bg.add(key)
        return ins

    def barrier(self):
        for e in self.eng:
            for k in self.eng:
                if k != e and self.cnt[k] > 0:
                    self._wait(e, k, self.cnt[k])
            for q in self.dsem:
                for i in range(len(self.dsem[q])):
                    if self.dcnt[q][i] > 0 and not (q, i) in self.bg:
                        self._wait(e, (q, i), self.dcnt[q][i])

    def finish(self):
        self.barrier()


_rr = {"i": 0}


def rr(lst):
    _rr["i"] += 1
    return lst[_rr["i"] % len(lst)]


class Rot:
    def __init__(self, c, n, shape, dtype=F32, psum=False):
        self.b = [(c.psum if psum else c.sbuf)(shape, dtype) for _ in range(n)]
        self.i = 0

    def get(self):
        self.i = (self.i + 1) % len(self.b)
        return self.b[self.i]


NF = 1536
NT = 1280
NW = NF + NT


def emit_mods(c, cT_d, adaw_d, adabT_d, modsT, R=2, ntile=96):
    c.push()
    cs = c.sbuf([128, KC, R])
    c.dma("sp", cs[:], cT_d.t.ap().rearrange("(kc p) r -> p kc r", p=128), reads=[cT_d], writes=[cs])
    sc = c.sbuf([128, KC, R])
    c.op("act", lambda e: e.activation(out=sc[:], in_=cs[:], func=AF.Silu), reads=[cs], writes=[sc])
    bT = c.sbuf([128, ntile])
    c.dma("sp", bT[:], adabT_d.t.ap(), reads=[adabT_d], writes=[bT])
    wrot = Rot(c, 2, [128, KC, 512])
    prot = Rot(c, 2, [128, 512], psum=True)
    wv = adaw_d.t.ap().rearrange("(kc p) n -> p kc n", p=128)
    for nb in range(ntile // 4):
        w = wrot.get()
        for h in range(2):
            c.dma(("sp", "pool")[h], w[:, h * 8:(h + 1) * 8, :], wv[:, h * 8:(h + 1) * 8, nb * 512:(nb + 1) * 512],
                  reads=[adaw_d], writes=[w])
        ps = prot.get()
        for jj in range(4):
            for kc in range(KC):
                for h_ in range(2):
                    c.op("pe", lambda e: e.matmul(out=ps[64 * h_:64 * h_ + 64, jj * R:(jj + 1) * R], lhsT=w[:, kc, jj * 128 + 64 * h_:jj * 128 + 64 * h_ + 64],
                                                  rhs=sc[:, kc, :], start=(kc == 0), stop=(kc == KC - 1)),
                         reads=[w, sc], writes=[ps])
        for jj in range(4):
            j = nb * 4 + jj
            c.op("dve", lambda e: e.tensor_scalar(out=modsT[:, j, :], in0=ps[:, jj * R:(jj + 1) * R],
                                                  scalar1=bT[:, j:j + 1], scalar2=None, op0=ALU.add),
                 reads=[ps, bT], writes=[modsT])
    c.pop()


def emit_inproj(c, hT_d, win_d, g1T_d, modsT, outs):
    c.push()
    w_sb = c.sbuf([128, KC, NW], BF16)
    wv = win_d.t.ap().rearrange("(kc p) n -> p kc n", p=128)
    WG = [(0, 384), (384, 768), (768, 1024), (1024, 1280), (1280, 1536), (1536, 1792), (1792, 2304), (2304, 2816)]
    wgv = [w_sb.view() for _ in WG]
    for gi, (g0, g1) in enumerate(WG):
        c.dma("pool", w_sb[:, :, g0:g1], wv[:, :, g0:g1], reads=[win_d], writes=[wgv[gi]])

    def wview(col):
        for gi, (g0, g1) in enumerate(WG):
            if g0 <= col < g1:
                return wgv[gi]
    g1 = c.sbuf([128, KC])
    c.dma("sp", g1[:], g1T_d.t.ap(), reads=[g1T_d], writes=[g1])
    G = c.sbuf([128, 2, KC])
    S = c.sbuf([128, 2, KC])
    for r in range(2):
        c.op("dve", lambda e: e.scalar_tensor_tensor(out=G[:, r, :], in0=modsT[:, 16:32, r], scalar=1.0, in1=g1[:],
                                                     op0=ALU.add, op1=ALU.mult), reads=[modsT, g1], writes=[G])
        c.op("dve", lambda e: e.tensor_copy(out=S[:, r, :], in_=modsT[:, 0:16, r]), reads=[modsT], writes=[S])
    ones = c.sbuf([128, 128])
    c.op("dve", lambda e: e.memset(ones[:], 1.0), writes=[ones])
    hblk = c.sbuf([128, KC, 512])
    hviews = [hblk.view() for _ in range(KC)]
    xrot = Rot(c, 2, [128, KC, 512], BF16)
    sqrot = Rot(c, 3, [128, 512])
    tmprot = Rot(c, 3, [128, 512])
    ssps = Rot(c, 1, [128, 512], psum=True)
    prot = Rot(c, 5, [128, 512], psum=True)
    ev32 = Rot(c, 3, [128, 512])
    ev16 = Rot(c, 3, [128, 512], BF16)
    sd = c.sbuf([128, 512])
    rstd = c.sbuf([128, 512])
    hv = hT_d.t.ap().rearrange("(kc p) t -> p kc t", p=128)
    blocks = [(0, CTX, 1)] + [(CTX + i * 512, 512, 0) for i in range(SEQ // 512)]
    evi = 0
    def do_norm(t0, nb, r):
        for kc in range(KC):
            c.dma(("sp", "pool")[kc % 2], hblk[:, kc, :nb], hv[:, kc, t0:t0 + nb], reads=[hT_d], writes=[hviews[kc]])
        ss = ssps.get()
        for kc in range(KC):
            sq = sqrot.get()
            c.op("act", lambda e: e.activation(out=sq[:, :nb], in_=hblk[:, kc, :nb], func=AF.Square),
                 reads=[hviews[kc]], writes=[sq])
            for h_ in range(2):
                c.op("pe", lambda e: e.matmul(out=ss[64 * h_:64 * h_ + 64, :nb], lhsT=ones[:, 0:64], rhs=sq[:, :nb], start=(kc == 0), stop=(kc == KC - 1)),
                     reads=[ones, sq], writes=[ss])
        c.op("act", lambda e: e.activation(out=sd[:, :nb], in_=ss[:, :nb], func=AF.Sqrt, scale=1.0 / D, bias=EPS),
             reads=[ss], writes=[sd])
        c.op("dve", lambda e: e.reciprocal(out=rstd[:, :nb], in_=sd[:, :nb]), reads=[sd], writes=[rstd])
        xn = xrot.get()
        for kc in range(KC):
            tmp = tmprot.get()
            c.op("dve", lambda e: e.tensor_tensor(out=tmp[:, :nb], in0=hblk[:, kc, :nb], in1=rstd[:, :nb], op=ALU.mult),
                 reads=[hviews[kc], rstd], writes=[tmp])
            c.op("act", lambda e: e.activation(out=xn[:, kc, :nb], in_=tmp[:, :nb], func=AF.Identity,
                                               scale=G[:, r, kc:kc + 1], bias=S[:, r, kc:kc + 1]),
                 reads=[tmp, G, S], writes=[xn])
        return xn

    def do_mm(t0, nb, r, xn):
        nonlocal evi
        fm = [("hyT", 0, 6, F32, 1.0), ("qT", 6, 2, BF16, 1.0), ("kT", 8, 2, BF16, 0.125), ("uT", 10, 2, F32, 1.0)]
        for (name, m0, nm, dt, scl) in fm:
            for mi in range(nm):
                col0 = (m0 + mi) * 128
                ps = prot.get()
                for kc in range(KC):
                    c.op("pe", lambda e: e.matmul(out=ps[:, :nb], lhsT=w_sb[:, kc, col0:col0 + 128], rhs=xn[:, kc, :nb],
                                                  start=(kc == 0), stop=(kc == KC - 1)),
                         reads=[wview(col0), xn], writes=[ps])
                ev = (ev32 if dt == F32 else ev16).get()
                evi += 1
                if evi % 2 == 0:
                    c.op("act", lambda e: e.activation(out=ev[:, :nb], in_=ps[:, :nb], func=AF.Copy, scale=scl),
                         reads=[ps], writes=[ev])
                else:
                    c.op("dve", lambda e: e.tensor_scalar(out=ev[:, :nb], in0=ps[:, :nb], scalar1=scl, scalar2=None,
                                                          op0=ALU.mult), reads=[ps], writes=[ev])
                o = outs[name]
                c.dma("sp", o[mi * 128:(mi + 1) * 128, t0:t0 + nb], ev[:, :nb], reads=[ev], writes=[o])
        tm = [("k", NF, 256, BF16, 0.125, None), ("v", NF + 256, 512, BF16, 1.0, None), ("sg", NF + 768, 512, F32, 1.0, AF.Silu)]
        for ti in range(nb // 128):
            for (name, col0, ncol, dt, scl, fn) in tm:
                ps = prot.get()
                for kc in range(KC):
                    c.op("pe", lambda e: e.matmul(out=ps[:, :ncol], lhsT=xn[:, kc, ti * 128:(ti + 1) * 128],
                                                  rhs=w_sb[:, kc, col0:col0 + ncol], start=(kc == 0), stop=(kc == KC - 1)),
                         reads=[wview(col0), xn], writes=[ps])
                ev = (ev32 if dt == F32 else ev16).get()
                evi += 1
                if fn is not None:
                    c.op("act", lambda e: e.activation(out=ev[:, :ncol], in_=ps[:, :ncol], func=fn), reads=[ps], writes=[ev])
                elif evi % 2 == 0:
                    c.op("act", lambda e: e.activation(out=ev[:, :ncol], in_=ps[:, :ncol], func=AF.Copy, scale=scl),
                         reads=[ps], writes=[ev])
                else:
                    c.op("dve", lambda e: e.tensor_scalar(out=ev[:, :ncol], in0=ps[:, :ncol], scalar1=scl, scalar2=None,
                                                          op0=ALU.mult), reads=[ps], writes=[ev])
                o = outs[name]
                c.dma("pool", o[t0 + ti * 128:t0 + (ti + 1) * 128, :], ev[:, :ncol], reads=[ev], writes=[o])

    xn_next = do_norm(*blocks[0])
    for bi, (t0, nb, r) in enumerate(blocks):
        xn_cur = xn_next
        if bi + 1 < len(blocks):
            xn_next = do_norm(*blocks[bi + 1])
        do_mm(t0, nb, r, xn_cur)
    c.pop()


NCH = T // 128


def ret_consts():
    m = np.arange(128)[:, None].astype(np.float64)
    cc = np.arange(128)[None, :].astype(np.float64)
    pf = np.where(cc >= m, cc - m, 1e9)
    pb = np.where(m > cc, m - cc, 1e9)
    posq = np.stack([np.broadcast_to(cc + 1, (128, 128)), np.broadcast_to(128 - cc, (128, 128))])
    posk = np.stack([127 - m[:, 0], m[:, 0]], 1)
    return {"ret_pf": pf.astype(np.float32), "ret_pb": pb.astype(np.float32),
            "ret_posq": np.ascontiguousarray(posq.transpose(1, 0, 2)).astype(np.float32),
            "ret_posk": posk.astype(np.float32)}


def emit_retention(c, qT_d, kT_d, k_d, v_d, sg_d, rdec_d, cst, ident, mixT_d, row0, extra=()):
    c.push()
    lg = c.sbuf([128, 8])
    c.dma("sp", lg[:], rdec_d.t.ap().partition_broadcast(128), reads=[rdec_d], writes=[lg])
    c.op("act", lambda e: e.activation(out=lg[:], in_=lg[:], func=AF.Exp), reads=[lg], writes=[lg])
    c.op("dve", lambda e: e.tensor_scalar(out=lg[:], in0=lg[:], scalar1=-1.0, scalar2=None, op0=ALU.mult), reads=[lg], writes=[lg])
    pf = c.sbuf([128, 128]); pb = c.sbuf([128, 128]); posq = c.sbuf([128, 2, 128]); posk = c.sbuf([128, 2])
    c.dma("sp", pf[:], cst["ret_pf"].t.ap(), reads=[cst["ret_pf"]], writes=[pf])
    c.dma("sp", pb[:], cst["ret_pb"].t.ap(), reads=[cst["ret_pb"]], writes=[pb])
    c.dma("sp", posq[:], cst["ret_posq"].t.ap(), reads=[cst["ret_posq"]], writes=[posq])
    c.dma("sp", posk[:], cst["ret_posk"].t.ap(), reads=[cst["ret_posk"]], writes=[posk])
    cd = c.sbuf([128, 8])
    c.op("act", lambda e: e.activation(out=cd[:], in_=lg[:], func=AF.Exp, scale=128.0), reads=[lg], writes=[cd])
    qT_h = c.sbuf([64, T], BF16); kT_h = c.sbuf([64, T], BF16)
    k_h = c.sbuf([128, NCH, 64], BF16); v_h = c.sbuf([128, NCH, 128], BF16); sg_h = c.sbuf([128, NCH, 128])
    qdf = c.sbuf([64, T], BF16); qdb = c.sbuf([64, T], BF16)
    kdf = c.sbuf([128, NCH, 64], BF16); kdb = c.sbuf([128, NCH, 64], BF16)
    Dm = c.sbuf([128, 128]); Dtmp = c.sbuf([128, 128]); Eq = c.sbuf([64, 2, 128]); ks = c.sbuf([128, 2])
    SB_all = c.sbuf([64, NCH, 128], BF16)
    Sb = c.sbuf([64, 128]); Sf = c.sbuf([64, 128]); Sf16 = c.sbuf([64, 128], BF16)
    ps_sc = Rot(c, 2, [128, 512], psum=True); ps_y = Rot(c, 2, [128, 512], psum=True)
    ps_kv = Rot(c, 2, [128, 512], psum=True); ps_tr = Rot(c, 2, [128, 512], psum=True)
    Prot = Rot(c, 2, [128, 128], BF16)
    orot = Rot(c, 3, [128, 128]); ssq = Rot(c, 2, [128, 1]); rs = Rot(c, 2, [128, 1]); junk = Rot(c, 2, [128, 128])
    oTst = Rot(c, 2, [128, 512])
    for hh in range(4):
        c.dma("sp", qT_h[:], qT_d[hh * 64:(hh + 1) * 64, :], reads=[qT_d], writes=[qT_h])
        c.dma("pool", kT_h[:], kT_d[hh * 64:(hh + 1) * 64, :], reads=[kT_d], writes=[kT_h])
        c.dma("sp", k_h[:], k_d.t.ap().rearrange("(n p) d -> p n d", p=128)[:, :, hh * 64:(hh + 1) * 64], reads=[k_d], writes=[k_h])
        c.dma("pool", v_h[:], v_d.t.ap().rearrange("(n p) d -> p n d", p=128)[:, :, hh * 128:(hh + 1) * 128], reads=[v_d], writes=[v_h])
        c.dma("sp", sg_h[:], sg_d.t.ap().rearrange("(n p) d -> p n d", p=128)[:, :, hh * 128:(hh + 1) * 128], reads=[sg_d], writes=[sg_h])
        lf = lg[:, hh:hh + 1]; lb = lg[:, 4 + hh:5 + hh]
        c.op("act", lambda e: e.activation(out=Dm[:], in_=pf[:], func=AF.Exp, scale=lf), reads=[pf, lg], writes=[Dm])
        c.op("act", lambda e: e.activation(out=Dtmp[:], in_=pb[:], func=AF.Exp, scale=lb), reads=[pb, lg], writes=[Dtmp])
        c.op("dve", lambda e: e.tensor_tensor(out=Dm[:], in0=Dm[:], in1=Dtmp[:], op=ALU.add), reads=[Dm, Dtmp], writes=[Dm])
        c.op("act", lambda e: e.activation(out=Eq[:, 0, :], in_=posq[0:64, 0, :], func=AF.Exp, scale=lg[0:64, hh:hh + 1]), reads=[posq, lg], writes=[Eq])
        c.op("act", lambda e: e.activation(out=Eq[:, 1, :], in_=posq[0:64, 1, :], func=AF.Exp, scale=lg[0:64, 4 + hh:5 + hh]), reads=[posq, lg], writes=[Eq])
        c.op("act", lambda e: e.activation(out=ks[:, 0:1], in_=posk[:, 0:1], func=AF.Exp, scale=lf), reads=[posk, lg], writes=[ks])
        c.op("act", lambda e: e.activation(out=ks[:, 1:2], in_=posk[:, 1:2], func=AF.Exp, scale=lb), reads=[posk, lg], writes=[ks])
        q3 = qT_h[:].rearrange("p (n c) -> p n c", c=128)
        c.op("dve", lambda e: e.tensor_tensor(out=qdf[:].rearrange("p (n c) -> p n c", c=128), in0=q3,
                                              in1=Eq[:, 0:1, :].broadcast_to([64, NCH, 128]), op=ALU.mult), reads=[qT_h, Eq], writes=[qdf])
        c.op("dve", lambda e: e.tensor_tensor(out=qdb[:].rearrange("p (n c) -> p n c", c=128), in0=q3,
                                              in1=Eq[:, 1:2, :].broadcast_to([64, NCH, 128]), op=ALU.mult), reads=[qT_h, Eq], writes=[qdb])
        c.op("dve", lambda e: e.tensor_scalar(out=kdf[:], in0=k_h[:], scalar1=ks[:, 0:1], scalar2=None, op0=ALU.mult), reads=[k_h, ks], writes=[kdf])
        c.op("dve", lambda e: e.tensor_scalar(out=kdb[:], in0=k_h[:], scalar1=ks[:, 1:2], scalar2=None, op0=ALU.mult), reads=[k_h, ks], writes=[kdb])
        c.op("dve", lambda e: e.memset(Sb[:], 0.0), writes=[Sb])
        for n in [1, 0] + list(range(NCH - 1, 1, -1)):
            c.op("act", lambda e: e.copy(out=SB_all[:, n, :], in_=Sb[:]), reads=[Sb], writes=[SB_all])
            kv = ps_kv.get()
            c.op("pe", lambda e: e.matmul(out=kv[0:64, 0:128], lhsT=kdb[:, n, :], rhs=v_h[:, n, :], start=True, stop=True),
                 reads=[kdb, v_h], writes=[kv])
            c.op("dve", lambda e: e.scalar_tensor_tensor(out=Sb[:], in0=Sb[:], scalar=cd[0:64, 4 + hh:5 + hh], in1=kv[0:64, 0:128],
                                                         op0=ALU.mult, op1=ALU.add), reads=[Sb, cd, kv], writes=[Sb])
        extra = list(extra)
        for _ in range(2 if hh < 2 else 1):
            if extra:
                extra.pop(0)()
        c.op("dve", lambda e: e.memset(Sf[:], 0.0), writes=[Sf])
        c.op("dve", lambda e: e.memset(Sf16[:], 0.0), writes=[Sf16])
        ost = None

        def emit_sc(n):
            ts_ = slice(n * 128, (n + 1) * 128)
            sc_ = ps_sc.get()
            c.op("pe", lambda e: e.matmul(out=sc_[:, 0:128], lhsT=kT_h[:, ts_], rhs=qT_h[:, ts_], start=True, stop=True),
                 reads=[kT_h, qT_h], writes=[sc_])
            return sc_

        def emit_tr(n, o):
            nonlocal ost
            tr = ps_tr.get()
            c.op("pe", lambda e: e.transpose(out=tr[:, 0:128], in_=o[:], identity=ident[:]), reads=[o, ident], writes=[tr])
            if n % 4 == 0:
                ost = oTst.get()
            j = n % 4
            c.op("act", lambda e: e.copy(out=ost[:, j * 128:(j + 1) * 128], in_=tr[:, 0:128]), reads=[tr], writes=[ost])
            if j == 3 or n == NCH - 1:
                n0 = n - j
                c.dma("sp", mixT_d[row0 + hh * 128:row0 + (hh + 1) * 128, n0 * 128:(n + 1) * 128], ost[:, 0:(j + 1) * 128],
                      reads=[ost], writes=[mixT_d])
        sc_next = emit_sc(0)
        o_prev = None
        for n in range(NCH):
            ts = slice(n * 128, (n + 1) * 128)
            sc = sc_next
            if n + 1 < NCH:
                sc_next = emit_sc(n + 1)
            P = Prot.get()
            c.op("dve", lambda e: e.tensor_tensor(out=P[:], in0=sc[:, 0:128], in1=Dm[:], op=ALU.mult), reads=[sc, Dm], writes=[P])
            y = ps_y.get()
            c.op("pe", lambda e: e.matmul(out=y[:, 0:128], lhsT=P[:], rhs=v_h[:, n, :], start=True, stop=False), reads=[P, v_h], writes=[y])
            c.op("pe", lambda e: e.matmul(out=y[:, 0:128], lhsT=qdf[:, ts], rhs=Sf16[:], start=False, stop=False), reads=[qdf, Sf16], writes=[y])
            c.op("pe", lambda e: e.matmul(out=y[:, 0:128], lhsT=qdb[:, ts], rhs=SB_all[:, n, :], start=False, stop=True), reads=[qdb, SB_all], writes=[y])
            kv = ps_kv.get()
            c.op("pe", lambda e: e.matmul(out=kv[0:64, 0:128], lhsT=kdf[:, n, :], rhs=v_h[:, n, :], start=True, stop=True),
                 reads=[kdf, v_h], writes=[kv])
            if o_prev is not None:
                emit_tr(n - 1, o_prev)
            c.op("dve", lambda e: e.scalar_tensor_tensor(out=Sf[:], in0=Sf[:], scalar=cd[0:64, hh:hh + 1], in1=kv[0:64, 0:128],
                                                         op0=ALU.mult, op1=ALU.add), reads=[Sf, cd, kv], writes=[Sf])
            c.op("act", lambda e: e.copy(out=Sf16[:], in_=Sf[:]), reads=[Sf], writes=[Sf16])
            sq = ssq.get(); jk = junk.get(); r_ = rs.get()
            c.op("act", lambda e: e.activation(out=jk[:], in_=y[:, 0:128], func=AF.Square, accum_out=sq[:]), reads=[y], writes=[jk, sq])
            c.op("act", lambda e: e.activation(out=sq[:], in_=sq[:], func=AF.Sqrt, scale=1.0 / 128, bias=EPS), reads=[sq], writes=[sq])
            c.op("dve", lambda e: e.reciprocal(out=r_[:], in_=sq[:]), reads=[sq], writes=[r_])
            o = orot.get()
            c.op("dve", lambda e: e.scalar_tensor_tensor(out=o[:], in0=y[:, 0:128], scalar=r_[:], in1=sg_h[:, n, :],
                                                         op0=ALU.mult, op1=ALU.mult), reads=[y, r_, sg_h], writes=[o])
            o_prev = o
        emit_tr(NCH - 1, o_prev)
    c.pop()


TWO_PI = 2.0 * math.pi
MAGIC = 12582912.0


def emit_sin(c, out_ap, arg_ap, shape, tmps, reads, writes):
    t, k = tmps
    sl = tuple(slice(0, s) for s in shape)
    c.op("dve", lambda e: e.tensor_scalar(out=t[sl], in0=arg_ap, scalar1=1.0 / TWO_PI, scalar2=MAGIC, op0=ALU.mult, op1=ALU.add),
         reads=reads, writes=[t])
    c.op("dve", lambda e: e.tensor_scalar(out=k[sl], in0=t[sl], scalar1=MAGIC, scalar2=-TWO_PI, op0=ALU.subtract, op1=ALU.mult),
         reads=[t], writes=[k])
    c.op("dve", lambda e: e.tensor_tensor(out=k[sl], in0=k[sl], in1=arg_ap, op=ALU.add), reads=[k] + list(reads), writes=[k])
    c.op("dve", lambda e: e.tensor_scalar(out=k[sl], in0=k[sl], scalar1=math.pi, scalar2=-math.pi, op0=ALU.min, op1=ALU.max),
         reads=[k], writes=[k])
    c.op("act", lambda e: e.activation(out=out_ap, in_=k[sl], func=AF.Sin), reads=[k], writes=writes)


def s5_consts():
    j = np.arange(128, dtype=np.float32)
    pos = np.stack([np.broadcast_to(j + 1, (128, 128)), np.broadcast_to(128 - j, (128, 128))], 1)
    return {"s5_pos": np.ascontiguousarray(pos).astype(np.float32)}


def s5_host(inp, l, hf):
    G0 = 16 * hf
    Bb = np.zeros([128, 2, 8, 2, 128], np.float32)
    Cb = np.zeros([128, 2, 8, 2, 128], np.float32)
    lam = np.zeros([128, 2, 16], np.float32)
    lst = np.zeros([128, 16], np.float32)
    for d in range(2):
        for pt in range(8):
            for g2 in range(2):
                gl = 2 * pt + g2
                G = G0 + gl
                r0 = (gl % 8) * 16
                Bb[r0:r0 + 16, d, pt, 0, g2 * 64:(g2 + 1) * 64] = inp["s5_b_re"][l, d, G].T
                Bb[r0:r0 + 16, d, pt, 1, g2 * 64:(g2 + 1) * 64] = inp["s5_b_im"][l, d, G].T
                Cb[g2 * 64:(g2 + 1) * 64, d, pt, 0, r0:r0 + 16] = inp["s5_c_re"][l, d, G].T
                Cb[g2 * 64:(g2 + 1) * 64, d, pt, 1, r0:r0 + 16] = inp["s5_c_im"][l, d, G].T
                lam[g2 * 64:(g2 + 1) * 64, 0, d * 8 + pt] = inp["s5_lam_re"][l, d, G]
                lam[g2 * 64:(g2 + 1) * 64, 1, d * 8 + pt] = inp["s5_lam_im"][l, d, G]
                lst[g2 * 64:(g2 + 1) * 64, d * 8 + pt] = inp["s5_log_step"][l, d, G]
    dT = np.ascontiguousarray(inp["s5_d"][l][256 * hf:256 * hf + 256].reshape(2, 128).T)
    return {"s5_B": Bb, "s5_C": Cb, "s5_lam": lam, "s5_lst": lst, "s5_dT": dT}


def emit_s5(c, uT_d, w, mixT_d, row0):
    c.push()
    Bb = c.sbuf([128, 2, 8, 2, 128]); Cb = c.sbuf([128, 2, 8, 2, 128])
    c.dma("sp", Bb[:], w["s5_B"].t.ap(), reads=[w["s5_B"]], writes=[Bb])
    c.dma("pool", Cb[:], w["s5_C"].t.ap(), reads=[w["s5_C"]], writes=[Cb])
    c.op("dve", lambda e: e.tensor_scalar(out=Cb[:, :, :, 1, :], in0=Cb[:, :, :, 1, :], scalar1=-1.0, scalar2=None, op0=ALU.mult),
         reads=[Cb], writes=[Cb])
    lam = c.sbuf([128, 2, 16]); lst = c.sbuf([128, 16]); dT = c.sbuf([128, 2]); pos = c.sbuf([128, 2, 128])
    c.dma("sp", lam[:], w["s5_lam"].t.ap(), reads=[w["s5_lam"]], writes=[lam])
    c.dma("sp", lst[:], w["s5_lst"].t.ap(), reads=[w["s5_lst"]], writes=[lst])
    c.dma("sp", dT[:], w["s5_dT"].t.ap(), reads=[w["s5_dT"]], writes=[dT])
    c.dma("sp", pos[:], w["s5_pos"].t.ap(), reads=[w["s5_pos"]], writes=[pos])
    P = {n: c.sbuf([128, 16]) for n in ["st", "lr", "ex", "rho", "th", "th2", "sn", "cs", "nr", "ni", "den", "cr", "ci", "t1", "t2"]}
    tm16 = (c.sbuf([128, 16]), c.sbuf([128, 16]))

    def tt(o, a, b, op):
        c.op("dve", lambda e: e.tensor_tensor(out=P[o][:], in0=P[a][:] if isinstance(a, str) else a, in1=P[b][:] if isinstance(b, str) else b, op=op),
             reads=[P[a] if isinstance(a, str) else lam, P[b] if isinstance(b, str) else lam], writes=[P[o]])
    c.op("act", lambda e: e.activation(out=P["st"][:], in_=lst[:], func=AF.Exp), reads=[lst], writes=[P["st"]])
    c.op("dve", lambda e: e.tensor_scalar(out=P["lr"][:], in0=lam[:, 0, :], scalar1=-1e-4, scalar2=None, op0=ALU.min), reads=[lam], writes=[P["lr"]])
    tt("ex", "lr", "st", ALU.mult)
    c.op("act", lambda e: e.activation(out=P["rho"][:], in_=P["ex"][:], func=AF.Exp), reads=[P["ex"]], writes=[P["rho"]])
    tt("th", lam[:, 1, :], "st", ALU.mult)
    c.op("dve", lambda e: e.tensor_scalar(out=P["th2"][:], in0=P["th"][:], scalar1=math.pi / 2, scalar2=None, op0=ALU.add), reads=[P["th"]], writes=[P["th2"]])
    emit_sin(c, P["sn"][:], P["th"][:], [128, 16], tm16, [P["th"]], [P["sn"]])
    emit_sin(c, P["cs"][:], P["th2"][:], [128, 16], tm16, [P["th2"]], [P["cs"]])
    tt("nr", "rho", "cs", ALU.mult)
    c.op("dve", lambda e: e.tensor_scalar(out=P["nr"][:], in0=P["nr"][:], scalar1=-1.0, scalar2=None, op0=ALU.add), reads=[P["nr"]], writes=[P["nr"]])
    tt("ni", "rho", "sn", ALU.mult)
    tt("t1", "lr", "lr", ALU.mult)
    tt("t2", lam[:, 1, :], lam[:, 1, :], ALU.mult)
    tt("den", "t1", "t2", ALU.add)
    c.op("dve", lambda e: e.reciprocal(out=P["den"][:], in_=P["den"][:]), reads=[P["den"]], writes=[P["den"]])
    tt("t1", "nr", "lr", ALU.mult); tt("t2", "ni", lam[:, 1, :], ALU.mult); tt("cr", "t1", "t2", ALU.add); tt("cr", "cr", "den", ALU.mult)
    tt("t1", "ni", "lr", ALU.mult); tt("t2", "nr", lam[:, 1, :], ALU.mult); tt("ci", "t1", "t2", ALU.subtract); tt("ci", "ci", "den", ALU.mult)
    Tab = c.sbuf([128, 16, 4, 128])
    arg = c.sbuf([128, 128]); arg2 = c.sbuf([128, 128]); tm = (c.sbuf([128, 128]), c.sbuf([128, 128])); ta = c.sbuf([128, 128]); tb = c.sbuf([128, 128])
    for q in range(16):
        d = q // 8
        c.op("dve", lambda e: e.tensor_scalar(out=arg[:], in0=pos[:, d, :], scalar1=P["th"][:, q:q + 1], scalar2=None, op0=ALU.mult), reads=[pos, P["th"]], writes=[arg])
        c.op("dve", lambda e: e.tensor_scalar(out=arg2[:], in0=arg[:], scalar1=math.pi / 2, scalar2=None, op0=ALU.add), reads=[arg], writes=[arg2])
        emit_sin(c, Tab[:, q, 1, :], arg[:], [128, 128], tm, [arg], [Tab])
        emit_sin(c, Tab[:, q, 0, :], arg2[:], [128, 128], tm, [arg2], [Tab])
        c.op("dve", lambda e: e.tensor_scalar(out=ta[:], in0=Tab[:, q, 1, :], scalar1=P["ci"][:, q:q + 1], scalar2=None, op0=ALU.mult), reads=[Tab, P["ci"]], writes=[ta])
        c.op("dve", lambda e: e.scalar_tensor_tensor(out=Tab[:, q, 2, :], in0=Tab[:, q, 0, :], scalar=P["cr"][:, q:q + 1], in1=ta[:], op0=ALU.mult, op1=ALU.add), reads=[Tab, P["cr"], ta], writes=[Tab])
        c.op("dve", lambda e: e.tensor_scalar(out=tb[:], in0=Tab[:, q, 1, :], scalar1=P["cr"][:, q:q + 1], scalar2=None, op0=ALU.mult), reads=[Tab, P["cr"]], writes=[tb])
        c.op("dve", lambda e: e.scalar_tensor_tensor(out=Tab[:, q, 3, :], in0=Tab[:, q, 0, :], scalar=P["ci"][:, q:q + 1], in1=tb[:], op0=ALU.mult, op1=ALU.subtract), reads=[Tab, P["ci"], tb], writes=[Tab])
    CL = c.sbuf([128, 16, 3])
    for q in range(16):
        Lq = 127 if q < 8 else 0
        c.op("dve", lambda e: e.tensor_copy(out=CL[:, q, 0:2], in_=Tab[:, q, 0:2, Lq]), reads=[Tab], writes=[CL])
    c.op("dve", lambda e: e.tensor_scalar(out=CL[:, :, 2], in0=CL[:, :, 1], scalar1=-1.0, scalar2=None, op0=ALU.mult), reads=[CL], writes=[CL])
    u_sb = c.sbuf([128, 2, T])
    c.dma("sp", u_sb[:, 0, :], uT_d[0:128, :], reads=[uT_d], writes=[u_sb])
    c.dma("pool", u_sb[:, 1, :], uT_d[128:256, :], reads=[uT_d], writes=[u_sb])
    yb = c.sbuf([128, 2, T])
    xrot = Rot(c, 2, [128, 8, 2, 128])
    ps_b = Rot(c, 3, [128, 4, 2, 128], psum=True)
    ps_y = Rot(c, 2, [128, 512], psum=True)
    r1 = Rot(c, 3, [128, 4, 128]); r2 = Rot(c, 3, [128, 4, 128]); rin = Rot(c, 2, [128, 4, 2, 128]); wv = Rot(c, 3, [128, 4, 2, 128])
    p1 = Rot(c, 3, [128, 4, 128]); p2 = Rot(c, 3, [128, 4, 128])
    zst = Rot(c, 2, [128, 512]); ytmp = Rot(c, 2, [128, 128])
    carr = Rot(c, 2, [128, 8, 2]); ctm = Rot(c, 4, [128, 4])
    def emit_bu(d, n, g4):
        ts = slice(n * 128, (n + 1) * 128)
        pb = ps_b.get()
        for p4 in range(4):
            pt = g4 * 4 + p4
            kc = pt // 4
            for r_ in range(2):
                for h_ in range(2):
                    hs_ = slice(64 * h_, 64 * h_ + 64)
                    c.op("pe", lambda e: e.matmul(out=pb[hs_, p4, r_, :], lhsT=Bb[:, d, pt, r_, hs_], rhs=u_sb[:, kc, ts], start=True, stop=True), reads=[Bb, u_sb], writes=[pb])
        return pb

    for d, order in ((1, [1, 0] + list(range(NCH - 1, 1, -1))), (0, list(range(NCH)))):
        xprev = None
        car = None
        zcur = [None, None]
        steps = [(n, g4) for n in order for g4 in range(2)]
        pbs = {0: emit_bu(d, steps[0][0], steps[0][1])}
        for si, (n, g4) in enumerate(steps):
            ts = slice(n * 128, (n + 1) * 128)
            if si + 1 < len(steps):
                pbs[si + 1] = emit_bu(d, steps[si + 1][0], steps[si + 1][1])
            pb = pbs.pop(si)
            if g4 == 0:
                x = xrot.get()
                carn = carr.get()
            if True:
                q0 = d * 8 + g4 * 4
                a1 = r1.get(); a2 = r2.get(); ri = rin.get(); w_ = wv.get()
                Tr = Tab[:, q0:q0 + 4, 2, :]; Ti = Tab[:, q0:q0 + 4, 3, :]
                Br = pb[:, :, 0, :]; Bi = pb[:, :, 1, :]
                c.op("dve", lambda e: e.tensor_tensor(out=a1[:], in0=Br, in1=Tr, op=ALU.mult), reads=[pb, Tab], writes=[a1])
                c.op("dve", lambda e: e.tensor_tensor(out=a2[:], in0=Bi, in1=Ti, op=ALU.mult), reads=[pb, Tab], writes=[a2])
                c.op("dve", lambda e: e.tensor_tensor(out=ri[:, :, 0, :], in0=a1[:], in1=a2[:], op=ALU.subtract), reads=[a1, a2], writes=[ri])
                a1 = r1.get(); a2 = r2.get()
                c.op("dve", lambda e: e.tensor_tensor(out=a1[:], in0=Bi, in1=Tr, op=ALU.mult), reads=[pb, Tab], writes=[a1])
                c.op("dve", lambda e: e.tensor_tensor(out=a2[:], in0=Br, in1=Ti, op=ALU.mult), reads=[pb, Tab], writes=[a2])
                c.op("dve", lambda e: e.tensor_tensor(out=ri[:, :, 1, :], in0=a1[:], in1=a2[:], op=ALU.add), reads=[a1, a2], writes=[ri])
                for p4 in range(4):
                    pt = g4 * 4 + p4
                    q = d * 8 + pt
                    rho_b = P["rho"][:, q:q + 1].broadcast_to([128, 128])
                    for ci_ in range(2):
                        if car is None:
                            init = 0.0
                            rd = [ri, P["rho"]]
                        else:
                            init = car[:, pt, ci_:ci_ + 1]
                            rd = [ri, P["rho"], car]
                        if d == 0:
                            c.op("dve", lambda e: e.tensor_tensor_scan(out=w_[:, p4, ci_, :], data0=rho_b, data1=ri[:, p4, ci_, :], initial=init, op0=ALU.mult, op1=ALU.add), reads=rd, writes=[w_])
                        else:
                            c.op("dve", lambda e: e.tensor_tensor_scan(out=w_[:, p4, ci_, ::-1], data0=rho_b, data1=ri[:, p4, ci_, ::-1], initial=init, op0=ALU.mult, op1=ALU.add), reads=rd, writes=[w_])
                Cj = Tab[:, q0:q0 + 4, 0, :]; Sj = Tab[:, q0:q0 + 4, 1, :]
                wr = w_[:, :, 0, :]; wi = w_[:, :, 1, :]
                xs = x[:, g4 * 4:g4 * 4 + 4, :, :]
                Lc = 127 if d == 0 else 0
                for p4 in range(4):
                    pt = g4 * 4 + p4
                    q = d * 8 + pt
                    wrL = w_[:, p4, 0, Lc:Lc + 1]; wiL = w_[:, p4, 1, Lc:Lc + 1]
                    k1 = ctm.get(); k2 = ctm.get()
                    c.op("act", lambda e: e.activation(out=k1[:, 0:1], in_=wiL, func=AF.Copy, scale=CL[:, q, 2:3]), reads=[w_, CL], writes=[k1])
                    c.op("act", lambda e: e.activation(out=carn[:, pt, 0:1], in_=wrL, func=AF.Identity, scale=CL[:, q, 0:1], bias=k1[:, 0:1]), reads=[w_, CL, k1], writes=[carn])
                    c.op("act", lambda e: e.activation(out=k2[:, 0:1], in_=wiL, func=AF.Copy, scale=CL[:, q, 0:1]), reads=[w_, CL], writes=[k2])
                    c.op("act", lambda e: e.activation(out=carn[:, pt, 1:2], in_=wrL, func=AF.Identity, scale=CL[:, q, 1:2], bias=k2[:, 0:1]), reads=[w_, CL, k2], writes=[carn])
                b1 = p1.get(); b2 = p2.get()
                c.op("pool", lambda e: e.tensor_tensor(out=b1[:], in0=wr, in1=Cj, op=ALU.mult), reads=[w_, Tab], writes=[b1])
                c.op("pool", lambda e: e.tensor_tensor(out=b2[:], in0=wi, in1=Sj, op=ALU.mult), reads=[w_, Tab], writes=[b2])
                c.op("pool", lambda e: e.tensor_tensor(out=xs[:, :, 0, :], in0=b1[:], in1=b2[:], op=ALU.subtract), reads=[b1, b2], writes=[x])
                b1 = p1.get(); b2 = p2.get()
                c.op("pool", lambda e: e.tensor_tensor(out=b1[:], in0=wr, in1=Sj, op=ALU.mult), reads=[w_, Tab], writes=[b1])
                c.op("pool", lambda e: e.tensor_tensor(out=b2[:], in0=wi, in1=Cj, op=ALU.mult), reads=[w_, Tab], writes=[b2])
                c.op("pool", lambda e: e.tensor_tensor(out=xs[:, :, 1, :], in0=b1[:], in1=b2[:], op=ALU.add), reads=[b1, b2], writes=[x])
            if g4 == 0:
                continue
            xprev = x
            car = carn
            for ot in range(2):
                py = ps_y.get()
                for i_, pt in enumerate(range(4 * ot, 4 * ot + 4)):
                    for h_ in range(2):
                        hs_ = slice(64 * h_, 64 * h_ + 64)
                        c.op("pe", lambda e: e.matmul(out=py[hs_, 0:128], lhsT=Cb[:, d, pt, 0, hs_], rhs=x[:, pt, 0, :], start=(i_ == 0), stop=False), reads=[Cb, x], writes=[py])
                        c.op("pe", lambda e: e.matmul(out=py[hs_, 0:128], lhsT=Cb[:, d, pt, 1, hs_], rhs=x[:, pt, 1, :], start=False, stop=(i_ == 3)), reads=[Cb, x], writes=[py])
                if d == 1:
                    c.op("act", lambda e: e.copy(out=yb[:, ot, ts], in_=py[:, 0:128]), reads=[py], writes=[yb])
                else:
                    yt = ytmp.get()
                    c.op("dve", lambda e: e.scalar_tensor_tensor(out=yt[:], in0=u_sb[:, ot, ts], scalar=dT[:, ot:ot + 1], in1=py[:, 0:128], op0=ALU.mult, op1=ALU.add), reads=[u_sb, dT, py], writes=[yt])
                    c.op("pool", lambda e: e.tensor_tensor(out=yt[:], in0=yt[:], in1=yb[:, ot, ts], op=ALU.add), reads=[yt, yb], writes=[yt])
                    j = n % 4
                    if j == 0 or zcur[ot] is None:
                        zcur[ot] = zst.get()
                    zz = zcur[ot]
                    c.op("act", lambda e: e.activation(out=zz[:, j * 128:(j + 1) * 128], in_=yt[:], func=AF.Gelu_apprx_tanh), reads=[yt], writes=[zz])
                    if j == 3 or n == NCH - 1:
                        n0 = n - j
                        c.dma("sp", mixT_d[row0 + ot * 128:row0 + (ot + 1) * 128, n0 * 128:(n + 1) * 128], zz[:, 0:(j + 1) * 128], reads=[zz], writes=[mixT_d])
    c.pop()


CG = 32


def hy_tables(L):
    N = 2 * L
    N1 = N // 128
    F1 = N1 // 2 + 1
    T1 = L // 128
    t1 = np.arange(N1, dtype=np.float64)[:, None]
    f1 = np.arange(F1, dtype=np.float64)[None, :]
    W1 = np.concatenate([np.cos(2 * np.pi * t1 * f1 / N1), -np.sin(2 * np.pi * t1 * f1 / N1)], 1)
    a = np.arange(128, dtype=np.float64)
    ang = 2 * np.pi * a[:, None, None] * (np.arange(F1)[None, :, None] + N1 * a[None, None, :]) / N
    TB = np.stack([np.cos(ang), np.sin(ang)], 2)
    angT = ang.transpose(2, 1, 0)
    TBT = np.stack([np.cos(angT), np.sin(angT)], 2)
    wf = np.full(F1, 2.0); wf[0] = 1.0; wf[F1 - 1] = 1.0
    tt = np.arange(T1, dtype=np.float64)[None, :]
    ff = np.arange(F1, dtype=np.float64)[:, None]
    Minv = np.concatenate([wf[:, None] / N * np.cos(2 * np.pi * ff * tt / N1), -wf[:, None] / N * np.sin(2 * np.pi * ff * tt / N1)], 0)
    idx = np.arange(L, dtype=np.float32)
    tpos = (idx / np.float32(max(L - 1, 1))).astype(np.float32)
    bands = np.linspace(1e-4, 15.0, 16, dtype=np.float32)
    angp = ((np.float32(2.0 * math.pi) * idx / np.float32(L))[:, None] * bands[None]).astype(np.float32)
    z = np.concatenate([tpos[:, None], np.cos(angp), -np.sin(angp)], -1).astype(np.float32)
    dmap = np.concatenate([np.arange(L), [0], np.arange(L - 1, 0, -1)])
    zT = np.ascontiguousarray(z[dmap].T)
    tn = tpos[dmap].astype(np.float32).copy()
    tn[L] = 1e9
    s = f"hy{L}_"
    out = {s + "W1": W1.astype(np.float32), s + "Minv": Minv.astype(np.float32), s + "zT": zT, s + "tn": tn}
    for nm, tab in (("TB", TB), ("TBT", TBT)):
        t32 = tab.astype(np.float32)
        hi = t32.astype(ml_dtypes.bfloat16)
        lo = (t32 - hi.astype(np.float32)).astype(ml_dtypes.bfloat16)
        out[s + nm + "h"] = hi
        out[s + nm + "l"] = lo
    return out


def hy_host(inp, l, hf):
    cols = np.concatenate([o + 256 * hf + np.arange(256) for o in (0, 512, 1024)])
    cw = inp["hy_conv_w"][l][:, cols]
    cwT = np.ascontiguousarray(cw.reshape(3, 6, 128).transpose(2, 1, 0))
    cbT = np.ascontiguousarray(inp["hy_conv_b"][l][cols].reshape(6, 128).T)
    w3 = inp["hy_w3"][l].reshape(64, 2, 2, 512)[:, :, :, 256 * hf:256 * hf + 256]
    deltas = np.abs(np.linspace(math.log(1e-2) / 1.5, math.log(1e-2) / 0.3, 512, dtype=np.float32))
    nd = -deltas[256 * hf:256 * hf + 256]
    return {"hy_cwT": cwT, "hy_cbT": cbT, "hy_w1": inp["hy_w1"][l], "hy_b1": inp["hy_b1"][l].reshape(64, 1),
            "hy_w2": inp["hy_w2"][l], "hy_b2": inp["hy_b2"][l].reshape(64, 1), "hy_w3": np.ascontiguousarray(w3),
            "hy_freqT": np.ascontiguousarray(inp["hy_freq"][l].T), "hy_ndT": np.ascontiguousarray(nd.reshape(2, 128).T),
            "hy_bias": np.ascontiguousarray(inp["hy_bias"][l][:, 256 * hf:256 * hf + 256])}


def hy_shortconv_thunks(c, hyT_d, w, hyc_d):
    cw = c.sbuf([128, 6, 3]); cb = c.sbuf([128, 6])
    c.dma("sp", cw[:], w["hy_cwT"].t.ap(), reads=[w["hy_cwT"]], writes=[cw])
    c.dma("sp", cb[:], w["hy_cbT"].t.ap(), reads=[w["hy_cbT"]], writes=[cb])
    xr = Rot(c, 2, [128, T]); pr = Rot(c, 2, [128, T])

    def mk(rt):
        def run():
            x = xr.get(); p = pr.get()
            c.dma(("sp", "pool")[rt % 2], x[:], hyT_d[rt * 128:(rt + 1) * 128, :], reads=[hyT_d], writes=[x])
            for (a, b) in ((0, CTX), (CTX, T)):
                c.op("dve", lambda e: e.tensor_scalar(out=p[:, a:b], in0=x[:, a:b], scalar1=cw[:, rt, 1:2], scalar2=cb[:, rt:rt + 1], op0=ALU.mult, op1=ALU.add),
                     reads=[x, cw, cb], writes=[p])
                c.op("dve", lambda e: e.scalar_tensor_tensor(out=p[:, a + 1:b], in0=x[:, a:b - 1], scalar=cw[:, rt, 0:1], in1=p[:, a + 1:b], op0=ALU.mult, op1=ALU.add),
                     reads=[x, cw, p], writes=[p])
                c.op("dve", lambda e: e.scalar_tensor_tensor(out=p[:, a:b - 1], in0=x[:, a + 1:b], scalar=cw[:, rt, 2:3], in1=p[:, a:b - 1], op0=ALU.mult, op1=ALU.add),
                     reads=[x, cw, p], writes=[p])
            c.dma(("sp", "pool")[rt % 2], hyc_d[rt * 128:(rt + 1) * 128, :], p[:], reads=[p], writes=[hyc_d])
        return run
    return [mk(rt) for rt in range(6)]


def emit_hy_shortconv(c, hyT_d, w, hyc_d):
    c.push()
    for th in hy_shortconv_thunks(c, hyT_d, w, hyc_d):
        th()
    c.pop()


def emit_hy_filters(c, L, w, taps_d):
    N = 2 * L
    CH = min(512, L)
    nch = N // CH
    s = f"hy{L}_"
    c.push()
    w1 = c.sbuf([33, 64]); w2 = c.sbuf([64, 64]); b1 = c.sbuf([64, 1]); b2 = c.sbuf([64, 1]); fq = c.sbuf([64, 2]); fb = c.sbuf([64, 2])
    w3 = c.sbuf([64, 2, 2, 256]); nd = c.sbuf([128, 2])
    for (dst, nm) in ((w1, "hy_w1"), (w2, "hy_w2"), (b1, "hy_b1"), (b2, "hy_b2"), (fq, "hy_freqT"), (w3, "hy_w3"), (nd, "hy_ndT")):
        c.dma("sp", dst[:], w[nm].t.ap(), reads=[w[nm]], writes=[dst])
    c.op("dve", lambda e: e.tensor_tensor(out=fb[:, 0:1], in0=fq[:, 0:1], in1=b1[:], op=ALU.mult), reads=[fq, b1], writes=[fb])
    c.op("dve", lambda e: e.tensor_tensor(out=fb[:, 1:2], in0=fq[:, 1:2], in1=b2[:], op=ALU.mult), reads=[fq, b2], writes=[fb])
    zT = c.sbuf([33, N]); h2T = c.sbuf([64, N]); tnb = c.sbuf([128, N])
    c.dma("sp", zT[:], w[s + "zT"].t.ap(), reads=[w[s + "zT"]], writes=[zT])
    c.dma("pool", tnb[:], w[s + "tn"].t.ap().partition_broadcast(128), reads=[w[s + "tn"]], writes=[tnb])
    ps = Rot(c, 3, [128, 512], psum=True)
    argr = Rot(c, 2, [64, CH]); h1r = Rot(c, 2, [64, CH]); tm = (c.sbuf([64, CH]), c.sbuf([64, CH]))
    for ch in range(nch):
        sl = slice(ch * CH, (ch + 1) * CH)
        p = ps.get()
        c.op("pe", lambda e: e.matmul(out=p[0:64, 0:CH], lhsT=w1[:], rhs=zT[:, sl], start=True, stop=True), reads=[w1, zT], writes=[p])
        ar = argr.get()
        c.op("act", lambda e: e.activation(out=ar[:], in_=p[0:64, 0:CH], func=AF.Identity, scale=fq[:, 0:1], bias=fb[:, 0:1]), reads=[p, fq, fb], writes=[ar])
        h1 = h1r.get()
        emit_sin(c, h1[:], ar[:], [64, CH], tm, [ar], [h1])
        p = ps.get()
        c.op("pe", lambda e: e.matmul(out=p[0:64, 0:CH], lhsT=w2[:], rhs=h1[:], start=True, stop=True), reads=[w2, h1], writes=[p])
        ar = argr.get()
        c.op("act", lambda e: e.activation(out=ar[:], in_=p[0:64, 0:CH], func=AF.Identity, scale=fq[:, 1:2], bias=fb[:, 1:2]), reads=[p, fq, fb], writes=[ar])
        emit_sin(c, h2T[:, sl], ar[:], [64, CH], tm, [ar], [h2T])
    tp = [c.sbuf([128, N]), c.sbuf([128, N])]
    acc = c.sbuf([128, 2, nch]); tot = c.sbuf([128, 2]); decr = Rot(c, 2, [128, CH])
    for ct in range(2):
        for ch in range(nch):
            sl = slice(ch * CH, (ch + 1) * CH)
            dr = 0 if ch * CH < L else 1
            dec = decr.get()
            c.op("act", lambda e: e.activation(out=dec[:], in_=tnb[:, sl], func=AF.Exp, scale=nd[:, ct:ct + 1]), reads=[tnb, nd], writes=[dec])
            for o in range(2):
                p = ps.get()
                c.op("pe", lambda e: e.matmul(out=p[:, 0:CH], lhsT=w3[:, dr, o, ct * 128:(ct + 1) * 128], rhs=h2T[:, sl], start=True, stop=True), reads=[w3, h2T], writes=[p])
                c.op("dve", lambda e: e.tensor_tensor(out=tp[o][:, sl], in0=p[:, 0:CH], in1=dec[:], op=ALU.mult), reads=[p, dec], writes=[tp[o]])
                c.op("dve", lambda e: e.tensor_reduce(out=acc[:, o, ch:ch + 1], in_=tp[o][:, sl], axis=AX.X, op=ALU.add, apply_absolute_value=True), reads=[tp[o]], writes=[acc])
        for o in range(2):
            c.op("dve", lambda e: e.tensor_reduce(out=tot[:, o:o + 1], in_=acc[:, o, :], axis=AX.X, op=ALU.add), reads=[acc], writes=[tot])
        c.op("dve", lambda e: e.reciprocal(out=tot[:], in_=tot[:]), reads=[tot], writes=[tot])
        for o in range(2):
            c.op(("dve", "pool")[o], lambda e: e.tensor_scalar(out=tp[o][:], in0=tp[o][:], scalar1=tot[:, o:o + 1], scalar2=1.0, op0=ALU.mult, op1=ALU.mult), reads=[tp[o], tot], writes=[tp[o]])
            c.dma(("sp", "pool")[o], taps_d[o, ct * 128:(ct + 1) * 128, :], tp[o][:], reads=[tp[o]], writes=[taps_d])
    c.pop()


def emit_hy_conv(c, L, tok0, w, hyc_d, taps_d, ident, mixT_d, row0):
    N = 2 * L
    N1 = N // 128
    F1 = N1 // 2 + 1
    T1 = L // 128
    F2 = 2 * F1
    NPB = 512 // (2 * CG)
    NPA = 512 // F2
    s = f"hy{L}_"
    c.push()
    W1 = c.sbuf([N1, F2]); Minv = c.sbuf([F2, T1])
    TBh = c.sbuf([128, F1, 2, 128], BF16); TBl = c.sbuf([128, F1, 2, 128], BF16)
    TBTh = c.sbuf([128, F1, 2, 128], BF16); TBTl = c.sbuf([128, F1, 2, 128], BF16)
    c.dma("sp", W1[:], w[s + "W1"].t.ap(), reads=[w[s + "W1"]], writes=[W1])
    for (dst_, nm_, q_) in ((TBh, "TBh", "sp"), (TBl, "TBl", "pool"), (TBTh, "TBTh", "sp"), (TBTl, "TBTl", "pool")):
        c.dma(q_, dst_[:], w[s + nm_].t.ap(), reads=[w[s + nm_]], writes=[dst_])
    c.dma("sp", Minv[:], w[s + "Minv"].t.ap(), reads=[w[s + "Minv"]], writes=[Minv])
    bias = c.sbuf([T1, 2, 256])
    c.dma("sp", bias[:], w["hy_bias"].t.ap().partition_broadcast(T1), reads=[w["hy_bias"]], writes=[bias])
    xp = Rot(c, 3, [N1, CG, 128])
    Ah = c.sbuf([128, F1, 3, CG], BF16); Al = c.sbuf([128, F1, 3, CG], BF16)
    Ksp = [c.sbuf([128, F1, 2, CG]), c.sbuf([128, F1, 2, CG])]
    Yh = c.sbuf([128, F1, 3, CG], BF16); Yl = c.sbuf([128, F1, 3, CG], BF16)
    y32 = Rot(c, 2, [128, 2, NPB, CG])
    G = c.sbuf([128, CG, F2])
    GT = c.sbuf([F2, CG, 128])
    tmp = c.sbuf([T1, 512]); tmq = Rot(c, 2, [128, NPB, CG]); tmr = Rot(c, 2, [128, NPB, CG])
    psA = Rot(c, 2, [128, 512], psum=True); psB = Rot(c, 3, [128, 512], psum=True); psT = Rot(c, 2, [128, 512], psum=True)
    evi = [0]

    def evac(out, in_, rd, wr, scale=None):
        evi[0] += 1
        if scale is not None:
            c.op("act", lambda e: e.activation(out=out, in_=in_, func=AF.Copy, scale=scale), reads=rd, writes=wr)
        elif evi[0] % 2 == 0:
            c.op("act", lambda e: e.copy(out=out, in_=in_), reads=rd, writes=wr)
        else:
            c.op("dve", lambda e: e.tensor_copy(out=out, in_=in_), reads=rd, writes=wr)

    def fwd(X, R, sink):
        for c0 in range(0, CG, NPA):
            n = min(NPA, CG - c0)
            pa = psA.get()
            for ci in range(n):
                for h_ in range(2):
                    c.op("pe", lambda e: e.matmul(out=pa[64 * h_:64 * h_ + 64, ci * F2:(ci + 1) * F2], lhsT=X[0:R, c0 + ci, 64 * h_:64 * h_ + 64], rhs=W1[0:R, :], start=True, stop=True),
                         reads=[X, W1], writes=[pa])
            src = pa[:, 0:n * F2].rearrange("p (c r f) -> p c r f", c=n, r=2)
            hv = Ah[:, :, 0:2, c0:c0 + n].rearrange("p f r c -> p c r f")
            lv = Al[:, :, 0:2, c0:c0 + n].rearrange("p f r c -> p c r f")
            c.op("act", lambda e: e.copy(out=hv, in_=src), reads=[pa], writes=[Ah])
            c.op("dve", lambda e: e.tensor_tensor(out=lv, in0=src, in1=hv, op=ALU.subtract), reads=[pa, Ah], writes=[Al])
            c.op("act", lambda e: e.activation(out=Ah[:, :, 2, c0:c0 + n], in_=Ah[:, :, 0, c0:c0 + n], func=AF.Copy, scale=-1.0), reads=[Ah], writes=[Ah])
            c.op("pool", lambda e: e.tensor_scalar(out=Al[:, :, 2, c0:c0 + n], in0=Al[:, :, 0, c0:c0 + n], scalar1=-1.0, scalar2=1.0, op0=ALU.mult, op1=ALU.mult), reads=[Al], writes=[Al])
        for f0 in range(0, F1, NPB):
            nf = min(NPB, F1 - f0)
            pb = psB.get()
            for fi in range(nf):
                f = f0 + fi
                o_ = pb[:, fi * 2 * CG:(fi + 1) * 2 * CG]
                seq = [(TBh, 0, Ah, slice(0, 2)), (TBh, 0, Al, slice(0, 2)), (TBl, 0, Ah, slice(0, 2)),
                       (TBh, 1, Ah, slice(1, 3)), (TBh, 1, Al, slice(1, 3)), (TBl, 1, Ah, slice(1, 3))]
                for k_, (tb_, cs_, dat_, sl_) in enumerate(seq):
                    c.op("pe", lambda e: e.matmul(out=o_, lhsT=tb_[:, f, cs_, :], rhs=dat_[:, f, sl_, :].rearrange("p r c -> p (r c)"), start=(k_ == 0), stop=(k_ == 5)),
                         reads=[tb_, dat_], writes=[pb])
            sink(pb, f0, nf)

    def inverse(epi):
        for f0 in range(0, F1, NPB):
            nf = min(NPB, F1 - f0)
            pb = psB.get()
            for fi in range(nf):
                f = f0 + fi
                o_ = pb[:, fi * 2 * CG:(fi + 1) * 2 * CG]
                seq = [(TBTh, 0, Yh, slice(1, 3)), (TBTh, 0, Yl, slice(1, 3)), (TBTl, 0, Yh, slice(1, 3)),
                       (TBTh, 1, Yh, slice(0, 2)), (TBTh, 1, Yl, slice(0, 2)), (TBTl, 1, Yh, slice(0, 2))]
                for k_, (tb_, cs_, dat_, sl_) in enumerate(seq):
                    c.op("pe", lambda e: e.matmul(out=o_, lhsT=tb_[:, f, cs_, :], rhs=dat_[:, f, sl_, :].rearrange("p r c -> p (r c)"), start=(k_ == 0), stop=(k_ == 5)),
                         reads=[tb_, dat_], writes=[pb])
            src = pb[:, 0:nf * 2 * CG].rearrange("p (f r c) -> p f r c", f=nf, r=2)
            for r in range(2):
                evac(G[:, :, r * F1 + f0:r * F1 + f0 + nf].rearrange("p c f -> p f c"), src[:, :, r, :], [pb], [G])
        for c0 in range(0, CG, 4):
            pt_ = psT.get()
            for ci in range(4):
                c.op("pe", lambda e: e.transpose(out=pt_[0:F2, ci * 128:(ci + 1) * 128], in_=G[:, c0 + ci, :], identity=ident[:]), reads=[G, ident], writes=[pt_])
            evac(GT[:, c0:c0 + 4, :], pt_[0:F2, :].rearrange("p (c t) -> p c t", c=4), [pt_], [GT])
        for c0 in range(0, CG, 4):
            py = psA.get()
            c.op("pe", lambda e: e.matmul(out=py[0:T1, :], lhsT=Minv[:], rhs=GT[:, c0:c0 + 4, :], start=True, stop=True), reads=[Minv, GT], writes=[py])
            epi(py, c0)

    for g in range(256 // CG):
        ch0 = g * CG

        def load(dst, R, src_d, r0, t0_):
            c.dma(rr(["sp", "pool"]), dst[0:R, :, :], src_d[r0:r0 + CG, t0_:t0_ + R * 128].rearrange("c (a b) -> a c b", b=128), reads=[src_d], writes=[dst])
        for o in range(2):
            X = xp.get()
            c.dma(rr(["sp", "pool"]), X[:], taps_d[o, ch0:ch0 + CG, :].rearrange("c (a b) -> a c b", b=128), reads=[taps_d], writes=[X])

            def ksink(pb, f0, nf, o=o):
                evac(Ksp[o][:, f0:f0 + nf, :, :], pb[:, 0:nf * 2 * CG].rearrange("p (f r c) -> p f r c", f=nf, r=2), [pb], [Ksp[o]])
            fwd(X, N1, ksink)
        pv = xp.get()
        load(pv, T1, hyc_d, ch0, tok0)
        px1 = xp.get()
        load(px1, T1, hyc_d, 256 + ch0, tok0)

        def make_psink(o):
            def psink(pb, f0, nf):
                src = pb[:, 0:nf * 2 * CG].rearrange("p (f r c) -> p f r c", f=nf, r=2)
                Xr = src[:, :, 0, :]; Xi = src[:, :, 1, :]
                Kr = Ksp[o][:, f0:f0 + nf, 0, :]; Ki = Ksp[o][:, f0:f0 + nf, 1, :]
                a = tmq.get(); b = tmr.get(); yy = y32.get()
                c.op("dve", lambda e: e.tensor_tensor(out=a[:, 0:nf, :], in0=Xr, in1=Kr, op=ALU.mult), reads=[pb, Ksp[o]], writes=[a])
                c.op("dve", lambda e: e.tensor_tensor(out=b[:, 0:nf, :], in0=Xi, in1=Ki, op=ALU.mult), reads=[pb, Ksp[o]], writes=[b])
                c.op("pool", lambda e: e.tensor_tensor(out=yy[:, 0, 0:nf, :], in0=a[:, 0:nf, :], in1=b[:, 0:nf, :], op=ALU.subtract), reads=[a, b], writes=[yy])
                a = tmq.get(); b = tmr.get()
                c.op("dve", lambda e: e.tensor_tensor(out=a[:, 0:nf, :], in0=Xr, in1=Ki, op=ALU.mult), reads=[pb, Ksp[o]], writes=[a])
                c.op("dve", lambda e: e.tensor_tensor(out=b[:, 0:nf, :], in0=Xi, in1=Kr, op=ALU.mult), reads=[pb, Ksp[o]], writes=[b])
                c.op("pool", lambda e: e.tensor_tensor(out=yy[:, 1, 0:nf, :], in0=a[:, 0:nf, :], in1=b[:, 0:nf, :], op=ALU.add), reads=[a, b], writes=[yy])
                hsl = Yh[:, f0:f0 + nf, 1:3, :].rearrange("p f r c -> p r f c")
                lsl = Yl[:, f0:f0 + nf, 1:3, :].rearrange("p f r c -> p r f c")
                c.op("act", lambda e: e.copy(out=hsl, in_=yy[:, :, 0:nf, :]), reads=[yy], writes=[Yh])
                c.op("pool", lambda e: e.tensor_tensor(out=lsl, in0=yy[:, :, 0:nf, :], in1=hsl, op=ALU.subtract), reads=[yy, Yh], writes=[Yl])
                c.op("act", lambda e: e.activation(out=Yh[:, f0:f0 + nf, 0, :], in_=Yh[:, f0:f0 + nf, 2, :], func=AF.Copy, scale=-1.0), reads=[Yh], writes=[Yh])
                c.op("act", lambda e: e.activation(out=Yl[:, f0:f0 + nf, 0, :], in_=Yl[:, f0:f0 + nf, 2, :], func=AF.Copy, scale=-1.0), reads=[Yl], writes=[Yl])
            return psink

        def make_epi(o, src_lin, gate, dst):
            def epi(py, c0):
                bb = bias[:, o, ch0 + c0:ch0 + c0 + 4].unsqueeze(2).broadcast_to([T1, 4, 128])
                t3 = tmp[:, :].rearrange("p (c t) -> p c t", c=4)
                c.op("dve", lambda e: e.tensor_tensor(out=t3, in0=src_lin[0:T1, c0:c0 + 4, :], in1=bb, op=ALU.mult), reads=[src_lin, bias], writes=[tmp])
                c.op("dve", lambda e: e.tensor_tensor(out=t3, in0=py[0:T1, :].rearrange("p (c t) -> p c t", c=4), in1=t3, op=ALU.add), reads=[py, tmp], writes=[tmp])
                c.op("pool", lambda e: e.tensor_tensor(out=dst[0:T1, c0:c0 + 4, :], in0=t3, in1=gate[0:T1, c0:c0 + 4, :], op=ALU.mult), reads=[tmp, gate], writes=[dst])
            return epi
        fwd(pv, T1, make_psink(0))
        zt = xp.get()
        inverse(make_epi(0, pv, px1, zt))
        px2 = xp.get()
        load(px2, T1, hyc_d, 512 + ch0, tok0)
        fwd(zt, T1, make_psink(1))
        yt = xp.get()
        inverse(make_epi(1, zt, px2, yt))
        c.dma(rr(["sp", "pool"]), mixT_d[row0 + ch0:row0 + ch0 + CG, tok0:tok0 + L].rearrange("c (a b) -> a c b", b=128), yt[0:T1, :, :], reads=[yt], writes=[mixT_d])
    c.pop()


NFF = DFF // 128
WIN = 640
OWN = 512
NWINTOK = 2048 + 128


def emit_cast_dram(c, src_d, dst_d, R, X):
    c.push()
    st = Rot(c, 2, [128, X]); sb = Rot(c, 2, [128, X], BF16)
    for r in range(R):
        a = st.get(); b = sb.get()
        c.dma(("sp", "pool")[r % 2], a[:], src_d[r], reads=[src_d], writes=[a])
        eng = ("act", "dve", "pool")[r % 3]
        if eng == "act":
            c.op("act", lambda e: e.copy(out=b[:], in_=a[:]), reads=[a], writes=[b])
        else:
            c.op(eng, lambda e: e.tensor_copy(out=b[:], in_=a[:]), reads=[a], writes=[b])
        c.dma(("pool", "sp")[r % 2], dst_d[r], b[:], reads=[b], writes=[dst_d])
    c.pop()


def emit_B(c, t, last, blocks):
    ZKC = [6, 7, 14, 15]
    c.push()
    modsT = t["modsT_sb"]
    g2 = c.sbuf([128, KC]); c.dma("sp", g2[:], t["g2T"].t.ap(), reads=[t["g2T"]], writes=[g2])
    G = c.sbuf([128, 2, KC]); S = c.sbuf([128, 2, KC])
    for r in range(2):
        c.op("dve", lambda e: e.scalar_tensor_tensor(out=G[:, r, :], in0=modsT[:, 64:80, r], scalar=1.0, in1=g2[:], op0=ALU.add, op1=ALU.mult), reads=[modsT, g2], writes=[G])
        c.op("dve", lambda e: e.tensor_copy(out=S[:, r, :], in_=modsT[:, 48:64, r]), reads=[modsT], writes=[S])
    gluw = c.sbuf([128, 4, 512], BF16); glub = c.sbuf([128, 4])
    c.push()
    gst = c.sbuf([128, 4, 512])
    c.dma("sp", gst[:], t["gluw"].t.ap().rearrange("(kc p) n -> p kc n", p=128), reads=[t["gluw"]], writes=[gst])
    c.op("dve", lambda e: e.tensor_copy(out=gluw[:], in_=gst[:]), reads=[gst], writes=[gluw])
    c.pop()
    c.dma("sp", glub[:], t["glubT"].t.ap(), reads=[t["glubT"]], writes=[glub])
    cw = c.sbuf([128, NFF, 9]); cb = c.sbuf([128, NFF])
    c.dma("sp", cw[:], t["cwT"].t.ap(), reads=[t["cwT"]], writes=[cw])
    c.dma("sp", cb[:], t["cbT"].t.ap(), reads=[t["cbT"]], writes=[cb])
    nf = c.sbuf([128, KC])
    if last:
        c.dma("sp", nf[:], t["nfT"].t.ap(), reads=[t["nfT"]], writes=[nf])
    ones = c.sbuf([128, 128]); c.op("dve", lambda e: e.memset(ones[:], 1.0), writes=[ones])
    mask = c.sbuf([128, WIN])
    hb = c.sbuf([128, KC, WIN]); hviews = [hb.view() for _ in range(KC)]
    mx = c.sbuf([128, KC, WIN], BF16); mviews = [mx.view() for _ in range(KC)]
    w640 = Rot(c, 6, [128, WIN]); w512 = Rot(c, 5, [128, OWN])
    mst = sqr = tmpr = gr = sgr = w640
    accr = ger = outr = w512
    hff = c.sbuf([128, NFF, OWN], BF16); fviews = [hff.view() for _ in range(NFF)]
    wor = Rot(c, 2, [128, KC, 128], BF16); wur = Rot(c, 2, [128, KC, 256], BF16); wdr = Rot(c, 3, [128, NFF // 2, 128], BF16)
    psG = Rot(c, 2, [128, 512], psum=True); psV = Rot(c, 2, [128, 512], psum=True); psO = Rot(c, 2, [128, 512], psum=True); psS = Rot(c, 1, [128, 512], psum=True)
    sd = c.sbuf([128, WIN]); rstd = c.sbuf([128, WIN])
    z32v = [c.sbuf([128, WIN]) for _ in range(4)]
    z16 = c.sbuf([128, 4, WIN], BF16); z16v = [z16.view() for _ in range(4)]
    hsrc = t["hsrc"]
    mixbufs = t["mixbufs"]

    def mm_tok(ps_list, lhsT_fn, rhs_buf, rviews, a, b, nk, extra_reads):
        pieces = []
        s0 = a
        while s0 < b:
            n = min(512, b - s0)
            pieces.append((s0, n))
            s0 += n
        for (pi, (s0, n)) in enumerate(pieces):
            ps = ps_list[pi]
            for kc in range(nk):
                c.op("pe", lambda e: e.matmul(out=ps[:, 0:n], lhsT=lhsT_fn(kc), rhs=rhs_buf[:, kc, s0:s0 + n], start=(kc == 0), stop=(kc == nk - 1)),
                     reads=[rviews[kc]] + extra_reads, writes=[ps])
        return pieces

    def block(col0, W, oo, O, r, grid, mk, dst, dyn):
        def q2(i):
            return ("sp", "pool")[i % 2]

        def load(i, dst_ap, dst_buf, src_buf, row0, out_dt_buf=None):
            c.dma(q2(i), dst_ap, src_buf[row0:row0 + 128, col0:col0 + W], reads=[src_buf], writes=[dst_buf])
            if dyn is not None:
                col1, sel = dyn
                b_ = mst.get()
                c.dma(q2(i + 1), b_[:, 0:W], src_buf[row0:row0 + 128, col1:col1 + W], reads=[src_buf], writes=[b_])
                e1 = ("dve", "pool")[i % 2]
                c.op(e1, lambda e: e.tensor_scalar(out=dst_ap, in0=dst_ap, scalar1=sel[:, 0:1], scalar2=1.0, op0=ALU.mult, op1=ALU.mult), reads=[dst_buf, sel], writes=[dst_buf])
                c.op("dve", lambda e: e.scalar_tensor_tensor(out=dst_ap, in0=b_[:, 0:W], scalar=sel[:, 1:2], in1=dst_ap, op0=ALU.mult, op1=ALU.add), reads=[b_, sel, dst_buf], writes=[dst_buf])
        if mk is not None:
            c.dma("sp", mask[:, 0:W], mk[0][:, mk[1]:mk[1] + W], reads=[mk[0]], writes=[mask])
        for kc in range(KC):
            mb = mixbufs[kc // 8]
            mrow = (kc % 8) * 128
            if kc not in ZKC and dyn is None:
                c.dma("pool", mx[:, kc, 0:W], mb[mrow:mrow + 128, col0:col0 + W], reads=[mb], writes=[mviews[kc]])
            elif kc not in ZKC:
                m_ = mst.get()
                load(kc, m_[:, 0:W], m_, mb, mrow)
                if kc % 2 == 0:
                    c.op("act", lambda e: e.copy(out=mx[:, kc, 0:W], in_=m_[:, 0:W]), reads=[m_], writes=[mviews[kc]])
                else:
                    c.op("pool", lambda e: e.tensor_copy(out=mx[:, kc, 0:W], in_=m_[:, 0:W]), reads=[m_], writes=[mviews[kc]])
            else:
                zi = ZKC.index(kc)
                load(kc, z32v[zi][:, 0:W], z32v[zi], mb, mrow)
                c.op("act", lambda e: e.copy(out=z16[:, zi, 0:W], in_=z32v[zi][:, 0:W]), reads=[z32v[zi]], writes=[z16v[zi]])
            load(kc + 1, hb[:, kc, 0:W], hviews[kc], hsrc, kc * 128)
        for n_ in range(4):
            pl = [psO.get(), psO.get()]
            pieces = mm_tok(pl, lambda kc: gluw[:, kc, n_ * 128:(n_ + 1) * 128], z16, z16v, 0, W, 4, [gluw])
            for (pi, (s0, n)) in enumerate(pieces):
                sg = sgr.get()
                c.op("act", lambda e: e.activation(out=sg[:, 0:n], in_=pl[pi][:, 0:n], func=AF.Sigmoid, bias=glub[:, n_:n_ + 1]), reads=[pl[pi], glub], writes=[sg])
                c.op("dve", lambda e: e.tensor_tensor(out=mx[:, ZKC[n_], s0:s0 + n], in0=sg[:, 0:n], in1=z32v[n_][:, s0:s0 + n], op=ALU.mult), reads=[sg, z32v[n_]], writes=[mviews[ZKC[n_]]])
        for m in range(KC):
            wo = wor.get()
            c.dma(("sp", "pool")[m % 2], wo[:], t["wout16"][m].rearrange("p (k n) -> p k n", k=KC), reads=[t["wout16_v"][m // 8]], writes=[wo])
            pl = [psO.get(), psO.get()]
            pieces = mm_tok(pl, lambda kc: wo[:, kc, :], mx, mviews, 0, W, KC, [wo])
            for (pi, (s0, n)) in enumerate(pieces):
                c.op("dve", lambda e: e.scalar_tensor_tensor(out=hb[:, m, s0:s0 + n], in0=pl[pi][:, 0:n], scalar=modsT[:, 32 + m, r:r + 1], in1=hb[:, m, s0:s0 + n], op0=ALU.mult, op1=ALU.add),
                     reads=[pl[pi], modsT, hviews[m]], writes=[hviews[m]])
        pieces = []
        s0 = 0
        while s0 < W:
            n = min(512, W - s0); pieces.append((s0, n)); s0 += n
        for (s0, n) in pieces:
            ss = psS.get()
            for kc in range(KC):
                sq = sqr.get()
                c.op("act", lambda e: e.activation(out=sq[:, 0:n], in_=hb[:, kc, s0:s0 + n], func=AF.Square), reads=[hviews[kc]], writes=[sq])
                for h_ in range(2):
                    c.op("pe", lambda e: e.matmul(out=ss[64 * h_:64 * h_ + 64, 0:n], lhsT=ones[:, 0:64], rhs=sq[:, 0:n], start=(kc == 0), stop=(kc == KC - 1)), reads=[ones, sq], writes=[ss])
            c.op("act", lambda e: e.activation(out=sd[:, s0:s0 + n], in_=ss[:, 0:n], func=AF.Sqrt, scale=1.0 / D, bias=EPS), reads=[ss], writes=[sd])
        c.op("dve", lambda e: e.reciprocal(out=rstd[:, 0:W], in_=sd[:, 0:W]), reads=[sd], writes=[rstd])
        for kc in range(KC):
            tm_ = tmpr.get()
            c.op("dve", lambda e: e.tensor_tensor(out=tm_[:, 0:W], in0=hb[:, kc, 0:W], in1=rstd[:, 0:W], op=ALU.mult), reads=[hviews[kc], rstd], writes=[tm_])
            c.op("act", lambda e: e.activation(out=mx[:, kc, 0:W], in_=tm_[:, 0:W], func=AF.Identity, scale=G[:, r, kc:kc + 1], bias=S[:, r, kc:kc + 1]),
                 reads=[tm_, G, S], writes=[mviews[kc]])
        for j in range(NFF):
            wu = wur.get()
            c.dma(("sp", "pool")[j % 2], wu[:], t["wup16"][j].rearrange("p (k n) -> p k n", k=KC), reads=[t["wup16_v"][j // 4]], writes=[wu])
            gl = [psG.get(), psG.get()]
            gp = mm_tok(gl, lambda kc: wu[:, kc, 0:128], mx, mviews, 0, W, KC, [wu])
            vl = [psV.get()]
            mm_tok(vl, lambda kc: wu[:, kc, 128:256], mx, mviews, oo, oo + O, KC, [wu])
            g = gr.get()
            for (pi, (s0, n)) in enumerate(gp):
                if mk is None:
                    c.op("act", lambda e: e.copy(out=g[:, s0:s0 + n], in_=gl[pi][:, 0:n]), reads=[gl[pi]], writes=[g])
                else:
                    c.op("dve", lambda e: e.tensor_tensor(out=g[:, s0:s0 + n], in0=gl[pi][:, 0:n], in1=mask[:, s0:s0 + n], op=ALU.mult), reads=[gl[pi], mask], writes=[g])
            acc = accr.get()
            if grid:
                g3 = g[:, 0:W].rearrange("p (r x) -> p r x", x=64)
                a3 = acc[:, 0:O].rearrange("p (r x) -> p r x", x=64)
                nr = O // 64
                c.op("dve", lambda e: e.tensor_scalar(out=a3, in0=g3[:, 1:1 + nr, :], scalar1=cw[:, j, 4:5], scalar2=cb[:, j:j + 1], op0=ALU.mult, op1=ALU.add), reads=[g, cw, cb], writes=[acc])
                for dr in (-1, 0, 1):
                    for dc in (-1, 0, 1):
                        if dr == 0 and dc == 0:
                            continue
                        k = (dr + 1) * 3 + (dc + 1)
                        xo = slice(max(0, -dc), 64 - max(0, dc))
                        xi = slice(max(0, dc), 64 - max(0, -dc))
                        c.op("dve", lambda e: e.scalar_tensor_tensor(out=a3[:, :, xo], in0=g3[:, 1 + dr:1 + dr + nr, xi], scalar=cw[:, j, k:k + 1], in1=a3[:, :, xo], op0=ALU.mult, op1=ALU.add),
                             reads=[g, cw, acc], writes=[acc])
            else:
                c.op("dve", lambda e: e.tensor_scalar(out=acc[:, 0:O], in0=g[:, 0:O], scalar1=cw[:, j, 4:5], scalar2=cb[:, j:j + 1], op0=ALU.mult, op1=ALU.add), reads=[g, cw, cb], writes=[acc])
                c.op("dve", lambda e: e.scalar_tensor_tensor(out=acc[:, 1:O], in0=g[:, 0:O - 1], scalar=cw[:, j, 3:4], in1=acc[:, 1:O], op0=ALU.mult, op1=ALU.add), reads=[g, cw, acc], writes=[acc])
                c.op("dve", lambda e: e.scalar_tensor_tensor(out=acc[:, 0:O - 1], in0=g[:, 1:O], scalar=cw[:, j, 5:6], in1=acc[:, 0:O - 1], op0=ALU.mult, op1=ALU.add), reads=[g, cw, acc], writes=[acc])
            ge = ger.get()
            c.op("act", lambda e: e.activation(out=ge[:, 0:O], in_=acc[:, 0:O], func=AF.Gelu_apprx_tanh), reads=[acc], writes=[ge])
            c.op("dve", lambda e: e.tensor_tensor(out=hff[:, j, 0:O], in0=vl[0][:, 0:O], in1=ge[:, 0:O], op=ALU.mult), reads=[vl[0], ge], writes=[fviews[j]])
        dbuf, dcol = dst
        for m in range(KC):
            wda = wdr.get(); wdb = wdr.get()
            wsrc = t["wdn16"][m].rearrange("p (k n) -> p k n", k=NFF)
            c.dma("sp", wda[:], wsrc[:, 0:NFF // 2, :], reads=[t["wdn16_v"][m // 4]], writes=[wda])
            c.dma("pool", wdb[:], wsrc[:, NFF // 2:NFF, :], reads=[t["wdn16_v"][m // 4]], writes=[wdb])
            pl = [psO.get()]
            mm_tok(pl, lambda kc: (wda[:, kc, :] if kc < NFF // 2 else wdb[:, kc - NFF // 2, :]), hff, fviews, 0, O, NFF, [wda, wdb])
            c.op("dve", lambda e: e.scalar_tensor_tensor(out=hb[:, m, oo:oo + O], in0=pl[0][:, 0:O], scalar=modsT[:, 80 + m, r:r + 1], in1=hb[:, m, oo:oo + O], op0=ALU.mult, op1=ALU.add),
                 reads=[pl[0], modsT, hviews[m]], writes=[hviews[m]])
            if not last:
                c.dma(("pool", "sp")[m % 2], dbuf[m * 128:(m + 1) * 128, dcol:dcol + O], hb[:, m, oo:oo + O], reads=[hviews[m]], writes=[dbuf])
        if last:
            ss = psS.get()
            for kc in range(KC):
                sq = sqr.get()
                c.op("act", lambda e: e.activation(out=sq[:, 0:O], in_=hb[:, kc, oo:oo + O], func=AF.Square), reads=[hviews[kc]], writes=[sq])
                for h_ in range(2):
                    c.op("pe", lambda e: e.matmul(out=ss[64 * h_:64 * h_ + 64, 0:O], lhsT=ones[:, 0:64], rhs=sq[:, 0:O], start=(kc == 0), stop=(kc == KC - 1)), reads=[ones, sq], writes=[ss])
            c.op("act", lambda e: e.activation(out=sd[:, 0:O], in_=ss[:, 0:O], func=AF.Sqrt, scale=1.0 / D, bias=EPS), reads=[ss], writes=[sd])
            c.op("dve", lambda e: e.reciprocal(out=rstd[:, 0:O], in_=sd[:, 0:O]), reads=[sd], writes=[rstd])
            for kc in range(KC):
                ot = outr.get()
                c.op("dve", lambda e: e.scalar_tensor_tensor(out=ot[:, 0:O], in0=hb[:, kc, oo:oo + O], scalar=nf[:, kc:kc + 1], in1=rstd[:, 0:O], op0=ALU.mult, op1=ALU.mult),
                     reads=[hviews[kc], nf, rstd], writes=[ot])
                c.dma(("pool", "sp")[kc % 2], dbuf[kc * 128:(kc + 1) * 128, dcol:dcol + O], ot[:, 0:O], reads=[ot], writes=[dbuf])

    for b_ in blocks:
        block(b_["col0"], b_["W"], b_["oo"], b_["O"], b_["r"], b_["grid"], b_.get("mask"), b_["dst"], b_.get("dyn"))
    c.pop()


def B_host_weights(inp, l, last):
    w = inp["ffn_w_up"][l]
    wup = np.stack([w[:, :DFF].reshape(KC, 128, NFF, 128), w[:, DFF:].reshape(KC, 128, NFF, 128)], axis=3)
    wup = np.ascontiguousarray(wup.transpose(2, 1, 0, 3, 4)).reshape(NFF, 128, KC * 256)
    wdn = np.ascontiguousarray(inp["ffn_w_down"][l].reshape(NFF, 128, KC, 128).transpose(2, 1, 0, 3)).reshape(KC, 128, NFF * 128)
    wout = np.ascontiguousarray(inp["w_out"][l].reshape(KC, 128, KC, 128).transpose(2, 1, 0, 3)).reshape(KC, 128, KC * 128)
    d = {"wup": wup, "wdn": wdn, "wout": wout,
         "g2T": np.ascontiguousarray(inp["norm2_g"][l].reshape(KC, 128).T),
         "gluw": inp["s5_glu_w"][l], "glubT": np.ascontiguousarray(inp["s5_glu_b"][l].reshape(4, 128).T),
         "cwT": np.ascontiguousarray(inp["ffn_conv_w"][l].reshape(9, NFF, 128).transpose(2, 1, 0)),
         "cbT": np.ascontiguousarray(inp["ffn_conv_b"][l].reshape(NFF, 128).T)}
    if last:
        d["nfT"] = np.ascontiguousarray(inp["norm_f"].reshape(KC, 128).T)
    return d


TP = T + 64
MIX_PERM = np.concatenate([np.arange(0, 256), np.arange(512, 1024), np.arange(1536, 1792),
                           np.arange(256, 512), np.arange(1024, 1536), np.arange(1792, 2048)])


def A_const_inputs():
    return {**ret_consts(), **s5_consts(), **hy_tables(SEQ), **hy_tables(CTX), "ident": np.eye(128, dtype=np.float32)}


def A_weight_inputs(inp, l, hf):
    r = np.arange
    colsf = np.concatenate([r(256) + 256 * hf, 512 + r(256) + 256 * hf, 1024 + r(256) + 256 * hf, 1536 + 256 * hf + r(256),
                            2048 + 256 * hf + r(256), 4608 + 256 * hf + r(256)])
    colst = np.concatenate([2048 + 256 * hf + r(256), 2560 + 512 * hf + r(512), 3584 + 512 * hf + r(512)])
    d = {"win": np.ascontiguousarray(inp["w_in"][l][:, np.concatenate([colsf, colst])]),
         "rdec": np.ascontiguousarray(inp["ret_decay"][l][:, 4 * hf:4 * hf + 4]).reshape(-1)}
    d.update(s5_host(inp, l, hf))
    d.update(hy_host(inp, l, hf))
    return d


def layer_inputs(inp, l):
    last = l == 1
    d = {"adaw": inp["ada_w"][l], "adabT": np.ascontiguousarray(inp["ada_b"][l].reshape(96, 128).T),
         "g1T": np.ascontiguousarray(inp["norm1_g"][l].reshape(KC, 128).T)}
    inp2 = dict(inp)
    wo = np.array(inp["w_out"], copy=True)
    wo[l] = inp["w_out"][l][MIX_PERM, :]
    inp2["w_out"] = wo
    d.update(B_host_weights(inp2, l, last))
    out = {f"{k}_{l}": v for k, v in d.items()}
    for hf in range(2):
        for k, v in A_weight_inputs(inp, l, hf).items():
            out[f"{k}_{l}{hf}"] = v
    return out


CONST_NAMES = (["ret_pf", "ret_pb", "ret_posq", "ret_posk", "s5_pos", "ident"]
               + [f"hy{L_}_{n_}" for L_ in (SEQ, CTX) for n_ in ("W1", "TBh", "TBl", "TBTh", "TBTl", "Minv", "zT", "tn")])


def build_fused(shapes):
    c = Ctx()
    nc = c.nc
    t = {n: c.dram(n, list(sh), BF16 if is16 else F32, "ExternalInput") for n, (sh, is16) in shapes.items()}
    out_d = c.dram("out", [D, 2048], F32, "ExternalOutput")
    sc = {"hyT": c.dram("hyT", [768, T], F32), "qT": c.dram("qT", [256, T], BF16), "kT": c.dram("kT", [256, T], BF16),
          "uT": c.dram("uT", [256, T], F32), "k": c.dram("k", [T, 256], BF16), "v": c.dram("v", [T, 512], BF16),
          "sg": c.dram("sg", [T, 512], F32)}
    hyc_d = c.dram("hyc", [768, T], F32)
    tapsL = c.dram("tapsL", [2, 256, 2 * SEQ], F32)
    tapsC = c.dram("tapsC", [2, 256, 2 * CTX], F32)
    mixbufs = [c.dram("mixT0", [1024, TP], F32), c.dram("mixT1", [1024, TP], F32)]
    h1T = c.dram("h1T", [D, TP], F32)
    w16 = [{"wup16": c.dram(f"wup16_{l}", [NFF, 128, KC * 256], BF16), "wdn16": c.dram(f"wdn16_{l}", [KC, 128, NFF * 128], BF16),
            "wout16": c.dram(f"wout16_{l}", [KC, 128, KC * 128], BF16)} for l in range(2)]
    c.push()
    modsT = c.sbuf([128, 96, 2])
    ident = c.sbuf([128, 128])
    c.dma("sp", ident[:], t["ident"].t.ap(), reads=[t["ident"]], writes=[ident])
    c.push()
    zt = c.sbuf([128, 16, 64])
    c.op("dve", lambda e: e.memset(zt[:], 0.0), writes=[zt])
    for mb in mixbufs:
        c.dma("sp", mb.t.ap().rearrange("(a p) t -> p a t", p=128)[:, :, T:TP], zt[:, 0:8, :], reads=[zt], writes=[mb])
    c.dma("sp", h1T.t.ap().rearrange("(a p) t -> p a t", p=128)[:, :, T:TP], zt[:], reads=[zt], writes=[h1T])
    c.pop()
    sel = c.sbuf([128, 2])
    c.dma("sp", sel[:], t["sel"].t.ap(), reads=[t["sel"]], writes=[sel])
    pending = []
    for l in range(2):
        for (nm, R, step) in (("wup", NFF, 4), ("wdn", KC, 4), ("wout", KC, 8)):
            for r0 in range(0, R, step):
                w16[l].setdefault(nm + "16_v", []).append(w16[l][nm + "16"].view())
                pending.append((w16[l][nm + "16"], w16[l][nm + "16_v"][-1], t[f"{nm}_{l}"], r0, step))

    def bg_issue(k):
        for _ in range(k):
            if pending:
                dst, dview, src_, r0, step = pending.pop(0)
                c.dma("pool", dst[r0:r0 + step], src_[r0:r0 + step], reads=[], writes=[dview], own_sem=True)
    for l in range(2):
        last = l == 1
        emit_mods(c, t["cT"], t[f"adaw_{l}"], t[f"adabT_{l}"], modsT)
        src = t["hT"] if l == 0 else h1T
        for hf in range(2):
            w = {k: t[k] for k in CONST_NAMES}
            sfx = f"_{l}{hf}"
            for k, v in t.items():
                if k.endswith(sfx):
                    w[k[:-len(sfx)]] = v
            emit_inproj(c, src, w["win"], t[f"g1T_{l}"], modsT, sc)
            bg_issue(5)
            c.push()
            thunks = hy_shortconv_thunks(c, sc["hyT"], w, hyc_d)
            emit_retention(c, sc["qT"], sc["kT"], sc["k"], sc["v"], sc["sg"], w["rdec"], w, ident, mixbufs[hf], 256, extra=thunks)
            c.pop()
            if not last:
                emit_hy_filters(c, CTX, w, tapsC)
            emit_hy_filters(c, SEQ, w, tapsL)
            if not last:
                emit_hy_conv(c, CTX, 0, w, hyc_d, tapsC, ident, mixbufs[hf], 0)
            emit_hy_conv(c, SEQ, CTX, w, hyc_d, tapsL, ident, mixbufs[hf], 0)
            bg_issue(5)
            bg_issue(5)
            emit_s5(c, sc["uT"], w, mixbufs[hf], 768)
        bg_issue(100 if l == 1 else 4)
        tb = {"modsT_sb": modsT, "mixbufs": mixbufs, "hsrc": src, **w16[l]}
        for k in ("wup", "wdn", "wout", "g2T", "gluw", "glubT", "cwT", "cbT"):
            tb[k] = t[f"{k}_{l}"]
        if last:
            tb["nfT"] = t["nfT_1"]
        blocks = []
        if not last:
            blocks.append(dict(col0=0, W=CTX, oo=0, O=CTX, r=1, grid=False, mask=None, dst=(h1T, 0)))
            for blk in range(8):
                mk = (t["maskL0"], 0) if blk == 0 else ((t["maskL0"], WIN) if blk == 7 else None)
                blocks.append(dict(col0=CTX - 64 + blk * OWN, W=WIN, oo=64, O=OWN, r=0, grid=True, mask=mk, dst=(h1T, CTX + blk * OWN)))
        else:
            for blk in range(4):
                blocks.append(dict(col0=CTX - 64 + blk * OWN, W=WIN, oo=64, O=OWN, r=0, grid=True,
                                   mask=(t["mask"], blk * OWN), dst=(out_d, blk * OWN), dyn=(CTX - 64 + blk * OWN + 2048, sel)))
        emit_B(c, tb, last, blocks)
    c.pop()
    return c


def kernel(**inp):
    inp = {k: np.asarray(v) for k, v in inp.items()}
    x, cc, ctx, c_ctx = inp["x"], inp["c"], inp["ctx"], inp["c_ctx"]
    NB = x.shape[0]
    cores = list(range(NCORES))
    consts = A_const_inputs()
    maskL0 = np.ones([128, 2 * WIN], np.float32)
    maskL0[:, 0:64] = 0.0
    maskL0[:, WIN + 576:WIN + 640] = 0.0
    shared = {**consts, "maskL0": maskL0}
    for l in range(2):
        shared.update(layer_inputs(inp, l))
    maps = []
    for i in cores:
        b, hf = i // 2, i % 2
        hT = np.zeros([D, TP], np.float32)
        hT[:, 0:CTX] = ctx[b].T
        hT[:, CTX:T] = x[b].T
        mask = np.ones([128, NWINTOK], np.float32)
        if hf == 0:
            mask[:, 0:64] = 0.0
        else:
            mask[:, NWINTOK - 64:] = 0.0
        sel = np.zeros([128, 2], np.float32)
        sel[:, hf] = 1.0
        m = {**shared, "hT": hT, "cT": np.ascontiguousarray(np.stack([cc[b], c_ctx], 1)), "mask": mask, "sel": sel}
        maps.append(m)
    cF = build_fused({n: (a.shape, a.dtype == ml_dtypes.bfloat16) for n, a in maps[0].items()})
    res = run_bass_kernel_spmd(cF.nc, maps, core_ids=cores).results
    out = np.stack([np.concatenate([res[2 * b]["out"], res[2 * b + 1]["out"]], 1).T for b in range(NB)], 0)
    return np.ascontiguousarray(out.astype(np.float32))
```
